# Optimizing a Trainium2 kernel written in Bass

```python
import jax, jax.numpy as jnp
from jax import lax
import numpy as np

D_MODEL = 1024
BATCH = 8
SEQ = 4096
DEPTH = 1

POOL_WINDOWS = (2, 4, 8, 16)
N_POOL_GROUPS = len(POOL_WINDOWS)
POOL_WIDTH = D_MODEL
POOL_GROUP = POOL_WIDTH // N_POOL_GROUPS
HEAD_DIM = 64
N_Q_HEADS = D_MODEL // HEAD_DIM
N_KV_HEADS = 2
GQA_GROUP = N_Q_HEADS // N_KV_HEADS
WINDOW = 128
BLOCK = 128
ROPE_DIM = HEAD_DIM // 4
ROPE_THETA = 500000.0
Q_WIDTH = N_Q_HEADS * HEAD_DIM
KV_WIDTH = N_KV_HEADS * HEAD_DIM
D_FF = 2816
CONV_WIDTH = 3
EPS = 1e-6
IN_WIDTH = POOL_WIDTH + Q_WIDTH + 2 * KV_WIDTH + 2 * D_MODEL

kernel_name = "hybrid_pool_swa_sink_convglu_block"


def rmsnorm(x, g):
    xf = x.astype(jnp.float32)
    r = lax.rsqrt(jnp.mean(xf * xf, axis=-1, keepdims=True) + EPS)
    return (xf * r * g.astype(jnp.float32)).astype(x.dtype)


def partial_rope(x, positions):
    half = ROPE_DIM // 2
    inv_freq = ROPE_THETA ** (-jnp.arange(0, ROPE_DIM, 2, dtype=jnp.float32) / ROPE_DIM)
    ang = positions.astype(jnp.float32)[..., None] * inv_freq
    cos = jnp.cos(ang)[:, :, None, :]
    sin = jnp.sin(ang)[:, :, None, :]
    xf = x.astype(jnp.float32)
    x1, x2, xp = xf[..., :half], xf[..., half:ROPE_DIM], xf[..., ROPE_DIM:]
    out = jnp.concatenate([x1 * cos - x2 * sin, x2 * cos + x1 * sin, xp], axis=-1)
    return out.astype(x.dtype)


def pool_mixer(u, w_pool, pool_scale):
    B, S, _ = u.shape
    ug = u.reshape(B, S, N_POOL_GROUPS, POOL_GROUP).astype(jnp.float32)
    cs = jnp.cumsum(ug, axis=1)
    t = jnp.arange(S, dtype=jnp.float32)
    pooled = []
    for g, w in enumerate(POOL_WINDOWS):
        csg = cs[:, :, g]
        shifted = jnp.pad(csg, ((0, 0), (w, 0), (0, 0)))[:, :S]
        count = jnp.minimum(t + 1.0, float(w))[None, :, None]
        pooled.append((csg - shifted) / count)
    pooled = jnp.stack(pooled, axis=2) - ug
    mixed = jnp.einsum('bsgc,gcd->bsgd', pooled.astype(u.dtype), w_pool)
    return mixed.reshape(B, S, POOL_WIDTH) * pool_scale


def swa_sink_attention(q, k, v, sinks):
    B, S = q.shape[0], q.shape[1]
    nb = S // BLOCK
    qb = q.reshape(B, nb, BLOCK, N_KV_HEADS, GQA_GROUP, HEAD_DIM)

    def band(t):
        tb = t.reshape(B, nb, BLOCK, N_KV_HEADS, HEAD_DIM)
        prev = jnp.pad(tb, ((0, 0), (1, 0), (0, 0), (0, 0), (0, 0)))[:, :-1]
        return jnp.concatenate([prev, tb], axis=2)

    kb, vb = band(k), band(v)
    s = jnp.einsum('bnqhgd,bnkhd->bhgnqk', qb, kb,
                   preferred_element_type=jnp.float32)
    q_pos = jnp.arange(BLOCK)[:, None] + BLOCK
    k_pos = jnp.arange(2 * BLOCK)[None, :]
    rel_ok = (k_pos <= q_pos) & (q_pos - k_pos < WINDOW)
    blk_ok = (jnp.arange(nb)[:, None, None] > 0) | (k_pos[None] >= BLOCK)
    mask = rel_ok[None] & blk_ok
    s = jnp.where(mask, s, -jnp.inf)
    sink = sinks.astype(jnp.float32).reshape(1, N_KV_HEADS, GQA_GROUP, 1, 1, 1)
    m = jnp.maximum(jnp.max(s, axis=-1, keepdims=True), sink)
    p = jnp.exp(s - m)
    denom = jnp.sum(p, axis=-1, keepdims=True) + jnp.exp(sink - m)
    probs = (p / denom).astype(v.dtype)
    out = jnp.einsum('bhgnqk,bnkhd->bnqhgd', probs, vb)
    return out.reshape(B, S, Q_WIDTH)


def causal_depthwise_conv(u, w, b):
    S = u.shape[1]
    up = jnp.pad(u, ((0, 0), (CONV_WIDTH - 1, 0), (0, 0)))
    y = b
    for j in range(CONV_WIDTH):
        y = y + w[j] * up[:, j:j + S]
    return y


def setup_inputs(seed: int = 0) -> dict:
    key = jax.random.key(seed)
    ks = jax.random.split(key, 18)
    f32 = jnp.float32
    nrm = lambda k, shape, s: jax.random.normal(k, shape, f32) * s
    x = jax.random.normal(ks[0], (BATCH, SEQ, D_MODEL), f32)
    offsets = jax.random.randint(ks[1], (BATCH, 1), 0, 1024, dtype=jnp.int32)
    positions = offsets + jnp.arange(SEQ, dtype=jnp.int32)[None, :]
    return {
        "x": x,
        "positions": positions,
        "attn_norm": 1.0 + nrm(ks[2], (DEPTH, D_MODEL), 0.05),
        "w_in": nrm(ks[3], (DEPTH, D_MODEL, IN_WIDTH), D_MODEL ** -0.5),
        "b_gate": nrm(ks[4], (DEPTH, 2 * D_MODEL), 0.1),
        "w_pool": nrm(ks[5], (DEPTH, N_POOL_GROUPS, POOL_GROUP, POOL_GROUP), POOL_GROUP ** -0.5),
        "pool_scale": 1.0 + nrm(ks[6], (DEPTH, POOL_WIDTH), 0.1),
        "q_norm": 1.0 + nrm(ks[7], (DEPTH, HEAD_DIM), 0.05),
        "k_norm": 1.0 + nrm(ks[8], (DEPTH, HEAD_DIM), 0.05),
        "sinks": nrm(ks[9], (DEPTH, N_Q_HEADS), 0.5),
        "w_out": nrm(ks[10], (DEPTH, D_MODEL, D_MODEL), D_MODEL ** -0.5),
        "ffn_norm": 1.0 + nrm(ks[11], (DEPTH, D_MODEL), 0.05),
        "w_up": nrm(ks[12], (DEPTH, D_MODEL, 2 * D_FF), D_MODEL ** -0.5),
        "conv_w": nrm(ks[13], (DEPTH, CONV_WIDTH, 2 * D_FF), CONV_WIDTH ** -0.5),
        "conv_b": nrm(ks[14], (DEPTH, 2 * D_FF), 0.02),
        "w_down": nrm(ks[15], (DEPTH, D_FF, D_MODEL), D_FF ** -0.5),
    }


def reference(x, positions, attn_norm, w_in, b_gate, w_pool, pool_scale, q_norm, k_norm,
              sinks, w_out, ffn_norm, w_up, conv_w, conv_b, w_down):
    B, S, _ = x.shape
    scale = HEAD_DIM ** -0.5
    for l in range(DEPTH):
        h = rmsnorm(x, attn_norm[l])
        z = h @ w_in[l]
        o1 = POOL_WIDTH
        o2 = o1 + Q_WIDTH
        o3 = o2 + KV_WIDTH
        o4 = o3 + KV_WIDTH
        u_pool = z[..., :o1]
        q = z[..., o1:o2].reshape(B, S, N_Q_HEADS, HEAD_DIM)
        k = z[..., o2:o3].reshape(B, S, N_KV_HEADS, HEAD_DIM)
        v = z[..., o3:o4].reshape(B, S, N_KV_HEADS, HEAD_DIM)
        gates = jax.nn.sigmoid((z[..., o4:] + b_gate[l]).astype(jnp.float32)).astype(x.dtype)
        g_pool, g_attn = gates[..., :D_MODEL], gates[..., D_MODEL:]

        a = pool_mixer(u_pool, w_pool[l], pool_scale[l])

        q = partial_rope(rmsnorm(q, q_norm[l]), positions) * scale
        k = partial_rope(rmsnorm(k, k_norm[l]), positions)
        b = swa_sink_attention(q, k, v, sinks[l])

        x = x + (g_pool * a + g_attn * b) @ w_out[l]

        h = rmsnorm(x, ffn_norm[l])
        up = causal_depthwise_conv(h @ w_up[l], conv_w[l], conv_b[l])
        gate, val = up[..., :D_FF], up[..., D_FF:]
        x = x + (jax.nn.silu(gate) * val) @ w_down[l]
    return x
```

```python
import math
from contextlib import ExitStack

import numpy as np
import concourse.bass as bass
import concourse.mybir as mybir
from concourse.bass_utils import run_bass_kernel_spmd

F32 = mybir.dt.float32
BF16 = mybir.dt.bfloat16
I32 = mybir.dt.int32
ALU = mybir.AluOpType
AF = mybir.ActivationFunctionType
AX = mybir.AxisListType

S = 4096
D = 1024
NBLK = S // 128
INW = 4352
DFF = 2816
EPS = 1e-6
ROPE_THETA = 500000.0
NEG = -30000.0

PP_BG = 0
PP_PS = 16
PP_SK = 24
PP_CW = 32
PP_CB = 164
NPP = 208
RB_G1 = 0
RB_GQ = 1024
RB_G2 = 2176
NRB = 3200
CS_ID = 0
CS_MC = 128
CS_MP = 640
CS_CORR = 1152
NCS = 1216


import os
STRICT_WAR = True


class Buf:
    __slots__ = ("name", "w", "r", "sem", "cnt", "excl")

    def __init__(self, name, excl=False):
        self.name = name
        self.excl = excl
        self.w = None
        self.r = {}
        self.sem = None
        self.cnt = 0


class Ins:
    __slots__ = ("eng", "fn", "deps", "idx", "inc", "dma", "semv", "val")


class Rec:
    ENGS = ("pe", "act", "dve", "pool", "sp")
    INORDER_FREE = ("pe", "sp")

    def __init__(self):
        self.streams = {e: [] for e in self.ENGS}
        self.bar = {e: [] for e in self.ENGS}
        self.dmas = []
        self.dma_bufs = []

    def add(self, eng, fn, r=(), w=(), dma=None, nodep=False):
        I = Ins()
        I.eng = eng
        I.fn = fn
        I.dma = dma
        I.inc = False
        I.val = 0
        I.semv = None
        deps = {}
        for b in r:
            if b.w is not None:
                deps[id(b.w)] = b.w
            if b.excl:
                for q in b.r.values():
                    if q.eng != eng:
                        deps[id(q)] = q
        for b in w:
            if nodep:
                continue
            if b.w is not None:
                deps[id(b.w)] = b.w
            for q in b.r.values():
                if q.dma is None and q.eng == eng and dma is None and not STRICT_WAR:
                    continue
                deps[id(q)] = q
        for q in self.bar[eng]:
            deps[id(q)] = q
        self.bar[eng] = []
        if dma is not None:
            if dma.sem is None:
                self.dma_bufs.append(dma)
                dma.sem = True
            dma.cnt += 16
            I.semv = (dma, dma.cnt)
            self.dmas.append(I)
        I.deps = list(deps.values())
        for b in r:
            key = eng if dma is None else ("d", len(b.r))
            b.r[key] = I
        for b in w:
            b.w = I
            b.r = {}
        st = self.streams[eng]
        I.idx = len(st)
        st.append(I)
        return I

    def barrier(self):
        last = []
        for e in self.ENGS:
            if e == "sp":
                continue
            if self.streams[e]:
                last.append(self.streams[e][-1])
        lat = {}
        for I in self.dmas:
            lat[id(I.semv[0])] = I
        last.extend(lat.values())
        for e in self.ENGS:
            self.bar[e] = list(last)

    def finish(self):
        lat = {}
        for I in self.dmas:
            lat[id(I.semv[0])] = I
        self.bar["sp"] = list(lat.values())
        self.add("sp", None)

    def emit(self, nc, stack):
        engobj = {"pe": "tensor", "act": "scalar", "dve": "vector", "pool": "gpsimd", "sp": "sync"}
        for e in self.ENGS:
            for I in self.streams[e]:
                for Dp in I.deps:
                    if Dp.dma is None:
                        if Dp.eng == I.eng and I.eng in self.INORDER_FREE:
                            continue
                        Dp.inc = True
        for e in self.ENGS:
            c = 0
            for I in self.streams[e]:
                if I.inc:
                    c += 1
                I.val = c
        sems = {e: stack.enter_context(nc.semaphore("s_" + e)) for e in self.ENGS}
        for i, b in enumerate(self.dma_bufs):
            b.sem = stack.enter_context(nc.semaphore("d%d" % i))
        block = stack.enter_context(nc.Block())

        def body_for(e):
            def body(engine):
                waited = {}
                for I in self.streams[e]:
                    for Dp in I.deps:
                        if Dp.dma is not None:
                            key = id(Dp.semv[0])
                            v = Dp.semv[1]
                            sem = Dp.semv[0].sem
                        else:
                            if Dp.eng == e and e in self.INORDER_FREE:
                                continue
                            key = Dp.eng
                            v = Dp.val
                            sem = sems[Dp.eng]
                        if waited.get(key, 0) >= v:
                            continue
                        waited[key] = v
                        engine.wait_ge(sem, v)
                    if I.fn is not None:
                        ins = I.fn(engine)
                        if I.dma is not None:
                            ins.then_inc(I.dma.sem, 16)
                        elif I.inc:
                            ins.then_inc(sems[e], 1)
            return body

        for e in self.ENGS:
            getattr(block, engobj[e])(body_for(e))


class Arena:
    def __init__(self, ap, nbytes):
        self.ap = ap
        self.n = nbytes
        self.off = 0

    def alloc(self, shape, dt):
        esz = 2 if dt == BF16 else 4
        n = 1
        for s in shape:
            n *= s
        nb = (n * esz + 31) // 32 * 32
        assert self.off + nb <= self.n, "arena overflow %d + %d > %d" % (self.off, nb, self.n)
        v = self.ap[:, self.off // 4:(self.off + nb) // 4]
        if dt != F32:
            v = v.bitcast(dt)
        v = v[:, 0:n]
        self.off += nb
        if len(shape) == 2:
            v = v.rearrange("p (a b) -> p a b", a=shape[0])
        elif len(shape) == 3:
            v = v.rearrange("p (a b c) -> p a b c", a=shape[0], b=shape[1])
        return v


def build_program():
    nc = bass.Bass("TRN2", target_bir_lowering=False)
    dr = lambda name, shape, dt, kind="ExternalInput": nc.dram_tensor(name, shape, dt, kind=kind).ap()
    x_d = dr("x", [S, D], F32)
    pos_d = dr("posT", [128, NBLK], I32)
    win_d = dr("w_in", [D, INW], F32)
    wpool_d = dr("w_pool", [4, 256, 256], F32)
    wout_d = dr("w_out", [D, D], F32)
    wup_d = dr("w_up", [D, 2 * DFF], F32)
    wdown_d = dr("w_down", [DFF, D], F32)
    pp_d = dr("pp", [128, NPP], F32)
    rb_d = dr("rb", [128, NRB], F32)
    cst_d = dr("cst", [128, NCS], F32)
    out_d = dr("out", [S, D], F32, kind="ExternalOutput")
    wups_d = dr("wup_bf16_scratch", [128, 8 * 2 * DFF], BF16, kind="Internal")
    wdns_d = dr("wdown_bf16_scratch", [128, 22 * 1024], BF16, kind="Internal")
    FC_GROUPS = [(0, 6), (6, 12), (12, 17), (17, 22)]
    wups_g = []
    _o = 0
    for (a_, b_) in FC_GROUPS:
        n_ = (b_ - a_) * 128
        wups_g.append(wups_d[:, _o:_o + 16 * n_].rearrange("p (h k c) -> p h k c", h=2, k=8))
        _o += 16 * n_
    wdns_v = wdns_d.rearrange("p (k c) -> p k c", k=22)

    win_v = win_d.rearrange("(k p) c -> p k c", p=128)
    wout_v = wout_d.rearrange("(k p) c -> p k c", p=128)
    wup_v = wup_d.rearrange("(k p) c -> p k c", p=128)
    wdown_v = wdown_d.rearrange("(k p) c -> p k c", p=128)
    wpool_v = wpool_d.rearrange("g (cc p) d -> p g cc d", p=128)

    R = Rec()
    stack = ExitStack()
    ARENA_BYTES = 212000
    arena_t = stack.enter_context(nc.sbuf_tensor("arena", [128, ARENA_BYTES // 4], F32))
    ps = stack.enter_context(nc.psum_tensor("ps", [128, 8, 512], F32))
    A = Arena(arena_t, ARENA_BYTES)
    PB = [Buf("ps%d" % i, excl=True) for i in range(8)]
    trp_bf = ps[:, 0, :].bitcast(BF16)

    def mm(out, lhsT, rhs, start, stop, r, w):
        R.add("pe", lambda e: e.matmul(out, lhsT, rhs, start=start, stop=stop), r=r, w=w)

    def tr(out, in_, ident, r, w):
        R.add("pe", lambda e: e.transpose(out, in_, ident), r=r, w=w)

    def act(out, in_, func, r, w, bias=None, scale=None, accum=None):
        kw = {}
        if bias is not None:
            kw["bias"] = bias
        if scale is not None:
            kw["scale"] = scale
        if accum is not None:
            kw["accum_out"] = accum
        R.add("act", lambda e: e.activation(out, in_, func, **kw), r=r, w=w)

    def tt(eng, out, in0, in1, op, r, w):
        R.add(eng, lambda e: e.tensor_tensor(out, in0, in1, op), r=r, w=w)

    def ts(eng, out, in0, s1, s2, op0, op1, r, w):
        if s2 is None:
            R.add(eng, lambda e: e.tensor_scalar(out, in0, s1, None, op0), r=r, w=w)
        else:
            R.add(eng, lambda e: e.tensor_scalar(out, in0, s1, s2, op0, op1), r=r, w=w)

    def stt(out, in0, scalar, in1, op0, op1, r, w):
        R.add("dve", lambda e: e.scalar_tensor_tensor(out, in0, scalar, in1, op0, op1), r=r, w=w)

    def cp(eng, out, in_, r, w):
        if eng == "act":
            R.add("act", lambda e: e.copy(out, in_), r=r, w=w)
        else:
            R.add(eng, lambda e: e.tensor_copy(out, in_), r=r, w=w)

    def mset(eng, ap, val, w):
        R.add(eng, lambda e: e.memset(ap, val), r=(), w=w)

    def dma(q, out, in_, r, w, sb, nodep=False):
        R.add(q, lambda e: e.dma_start(out=out, in_=in_), r=r, w=w, dma=sb, nodep=nodep)

    pp = A.alloc([NPP], F32)
    PPb = Buf("pp")
    bgh = A.alloc([16], F32)
    BGH = Buf("bgh")
    exps = A.alloc([8], F32)
    EXPS = Buf("exps")
    ident = A.alloc([128], BF16)
    IDENT = Buf("ident")
    neghalf = A.alloc([32], F32)
    NEGH = Buf("neghalf")
    negpi = A.alloc([8], F32)
    NEGPI = Buf("negpi")
    cwv = pp[:, PP_CW:PP_CW + 132].rearrange("p (j c) -> p j c", j=3)
    persist_mark = A.off

    win = A.alloc([8, INW], BF16)
    WIN_G, WIN_U, WIN_Q = Buf("win_g"), Buf("win_u"), Buf("win_q")
    wpool = A.alloc([4, 2, 256], BF16)
    WPOOL = Buf("wpool")
    wout = A.alloc([8, 1024], BF16)
    WOUT = Buf("wout")
    CST = Buf("cst")
    corrp = A.alloc([64], F32)
    rb1 = A.alloc([2176], F32)
    RB1 = Buf("rb1")
    g1 = rb1[:, 0:1024]
    gqk = rb1[:, 1024:2176]
    maskC = A.alloc([512], BF16)
    maskP = A.alloc([512], BF16)
    MASK = Buf("mask")
    posi = A.alloc([NBLK], I32)
    POSI = Buf("posi")
    posf = A.alloc([NBLK], F32)
    ANG = Buf("ang")
    cos_t = A.alloc([NBLK, 8], F32)
    sin_t = A.alloc([NBLK, 8], F32)
    ROPE = Buf("rope")
    onesz = A.alloc([2, 128], BF16)
    ONESZ = Buf("onesz")
    halo = A.alloc([8, 16], F32)
    HALO = [Buf("halo%d" % c) for c in range(8)]
    xn = [A.alloc([1024], F32) for _ in range(2)]
    XN = [Buf("xn%d" % i) for i in range(2)]
    xr = [A.alloc([1024], F32) for _ in range(1)]
    XR = [Buf("xr%d" % i) for i in range(1)]
    junk = A.alloc([1280], BF16)
    JUNK = Buf("junk")
    JUNKN = JUNK
    ss1 = [A.alloc([8], F32) for _ in range(2)]
    SS1 = [Buf("ss1_%d" % i) for i in range(2)]
    hn = [A.alloc([1024], BF16) for _ in range(2)]
    HN = [Buf("hn%d" % i) for i in range(2)]
    hT = [A.alloc([8, 256], BF16) for _ in range(1)]
    HT = [[Buf("hT%d_%d" % (s, j)) for j in range(2)] for s in range(1)]
    gT = A.alloc([16, 256], F32)
    GT = [Buf("gT%d" % c) for c in range(16)]
    u2 = [A.alloc([2, 272], F32) for _ in range(2)]
    U2 = [[Buf("u2_%d_%d" % (s, cc)) for cc in range(2)] for s in range(2)]
    U2H = [[Buf("u2h%d_%d" % (s, cc)) for cc in range(2)] for s in range(2)]
    sA = A.alloc([2, 272], F32)
    sB = A.alloc([2, 272], F32)
    SA, SB = [Buf("sA0"), Buf("sA1")], [Buf("sB0"), Buf("sB1")]
    pooledT = [A.alloc([2, 256], BF16) for _ in range(4)]
    POOLED = [Buf("pooled%d" % s) for s in range(4)]
    ag = A.alloc([8, 256], F32)
    AG = [Buf("ag%d" % c) for c in range(8)]
    sq = junk[:, 0:1152]
    SQ = JUNK
    st18 = A.alloc([3, 18], F32)
    ST18 = Buf("st18")
    qn = A.alloc([18, 64], F32)
    QN = Buf("qn")
    rtmp = A.alloc([4, 18, 8], F32)
    RT = [Buf("rtmp%d" % i) for i in range(4)]
    qkb = [A.alloc([18, 64], BF16) for _ in range(2)]
    QKB = [Buf("qkb%d" % i) for i in range(2)]
    kz = [A.alloc([4, 128], BF16) for _ in range(3)]
    KZ = [Buf("kz%d" % i) for i in range(3)]
    vz = [A.alloc([4, 128], BF16) for _ in range(3)]
    VZ = [Buf("vz%d" % i) for i in range(3)]
    qT = [A.alloc([8, 128], BF16) for _ in range(2)]
    QT = [Buf("qT%d" % i) for i in range(2)]
    kT = [A.alloc([4, 128], BF16) for _ in range(3)]
    KT = [Buf("kT%d" % i) for i in range(3)]
    scr_base = A.off
    PT = [[A.alloc([512], BF16) for _ in range(8)] for _ in range(1)]
    PTB = [[Buf("PT%d_%d" % (s, i)) for i in range(8)] for s in range(1)]
    rec = [A.alloc([4, 128], F32) for _ in range(2)]
    REC = [Buf("rec%d" % i) for i in range(2)]
    A2 = Arena(arena_t, A.off)
    A2.off = scr_base
    cstf = A2.alloc([NCS], F32)
    ang = A2.alloc([NBLK, 8], F32)
    angs = A2.alloc([NBLK, 8], F32)
    angk = A2.alloc([NBLK, 8], F32)
    angi = A2.alloc([NBLK, 8], I32)
    cT = [A.alloc([8, 256], BF16) for _ in range(2)]
    CT = [[Buf("cT%d_%d" % (s, j)) for j in range(2)] for s in range(2)]
    OUTB = [Buf("out%d" % i) for i in range(NBLK)]
    p1_end = A.off
    print('[arena] phase1 end', p1_end, 'of', ARENA_BYTES)

    dma("sp", pp, pp_d[:, :], [], [PPb], PPb)
    dma("sp", cstf, cst_d[:, :], [], [CST], CST)
    dma("sp", rb1, rb_d[:, 0:2176], [], [RB1], RB1)
    dma("sp", posi, pos_d[:, :], [], [POSI], POSI)
    WB = {kd: {"act": Buf("w%s_a" % kd), "dve": Buf("w%s_d" % kd)} for kd in "QUGPO"}
    WIN_Q_L = list(WB["Q"].values())
    WIN_U_L = list(WB["U"].values())
    WIN_G_L = list(WB["G"].values())
    WPOOL_L = list(WB["P"].values())
    WOUT_L = list(WB["O"].values())
    stage_jobs = []
    for k in range(8):
        for h in range(2):
            c0 = 1024 + h * 640
            stage_jobs.append((win[:, k, c0:c0 + 640], win_v[:, k, c0:c0 + 640], "Q", 640))
    for k in range(8):
        stage_jobs.append((win[:, k, 0:1024], win_v[:, k, 0:1024], "U", None))
    for k in range(8):
        for h in range(2):
            c0 = 2304 + h * 1024
            stage_jobs.append((win[:, k, c0:c0 + 1024], win_v[:, k, c0:c0 + 1024], "G", None))
    for h in range(2):
        stage_jobs.append((wpool[:, 2 * h:2 * h + 2], wpool_v[:, 2 * h:2 * h + 2], "P", "p (g c) d -> p g c d"))
    for k in range(8):
        stage_jobs.append((wout[:, k, :], wout_v[:, k, :], "O", None))

    stage_ctr = [0]

    def emit_stage_jobs(kinds):
        todo = [j_ for j_ in stage_jobs if j_[2] in kinds]
        for (dst, src, kd, rr) in todo:
            i = stage_ctr[0]
            stage_ctr[0] += 1
            sl_ = i % 4
            sv = gT[:, 4 * sl_:4 * sl_ + 4, :]
            if isinstance(rr, int):
                sv = sv.rearrange("p a b -> p (a b)")[:, 0:rr]
            elif rr:
                sv = sv.rearrange(rr, g=2)
            else:
                sv = sv.rearrange("p a b -> p (a b)")
            GS = GT[4 * sl_:4 * sl_ + 4]
            dma("sp", sv, src, [], GS, GT[4 * sl_])
            eng = "act" if i % 2 == 0 else "dve"
            if eng == "act":
                R.add("act", (lambda d_, s_: (lambda e: e.copy(d_, s_)))(dst, sv), r=GS, w=[WB[kd][eng]], nodep=True)
            else:
                R.add("dve", (lambda d_, s_: (lambda e: e.tensor_copy(d_, s_)))(dst, sv), r=GS, w=[WB[kd][eng]], nodep=True)

    cp("dve", corrp, cstf[:, CS_CORR:CS_CORR + 64], [CST], [BGH])
    cp("dve", ident, cstf[:, CS_ID:CS_ID + 128], [CST], [IDENT])
    cp("dve", maskC, cstf[:, CS_MC:CS_MC + 512], [CST], [MASK])
    cp("dve", maskP, cstf[:, CS_MP:CS_MP + 512], [CST], [MASK])
    corr = corrp.rearrange("p (g t) -> p g t", g=4)
    ts("dve", bgh, pp[:, PP_BG:PP_BG + 16], 0.5, None, ALU.mult, None, [PPb], [BGH])
    ts("dve", gqk[:, 0:1024], gqk[:, 0:1024], 0.125, None, ALU.mult, None, [RB1], [RB1])
    act(exps, pp[:, PP_SK:PP_SK + 8], AF.Exp, [PPb], [EXPS])
    mset("dve", neghalf, -0.5, [NEGH])
    mset("dve", negpi, -math.pi, [NEGPI])
    mset("pool", halo, 0.0, HALO)
    for i in range(3):
        mset("pool", kz[i], 0.0, [KZ[i]])
        mset("pool", vz[i], 0.0, [VZ[i]])
    mset("pool", onesz, 0.0, [ONESZ])
    mset("pool", onesz[:, 0, 0:64], 1.0, [ONESZ])
    mset("pool", onesz[:, 1, 64:128], 1.0, [ONESZ])
    inv_freq = [float(ROPE_THETA ** (-(2.0 * i) / 16.0)) for i in range(8)]
    TWO_PI = 2.0 * math.pi

    def range_reduce_sin(dst, shift):
        ts("dve", angs, ang, shift, None, ALU.add, None, [ANG], [ANG])
        ts("dve", angi, angs, 1.0 / TWO_PI, None, ALU.mult, None, [ANG], [ANG])
        cp("dve", angk, angi, [ANG], [ANG])
        stt(angs, angk, -TWO_PI, angs, ALU.mult, ALU.add, [ANG], [ANG])
        ts("dve", angk, angs, math.pi, -TWO_PI, ALU.is_gt, ALU.mult, [ANG], [ANG])
        tt("dve", angs, angs, angk, ALU.add, [ANG], [ANG])
        ts("dve", angk, angs, -math.pi, TWO_PI, ALU.is_lt, ALU.mult, [ANG], [ANG])
        tt("dve", angs, angs, angk, ALU.add, [ANG], [ANG])
        ts("dve", angs, angs, math.pi, -math.pi, ALU.min, ALU.max, [ANG], [ANG])
        act(dst, angs, AF.Sin, [ANG], [ROPE])

    def rope_setup():
        cp("dve", posf, posi, [POSI], [ANG])
        for i in range(8):
            ts("dve", ang[:, :, i], posf, inv_freq[i], None, ALU.mult, None, [ANG], [ANG])
        range_reduce_sin(sin_t, 0.0)
        range_reduce_sin(cos_t, math.pi / 2)

    FM = (1, 2)
    SC = (6, 7)

    def norm_pre(gb, src_ap, src_deps, gain, GAINB, xnring, XNring, skip_load=False):
        s3 = gb % len(xnring)
        s2 = gb % 2
        s1 = gb % len(hn)
        if not skip_load:
            dma("sp", xnring[s3], src_ap, src_deps, [XNring[s3]], XNring[s3])
        act(hn[s1], xnring[s3], AF.Square, [XNring[s3]], [HN[s1], SS1[s2]], accum=ss1[s2][:, 0:1])
        ts("dve", ss1[s2][:, 1:2], ss1[s2][:, 0:1], 1.0 / D, EPS, ALU.mult, ALU.add, [SS1[s2]], [SS1[s2]])
        tt("pool", ss1[s2][:, 2:3], ss1[s2][:, 1:2], neghalf[:, 0:1], ALU.pow, [SS1[s2], NEGH], [SS1[s2]])
        stt(hn[s1], xnring[s3], ss1[s2][:, 2:3], gain, ALU.mult, ALU.mult, [XNring[s3], SS1[s2], GAINB], [HN[s1]])

    def norm_tr(gb, hT_dst, HT_dstB):
        s1 = gb % len(hn)
        for k in range(8):
            tr(trp_bf[:, k * 128:(k + 1) * 128], hn[s1][:, k * 128:(k + 1) * 128], ident, [HN[s1], IDENT], [PB[0]])
        cp("act", hT_dst, trp_bf.rearrange("p (k n) -> p k n", k=8), [PB[0]], [HT_dstB])

    def norm_block(gb, src_ap, src_deps, gain, GAINB, hT_dst, HT_dstB, xnring, XNring):
        norm_pre(gb, src_ap, src_deps, gain, GAINB, xnring, XNring)
        norm_tr(gb, hT_dst, HT_dstB)

    NT1 = S // 256
    hs = 0
    HTr = [HT[hs][0], HT[hs][1]]
    ps4 = lambda bk: ps[:, bk, :].rearrange("p (a b) -> p a b", a=4)

    def p1_norm_pre(ti, j):
        gb = 2 * ti + j
        norm_pre(gb, x_d[gb * 128:(gb + 1) * 128, :], [], g1, RB1, xn, XN)

    def p1_norm_tr(ti, j):
        norm_tr(2 * ti + j, hT[hs][:, :, j * 128:(j + 1) * 128], HT[hs][j])

    def P_mm(ti, j):
        gb = 2 * ti + j
        tok = slice(j * 128, (j + 1) * 128)
        for k in range(8):
            lh = hT[hs][:, k, tok]
            mm(ps[:, 3, :], lh, win[:, k, 1024:1536], k == 0, k == 7, [HT[hs][j]] + WIN_Q_L, [PB[3]])
            mm(ps[:, 4, :], lh, win[:, k, 1536:2048], k == 0, k == 7, [HT[hs][j]] + WIN_Q_L, [PB[4]])
            mm(ps[:, 5, 0:256], lh, win[:, k, 2048:2304], k == 0, k == 7, [HT[hs][j]] + WIN_Q_L, [PB[5]])

    def P_chain(ti, j):
        gb = 2 * ti + j
        sl = gb % 3
        qb = gb % 2
        qn2 = qn.rearrange("p h d -> p (h d)")
        tt("dve", qn2[:, 0:1024].rearrange("p (a b) -> p a b", a=2), ps[:, 3:5, :],
           gqk[:, 0:1024].rearrange("p (a b) -> p a b", a=2), ALU.mult, [PB[3], PB[4], RB1], [QN])
        tt("dve", qn2[:, 1024:1152], ps[:, 5, 0:128], gqk[:, 1024:1152], ALU.mult, [PB[5], RB1], [QN])
        act(sq[:, 0:1024].rearrange("p (a b) -> p a b", a=2), ps[:, 3:5, :], AF.Square, [PB[3], PB[4]], [SQ])
        act(sq[:, 1024:1152], ps[:, 5, 0:128], AF.Square, [PB[5]], [SQ])
        vz4 = vz[sl].rearrange("p (kv par) d -> p kv par d", kv=2)
        vsrc = ps[:, 5, 128:256].rearrange("p (kv d) -> p kv d", kv=2)
        cp("act", vz4[:, :, 0, 0:64], vsrc, [PB[5]], [VZ[sl]])
        cp("act", vz4[:, :, 1, 64:128], vsrc, [PB[5]], [VZ[sl]])
        R.add("dve", lambda e: e.tensor_reduce(st18[:, 0, :], sq.rearrange("p (h d) -> p h d", d=64), AX.X, ALU.add),
              r=[SQ], w=[ST18])
        ts("dve", st18[:, 1, :], st18[:, 0, :], 1.0 / 64, EPS, ALU.mult, ALU.add, [ST18], [ST18])
        tt("pool", st18[:, 2, :], st18[:, 1, :], neghalf[:, 0:18], ALU.pow, [ST18, NEGH], [ST18])
        cosb = cos_t[:, gb, :].unsqueeze(1).to_broadcast([128, 18, 8])
        sinb = sin_t[:, gb, :].unsqueeze(1).to_broadcast([128, 18, 8])
        x1 = qn[:, :, 0:8]
        x2 = qn[:, :, 8:16]
        tt("dve", rtmp[:, 0], x1, cosb, ALU.mult, [QN, ROPE], [RT[0]])
        tt("dve", rtmp[:, 1], x2, sinb, ALU.mult, [QN, ROPE], [RT[1]])
        tt("dve", rtmp[:, 2], x2, cosb, ALU.mult, [QN, ROPE], [RT[2]])
        tt("dve", rtmp[:, 3], x1, sinb, ALU.mult, [QN, ROPE], [RT[3]])
        tt("dve", qn[:, :, 0:8], rtmp[:, 0], rtmp[:, 1], ALU.subtract, [RT[0], RT[1]], [QN])
        tt("dve", qn[:, :, 8:16], rtmp[:, 2], rtmp[:, 3], ALU.add, [RT[2], RT[3]], [QN])
        tt("dve", qkb[qb], qn, st18[:, 2, :].unsqueeze(2).to_broadcast([128, 18, 64]), ALU.mult,
           [QN, ST18], [QKB[qb]])
        kz4 = kz[sl].rearrange("p (kv par) d -> p kv par d", kv=2)
        cp("pool", kz4[:, :, 0, 0:64], qkb[qb][:, 16:18, :], [QKB[qb]], [KZ[sl]])
        cp("pool", kz4[:, :, 1, 64:128], qkb[qb][:, 16:18, :], [QKB[qb]], [KZ[sl]])

    def P_tr(ti, j):
        gb = 2 * ti + j
        sl = gb % 3
        qb = gb % 2
        qkb2 = qkb[qb].rearrange("p h d -> p (h d)")
        for c in range(8):
            tr(trp_bf[:, c * 128:(c + 1) * 128], qkb2[:, c * 128:(c + 1) * 128], ident, [QKB[qb], IDENT], [PB[0]])
        cp("act", qT[qb], trp_bf.rearrange("p (k n) -> p k n", k=8), [PB[0]], [QT[qb]])
        for i in range(4):
            tr(trp_bf[:, i * 128:(i + 1) * 128], kz[sl][:, i, :], ident, [KZ[sl], IDENT], [PB[0]])
        cp("act", kT[sl], trp_bf[:, 0:512].rearrange("p (k n) -> p k n", k=4), [PB[0]], [KT[sl]])

    def gates(ti, c, banks=None):
        bank = (banks or SC)[c % 2]
        for k in range(8):
            mm(ps[:, bank, 0:256], win[:, k, 2304 + c * 128:2304 + (c + 1) * 128], hT[hs][:, k, :],
               k == 0, k == 7, WIN_G_L + HTr, [PB[bank]])
        act(gT[:, c, :], ps[:, bank, 0:256], AF.Tanh, [PB[bank], BGH], [GT[c]],
            bias=bgh[:, c:c + 1], scale=0.5)

    UBANK = ((1, 2), (3, 4))

    def C_u(ti, g):
        us = g % 2
        U = u2[us]
        for cc in range(2):
            c = 2 * g + cc
            bank = UBANK[us][cc]
            for k in range(8):
                mm(ps[:, bank, 0:256], win[:, k, c * 128:(c + 1) * 128], hT[hs][:, k, :],
                   k == 0, k == 7, WIN_U_L + HTr, [PB[bank]])
            cp("pool", U[:, cc, 0:16], halo[:, c, :], [HALO[c]], [U2H[us][cc]])
            cp("act", U[:, cc, 16:272], ps[:, bank, 0:256], [PB[bank]], [U2[us][cc]])
            cp("pool", halo[:, c, :], U[:, cc, 256:272], [U2[us][cc]], [HALO[c]])

    def C_pool(ti, g):
        us = g % 2
        U = u2[us]
        Ur = [[U2[us][cc], U2H[us][cc]] for cc in range(2)]
        for cc in range(2):
            tt("dve", sA[:, cc, 1:272], U[:, cc, 1:272], U[:, cc, 0:271], ALU.add, Ur[cc], [SA[cc]])
        cur, CUR = sA, SA
        if g >= 1:
            for cc in range(2):
                tt("dve", sB[:, cc, 3:272], sA[:, cc, 3:272], sA[:, cc, 1:270], ALU.add, [SA[cc]], [SB[cc]])
            cur, CUR = sB, SB
        if g >= 2:
            for cc in range(2):
                tt("dve", sA[:, cc, 7:272], sB[:, cc, 7:272], sB[:, cc, 3:268], ALU.add, [SB[cc]], [SA[cc]])
            cur, CUR = sA, SA
        if g >= 3:
            for cc in range(2):
                tt("dve", sB[:, cc, 15:272], sA[:, cc, 15:272], sA[:, cc, 7:264], ALU.add, [SA[cc]], [SB[cc]])
            cur, CUR = sB, SB
        if ti == 0:
            for cc in range(2):
                tt("dve", cur[:, cc, 16:32], cur[:, cc, 16:32], corr[:, g, :], ALU.mult, [CUR[cc], BGH], [CUR[cc]])
        pl = pooledT[g]
        for cc in range(2):
            stt(pl[:, cc, :], cur[:, cc, 16:272], 1.0 / (2 ** (g + 1)), U[:, cc, 16:272],
                ALU.mult, ALU.subtract, [CUR[cc]] + Ur[cc], [POOLED[g]])

    def C_map(ti, g):
        pl = pooledT[g]
        for dc in range(2):
            ch = 2 * g + dc
            bk = (5, 1, 2, 3, 4, 6, 7, 5)[ch]
            hf = 1 if ch == 7 else 0
            o = ps[:, bk, hf * 256:(hf + 1) * 256]
            for cc in range(2):
                mm(o, wpool[:, g, cc, dc * 128:(dc + 1) * 128], pl[:, cc, :],
                   cc == 0, cc == 1, WPOOL_L + [POOLED[g]], [PB[bk]])
        for dc in range(2):
            ch = 2 * g + dc
            bk = (5, 1, 2, 3, 4, 6, 7, 5)[ch]
            hf = 1 if ch == 7 else 0
            o = ps[:, bk, hf * 256:(hf + 1) * 256]
            stt(ag[:, ch, :], o, pp[:, PP_PS + ch:PP_PS + ch + 1], gT[:, ch, :],
                ALU.mult, ALU.mult, [PB[bk], PPb, GT[ch]], [AG[ch]])

    def S_core(ti, j):
        gb = 2 * ti + j
        sl = gb % 3
        psl = (gb - 1) % 3
        qb = gb % 2
        kbs = [(psl, maskP), (sl, maskC)] if gb > 0 else [(sl, maskC)]
        cnt = 0
        for kv in range(2):
            for par in range(2):
                for kbi, (ksl, mk) in enumerate(kbs):
                    bank = SC[cnt % 2]
                    cnt += 1
                    u = (kv * 2 + par) * 2 + kbi
                    mm(ps4(bank), kT[ksl][:, kv * 2 + par, :],
                       qT[qb][:, kv * 4:(kv + 1) * 4, :], True, False, [KT[ksl], QT[qb]], [PB[bank]])
                    mm(ps[:, bank, :], ident, mk, False, True, [IDENT, MASK], [PB[bank]])
                    act(PT[0][u], ps[:, bank, :], AF.Exp, [PB[bank]], [PTB[0][u]])

    PVB = (((1, 2), (3, 4)), ((1, 2), (6, 7)))

    def S_pv(ti, j):
        gb = 2 * ti + j
        sl = gb % 3
        psl = (gb - 1) % 3
        kbs = [(psl, maskP), (sl, maskC)] if gb > 0 else [(sl, maskC)]
        for kv in range(2):
            pvb, denb = PVB[j][kv]
            n = 2 * len(kbs)
            i = 0
            for par in range(2):
                for kbi, (ksl, mk) in enumerate(kbs):
                    u = (kv * 2 + par) * 2 + kbi
                    mm(ps[:, pvb, :], vz[ksl][:, kv * 2 + par, :], PT[0][u], i == 0, i == n - 1,
                       [VZ[ksl], PTB[0][u]], [PB[pvb]])
                    i += 1
            i = 0
            for par in range(2):
                for kbi, (ksl, mk) in enumerate(kbs):
                    u = (kv * 2 + par) * 2 + kbi
                    mm(ps[:, denb, :], onesz[:, par, :], PT[0][u], i == 0, i == n - 1,
                       [ONESZ, PTB[0][u]], [PB[denb]])
                    i += 1

    def N_chain(ti, j):
        N_a(ti, j)
        N_b(ti, j)

    def N_a(ti, j):
        banks = PVB[j]
        for kv in range(2):
            ch = slice(kv * 4, (kv + 1) * 4)
            tt("dve", rec[kv], ps4(banks[kv][1]), exps[:, ch].unsqueeze(2).to_broadcast([128, 4, 128]), ALU.add,
               [PB[banks[kv][1]], EXPS], [REC[kv]])
        for kv in range(2):
            rk = rec[kv]
            R.add("dve", (lambda rk: (lambda e: e.reciprocal(rk, rk)))(rk), r=[REC[kv]], w=[REC[kv]])

    def N_b(ti, j):
        cs = ti % 2
        tok = slice(j * 128, (j + 1) * 128)
        banks = PVB[j]
        for kv in range(2):
            tt("dve", rec[kv], ps4(banks[kv][0]), rec[kv], ALU.mult, [PB[banks[kv][0]], REC[kv]], [REC[kv]])
        for kv in range(2):
            gch = slice(8 + kv * 4, 8 + (kv + 1) * 4)
            tt("pool", rec[kv], rec[kv], gT[:, gch, tok], ALU.mult, [REC[kv]] + GT[gch], [REC[kv]])
        for kv in range(2):
            ch = slice(kv * 4, (kv + 1) * 4)
            tt("pool", cT[cs][:, ch, tok], rec[kv], ag[:, ch, tok], ALU.add, [REC[kv]] + AG[ch], [CT[cs][j]])

    def E_mm(ti, j):
        gb = 2 * ti + j
        cs = ti % 2
        rs = 0
        tok = slice(j * 128, (j + 1) * 128)
        b0, b1 = (3, 4) if j == 0 else (1, 2)
        dma("sp", xr[rs], x_d[gb * 128:(gb + 1) * 128, :], [], [XR[rs]], XR[rs])
        for k in range(8):
            lh = cT[cs][:, k, tok]
            mm(ps[:, b0, :], lh, wout[:, k, 0:512], k == 0, k == 7, [CT[cs][j]] + WOUT_L, [PB[b0]])
            mm(ps[:, b1, :], lh, wout[:, k, 512:1024], k == 0, k == 7, [CT[cs][j]] + WOUT_L, [PB[b1]])

    def E_add(ti, j):
        gb = 2 * ti + j
        rs = 0
        b0, b1 = (3, 4) if j == 0 else (1, 2)
        xr3 = xr[rs].rearrange("p (a b) -> p a b", a=2)
        tt("dve", xr3, ps[:, b0:b1 + 1, :], xr3, ALU.add, [PB[b0], PB[b1], XR[rs]], [XR[rs]])
        dma("sp", out_d[gb * 128:(gb + 1) * 128, :], xr[rs], [XR[rs]], [OUTB[gb]], XR[rs])

    WUPS, WDNS = Buf("wups"), Buf("wdns")
    conv_jobs = []
    for gi_, (a_, b_) in enumerate(FC_GROUPS):
        for h_ in range(2):
            for k in range(8):
                conv_jobs.append((wups_g[gi_][:, h_, k, :],
                                  wup_v[:, k, h_ * DFF + a_ * 128:h_ * DFF + b_ * 128], WUPS))
    for kk in range(22):
        conv_jobs.append((wdns_v[:, kk, :], wdown_v[:, kk, :], WDNS))

    def emit_conv(n):
        for _ in range(n):
            if conv_jobs:
                o, i, B_ = conv_jobs.pop(0)
                dma("pool", o, i, [], [B_], B_, nodep=True)

    for j in range(2):
        p1_norm_pre(0, j)
        p1_norm_tr(0, j)
    emit_stage_jobs("Q")
    rope_setup()
    P_mm(0, 0)
    P_chain(0, 0)
    emit_stage_jobs("U")
    for ti in range(NT1):
        P_mm(ti, 1)
        P_chain(ti, 1)
        C_u(ti, 0)
        C_u(ti, 1)
        if ti == 0:
            emit_stage_jobs("GPO")
        if ti == 0:
            for c in range(0, 4):
                gates(ti, c)
            C_pool(ti, 0)
            C_u(ti, 2)
            for c in range(4, 8):
                gates(ti, c)
            P_tr(ti, 0)
            ts("pool", gT[:, 0:8, :], gT[:, 0:8, :], 0.5, 0.5, ALU.mult, ALU.add, GT[0:8], GT[0:8])
            C_pool(ti, 1)
            C_u(ti, 3)
            for c in range(8, 16):
                gates(ti, c)
        else:
            for c in range(8, 12):
                gates(ti, c)
            C_pool(ti, 0)
            C_u(ti, 2)
            for c in range(12, 16):
                gates(ti, c)
            P_tr(ti, 0)
            C_pool(ti, 1)
            C_u(ti, 3)
        ts("pool", gT[:, 8:16, :], gT[:, 8:16, :], 0.5, 0.5, ALU.mult, ALU.add, GT[8:16], GT[8:16])
        P_tr(ti, 1)
        C_pool(ti, 2)
        C_pool(ti, 3)
        for g in range(4):
            C_map(ti, g)
        nxt = ti + 1 < NT1
        if nxt:
            p1_norm_pre(ti + 1, 0)
            p1_norm_pre(ti + 1, 1)
        S_core(ti, 0)
        S_pv(ti, 0)
        if nxt:
            p1_norm_tr(ti + 1, 0)
        S_core(ti, 1)
        N_chain(ti, 0)
        S_pv(ti, 1)
        if nxt:
            p1_norm_tr(ti + 1, 1)
        E_mm(ti, 0)
        if nxt:
            for c in range(0, 8):
                gates(ti + 1, c, banks=(5, 0))
        N_a(ti, 1)
        E_add(ti, 0)
        if nxt:
            P_mm(ti + 1, 0)
        N_b(ti, 1)
        if nxt:
            ts("pool", gT[:, 0:8, :], gT[:, 0:8, :], 0.5, 0.5, ALU.mult, ALU.add, GT[0:8], GT[0:8])
            P_chain(ti + 1, 0)
        E_mm(ti, 1)
        E_add(ti, 1)
        emit_conv(6)
    emit_conv(100)

    R.barrier()
    A.off = persist_mark
    fc_groups = FC_GROUPS
    wupg = [A.alloc([2, 8, (b_ - a_) * 128], BF16) for (a_, b_) in fc_groups]
    WUPG = [Buf("wup%d" % i) for i in range(4)]
    fc2grp = {}
    for gi, (a, b) in enumerate(fc_groups):
        for fc in range(a, b):
            fc2grp[fc] = gi
    wdown = A.alloc([22, 1024], BF16)
    WDN = [Buf("wdn0"), Buf("wdn1")]
    g2 = A.alloc([1024], F32)
    G2 = Buf("g2")
    chalo = A.alloc([44, 2], F32)
    CH = [Buf("chalo%d" % i) for i in range(44)]
    xn2 = [A.alloc([1024], F32) for _ in range(2)]
    XN2 = [Buf("xn2_%d" % i) for i in range(2)]
    xr2 = [A.alloc([1024], F32) for _ in range(2)]
    XR2 = [Buf("xr2_%d" % i) for i in range(2)]
    ss1 = [A.alloc([8], F32) for _ in range(2)]
    hn = [A.alloc([1024], BF16) for _ in range(1)]
    hT2 = A.alloc([8, 512], BF16)
    HT2 = [Buf("hT2_%d" % j) for j in range(4)]
    yb = [[A.alloc([512], F32) for _ in range(2)] for _ in range(2)]
    YB = [[Buf("y%d_%d" % (h, s)) for s in range(2)] for h in range(2)]
    ub = [[A.alloc([514], F32) for _ in range(2)] for _ in range(2)]
    UH = [[Buf("uh%d_%d" % (h, s)) for s in range(2)] for h in range(2)]
    UM = [[Buf("um%d_%d" % (h, s)) for s in range(2)] for h in range(2)]
    sg = [A.alloc([512], F32) for _ in range(2)]
    SG = [Buf("sg%d" % s) for s in range(2)]
    hid = A.alloc([22, 512], BF16)
    HID = [Buf("hid%d" % fc) for fc in range(22)]
    print('[arena] phase2 end', A.off)

    dma("sp", g2, rb_d[:, RB_G2:RB_G2 + 1024], [], [G2], G2)
    mset("pool", chalo, 0.0, CH)

    def p2_wload(gi):
        dma("sp", wupg[gi].rearrange("p h k c -> p (h k c)"), wups_g[gi].rearrange("p h k c -> p (h k c)"),
            [WUPS], [WUPG[gi]], WUPG[gi])

    def p2_xload(j):
        dma("sp", xn2[j % 2], out_d[j * 128:(j + 1) * 128, :], [OUTB[j]], [XN2[j % 2]], XN2[j % 2])

    GB = ((1, 2, 3), (4, 5, 6))
    NT2 = S // 512

    def p2_norm_pre(ti, j, skip_load=False):
        gb = 4 * ti + j
        norm_pre(gb, out_d[gb * 128:(gb + 1) * 128, :], [OUTB[gb]], g2, G2, xn2, XN2, skip_load=skip_load)

    def p2_loads(ti, j):
        gb = 4 * ti + j
        rs = gb % 2
        dma("sp", xr2[rs], out_d[gb * 128:(gb + 1) * 128, :], [OUTB[gb]], [XR2[rs]], XR2[rs])
        if ti + 1 < NT2:
            gn = 4 * (ti + 1) + j
            dma("sp", xn2[gn % 2], out_d[gn * 128:(gn + 1) * 128, :], [OUTB[gn]], [XN2[gn % 2]], XN2[gn % 2])

    def p2_norm_tr(ti, j):
        norm_tr(4 * ti + j, hT2[:, :, j * 128:(j + 1) * 128], HT2[j])

    def ffn_front(fc):
        rg = fc % 2
        for half in range(2):
            chn = half * 22 + fc
            cp("pool", ub[half][rg][:, 0:2], chalo[:, chn, :], [CH[chn]], [UH[half][rg]])
        for half in range(2):
            chn = half * 22 + fc
            col0 = half * DFF + fc * 128
            bank = GB[half][fc % 3]
            for k in range(8):
                gi_ = fc2grp[fc]
                lc = (fc - fc_groups[gi_][0]) * 128
                mm(ps[:, bank, :], wupg[gi_][:, half, k, lc:lc + 128], hT2[:, k, :], k == 0, k == 7,
                   [WUPG[gi_]] + HT2, [PB[bank]])
        for half in range(2):
            chn = half * 22 + fc
            bank = GB[half][fc % 3]
            act(yb[half][rg], ps[:, bank, :], AF.Identity, [PB[bank], PPb], [YB[half][rg]],
                bias=pp[:, PP_CB + chn:PP_CB + chn + 1], scale=cwv[:, 2, chn:chn + 1])
            cp("act", ub[half][rg][:, 2:514], ps[:, bank, :], [PB[bank]], [UM[half][rg]])
        for half in range(2):
            chn = half * 22 + fc
            cp("pool", chalo[:, chn, :], ub[half][rg][:, 512:514], [UM[half][rg]], [CH[chn]])
        for tap, off in ((1, 1), (0, 0)):
            for half in range(2):
                chn = half * 22 + fc
                y = yb[half][rg]
                stt(y, ub[half][rg][:, off:off + 512], cwv[:, tap, chn:chn + 1], y, ALU.mult, ALU.add,
                    [UH[half][rg], UM[half][rg], PPb, YB[half][rg]], [YB[half][rg]])

    def ffn_silu(fc):
        rg = fc % 2
        act(sg[rg], yb[0][rg], AF.Silu, [YB[0][rg]], [SG[rg]])

    def ffn_mult(fc):
        rg = fc % 2
        tt("pool", hid[:, fc, :], sg[rg], yb[1][rg], ALU.mult, [SG[rg], YB[1][rg]], [HID[fc]])

    p2_xload(0)
    p2_xload(1)
    p2_wload(0)
    for j in range(4):
        p2_norm_pre(0, j, skip_load=True)
        p2_norm_tr(0, j)
        if j + 2 < 4:
            p2_xload(j + 2)
        if j == 1:
            p2_wload(1)
            dma("sp", wdown[:, 0:11, :], wdns_v[:, 0:11, :], [WDNS], [WDN[0]], WDN[0])
    p2_wload(2)
    p2_wload(3)
    dma("sp", wdown[:, 11:22, :], wdns_v[:, 11:22, :], [WDNS], [WDN[1]], WDN[1])
    for ti in range(NT2):
        nxt2 = ti + 1 < NT2
        for fc in range(22):
            ffn_front(fc)
            if fc > 0:
                ffn_silu(fc - 1)
                ffn_mult(fc - 1)
            if fc == 17:
                p2_loads(ti, 0)
                if nxt2:
                    p2_norm_pre(ti + 1, 0, skip_load=True)
        ffn_silu(21)
        ffn_mult(21)
        for j in range(4):
            gb = 4 * ti + j
            rs = gb % 2
            tok = slice(j * 128, (j + 1) * 128)
            if j + 1 < 4:
                p2_loads(ti, j + 1)
            if nxt2:
                p2_norm_tr(ti + 1, j)
                if j + 1 < 4:
                    p2_norm_pre(ti + 1, j + 1, skip_load=True)
            g0 = 1 if j % 2 == 0 else 4
            for fc in range(22):
                lh = hid[:, fc, tok]
                mm(ps[:, g0, :], lh, wdown[:, fc, 0:512], fc == 0, fc == 21, [HID[fc], WDN[fc // 11]], [PB[g0]])
                mm(ps[:, g0 + 1, :], lh, wdown[:, fc, 512:1024], fc == 0, fc == 21, [HID[fc], WDN[fc // 11]], [PB[g0 + 1]])
            xr3 = xr2[rs].rearrange("p (a b) -> p a b", a=2)
            tt("dve", xr3, ps[:, g0:g0 + 2, :], xr3, ALU.add, [PB[g0], PB[g0 + 1], XR2[rs]], [XR2[rs]])
            dma("sp", out_d[gb * 128:(gb + 1) * 128, :], xr2[rs], [XR2[rs]], [OUTB[gb]], XR2[rs])

    R.finish()
    R.emit(nc, stack)
    stack.close()
    return nc


def _host_layout(inputs):
    f = lambda a: np.ascontiguousarray(np.asarray(a), dtype=np.float32)
    b_gate = f(inputs["b_gate"])[0]
    pool_scale = f(inputs["pool_scale"])[0]
    sinks = f(inputs["sinks"])[0]
    conv_w = f(inputs["conv_w"])[0]
    conv_b = f(inputs["conv_b"])[0]
    pp = np.zeros((128, NPP), np.float32)
    pp[:, PP_BG:PP_BG + 16] = b_gate.reshape(16, 128).T
    pp[:, PP_PS:PP_PS + 8] = pool_scale.reshape(8, 128).T
    pp[:, PP_SK:PP_SK + 8] = np.repeat(sinks.reshape(8, 2).T, 64, axis=0)
    pp[:, PP_CW:PP_CW + 132] = conv_w.reshape(3, 44, 128).transpose(2, 0, 1).reshape(128, 132)
    pp[:, PP_CB:PP_CB + 44] = conv_b.reshape(44, 128).T
    rb = np.zeros((128, NRB), np.float32)
    rb[:, RB_G1:RB_G1 + 1024] = f(inputs["attn_norm"])[0][None, :]
    rb[:, RB_GQ:RB_GQ + 1024] = np.tile(f(inputs["q_norm"])[0], 16)[None, :]
    rb[:, RB_GQ + 1024:RB_GQ + 1152] = np.tile(f(inputs["k_norm"])[0], 2)[None, :]
    rb[:, RB_G2:RB_G2 + 1024] = f(inputs["ffn_norm"])[0][None, :]
    cst = np.zeros((128, NCS), np.float32)
    cst[:, CS_ID:CS_ID + 128] = np.eye(128, dtype=np.float32)
    jj = np.arange(128)[:, None]
    ii = np.arange(128)[None, :]
    mc = np.where(jj <= ii, 0.0, NEG).astype(np.float32)
    mp = np.where(jj > ii, 0.0, NEG).astype(np.float32)
    cst[:, CS_MC:CS_MC + 512] = np.tile(mc, (1, 4))
    cst[:, CS_MP:CS_MP + 512] = np.tile(mp, (1, 4))
    corr = np.ones((4, 16), np.float32)
    for g in range(4):
        w = 2 ** (g + 1)
        for t in range(16):
            corr[g, t] = w / min(t + 1, w)
    cst[:, CS_CORR:CS_CORR + 64] = corr.reshape(1, 64)
    return pp, rb, cst


_NC_CACHE = {}


def kernel(**inputs):
    x = np.ascontiguousarray(np.asarray(inputs["x"]), dtype=np.float32)
    positions = np.ascontiguousarray(np.asarray(inputs["positions"]), dtype=np.int32)
    pp, rb, cst = _host_layout(inputs)
    f = lambda a: np.ascontiguousarray(np.asarray(a), dtype=np.float32)
    w_in = f(inputs["w_in"])[0]
    w_pool = f(inputs["w_pool"])[0]
    w_out = f(inputs["w_out"])[0]
    w_up = f(inputs["w_up"])[0]
    w_down = f(inputs["w_down"])[0]
    if "nc" not in _NC_CACHE:
        _NC_CACHE["nc"] = build_program()
    nc = _NC_CACHE["nc"]
    n = x.shape[0]
    in_maps = []
    for i in range(n):
        in_maps.append({
            "x": x[i],
            "posT": np.ascontiguousarray(positions[i].reshape(NBLK, 128).T),
            "w_in": w_in, "w_pool": w_pool, "w_out": w_out, "w_up": w_up, "w_down": w_down,
            "pp": pp, "rb": rb, "cst": cst,
        })
    res = run_bass_kernel_spmd(nc, in_maps, core_ids=list(range(n)))
    out = np.stack([np.asarray(r["out"]) for r in res.results], axis=0)
    return out.astype(np.float32)
```

```python
import math
from contextlib import ExitStack

import numpy as np
import concourse.bass as bass
import concourse.mybir as mybir
from concourse.bass_utils import run_bass_kernel_spmd

F32 = mybir.dt.float32
BF16 = mybir.dt.bfloat16
I32 = mybir.dt.int32
ALU = mybir.AluOpType
AF = mybir.ActivationFunctionType
AX = mybir.AxisListType

S = 4096
D = 1024
NBLK = S // 128
INW = 4352
DFF = 2816
EPS = 1e-6
ROPE_THETA = 500000.0
NEG = -30000.0

PP_BG = 0
PP_PS = 16
PP_SK = 24
PP_CW = 32
PP_CB = 164
NPP = 208
RB_G1 = 0
RB_GQ = 1024
RB_G2 = 2176
NRB = 3200
CS_ID = 0
CS_MC = 128
CS_MP = 640
CS_CORR = 1152
NCS = 1216


import os
STRICT_WAR = False


class Buf:
    __slots__ = ("name", "w", "r", "sem", "cnt", "excl")

    def __init__(self, name, excl=False):
        self.name = name
        self.excl = excl
        self.w = None
        self.r = {}
        self.sem = None
        self.cnt = 0


class Ins:
    __slots__ = ("eng", "fn", "deps", "idx", "inc", "dma", "semv", "val")


class Rec:
    ENGS = ("pe", "act", "dve", "pool", "sp")
    INORDER_FREE = ("pe", "sp")

    def __init__(self):
        self.streams = {e: [] for e in self.ENGS}
        self.bar = {e: [] for e in self.ENGS}
        self.dmas = []
        self.dma_bufs = []

    def add(self, eng, fn, r=(), w=(), dma=None, nodep=False):
        I = Ins()
        I.eng = eng
        I.fn = fn
        I.dma = dma
        I.inc = False
        I.val = 0
        I.semv = None
        deps = {}
        for b in r:
            if b.w is not None:
                deps[id(b.w)] = b.w
            if b.excl:
                for q in b.r.values():
                    if q.eng != eng:
                        deps[id(q)] = q
        for b in w:
            if nodep:
                continue
            if b.w is not None:
                deps[id(b.w)] = b.w
            for q in b.r.values():
                if q.dma is None and q.eng == eng and dma is None and not STRICT_WAR:
                    continue
                deps[id(q)] = q
        for q in self.bar[eng]:
            deps[id(q)] = q
        self.bar[eng] = []
        if dma is not None:
            if dma.sem is None:
                self.dma_bufs.append(dma)
                dma.sem = True
            dma.cnt += 16
            I.semv = (dma, dma.cnt)
            self.dmas.append(I)
        I.deps = list(deps.values())
        for b in r:
            key = eng if dma is None else ("d", len(b.r))
            b.r[key] = I
        for b in w:
            b.w = I
            b.r = {}
        st = self.streams[eng]
        I.idx = len(st)
        st.append(I)
        return I

    def barrier(self):
        last = []
        for e in self.ENGS:
            if e == "sp":
                continue
            if self.streams[e]:
                last.append(self.streams[e][-1])
        lat = {}
        for I in self.dmas:
            lat[id(I.semv[0])] = I
        last.extend(lat.values())
        for e in self.ENGS:
            self.bar[e] = list(last)

    def finish(self):
        lat = {}
        for I in self.dmas:
            lat[id(I.semv[0])] = I
        self.bar["sp"] = list(lat.values())
        self.add("sp", None)

    def emit(self, nc, stack):
        engobj = {"pe": "tensor", "act": "scalar", "dve": "vector", "pool": "gpsimd", "sp": "sync"}
        for e in self.ENGS:
            for I in self.streams[e]:
                for Dp in I.deps:
                    if Dp.dma is None:
                        if Dp.eng == I.eng and I.eng in self.INORDER_FREE:
                            continue
                        Dp.inc = True
        for e in self.ENGS:
            c = 0
            for I in self.streams[e]:
                if I.inc:
                    c += 1
                I.val = c
        sems = {e: stack.enter_context(nc.semaphore("s_" + e)) for e in self.ENGS}
        for i, b in enumerate(self.dma_bufs):
            b.sem = stack.enter_context(nc.semaphore("d%d" % i))
        block = stack.enter_context(nc.Block())

        def body_for(e):
            def body(engine):
                waited = {}
                for I in self.streams[e]:
                    for Dp in I.deps:
                        if Dp.dma is not None:
                            key = id(Dp.semv[0])
                            v = Dp.semv[1]
                            sem = Dp.semv[0].sem
                        else:
                            if Dp.eng == e and e in self.INORDER_FREE:
                                continue
                            key = Dp.eng
                            v = Dp.val
                            sem = sems[Dp.eng]
                        if waited.get(key, 0) >= v:
                            continue
                        waited[key] = v
                        engine.wait_ge(sem, v)
                    if I.fn is not None:
                        ins = I.fn(engine)
                        if I.dma is not None:
                            ins.then_inc(I.dma.sem, 16)
                        elif I.inc:
                            ins.then_inc(sems[e], 1)
            return body

        for e in self.ENGS:
            getattr(block, engobj[e])(body_for(e))


class Arena:
    def __init__(self, ap, nbytes):
        self.ap = ap
        self.n = nbytes
        self.off = 0

    def alloc(self, shape, dt):
        esz = 2 if dt == BF16 else 4
        n = 1
        for s in shape:
            n *= s
        nb = (n * esz + 31) // 32 * 32
        assert self.off + nb <= self.n, "arena overflow %d + %d > %d" % (self.off, nb, self.n)
        v = self.ap[:, self.off // 4:(self.off + nb) // 4]
        if dt != F32:
            v = v.bitcast(dt)
        v = v[:, 0:n]
        self.off += nb
        if len(shape) == 2:
            v = v.rearrange("p (a b) -> p a b", a=shape[0])
        elif len(shape) == 3:
            v = v.rearrange("p (a b c) -> p a b c", a=shape[0], b=shape[1])
        return v


def build_program():
    nc = bass.Bass("TRN2", target_bir_lowering=False)
    dr = lambda name, shape, dt, kind="ExternalInput": nc.dram_tensor(name, shape, dt, kind=kind).ap()
    x_d = dr("x", [S, D], F32)
    pos_d = dr("posT", [128, NBLK], I32)
    win_d = dr("w_in", [D, INW], F32)
    wpool_d = dr("w_pool", [4, 256, 256], F32)
    wout_d = dr("w_out", [D, D], F32)
    wup_d = dr("w_up", [D, 2 * DFF], F32)
    wdown_d = dr("w_down", [DFF, D], F32)
    pp_d = dr("pp", [128, NPP], F32)
    rb_d = dr("rb", [128, NRB], F32)
    cst_d = dr("cst", [128, NCS], F32)
    out_d = dr("out", [S, D], F32, kind="ExternalOutput")
    wups_d = dr("wup_bf16_scratch", [128, 8 * 2 * DFF], BF16, kind="Internal")
    wdns_d = dr("wdown_bf16_scratch", [128, 22 * 1024], BF16, kind="Internal")
    FC_GROUPS = [(0, 6), (6, 12), (12, 17), (17, 22)]
    wups_g = []
    _o = 0
    for (a_, b_) in FC_GROUPS:
        n_ = (b_ - a_) * 128
        wups_g.append(wups_d[:, _o:_o + 16 * n_].rearrange("p (h k c) -> p h k c", h=2, k=8))
        _o += 16 * n_
    wdns_v = wdns_d.rearrange("p (k c) -> p k c", k=22)

    win_v = win_d.rearrange("(k p) c -> p k c", p=128)
    wout_v = wout_d.rearrange("(k p) c -> p k c", p=128)
    wup_v = wup_d.rearrange("(k p) c -> p k c", p=128)
    wdown_v = wdown_d.rearrange("(k p) c -> p k c", p=128)
    wpool_v = wpool_d.rearrange("g (cc p) d -> p g cc d", p=128)

    R = Rec()
    stack = ExitStack()
    ARENA_BYTES = 212000
    arena_t = stack.enter_context(nc.sbuf_tensor("arena", [128, ARENA_BYTES // 4], F32))
    ps = stack.enter_context(nc.psum_tensor("ps", [128, 8, 512], F32))
    A = Arena(arena_t, ARENA_BYTES)
    PB = [Buf("ps%d" % i, excl=True) for i in range(8)]
    trp_bf = ps[:, 0, :].bitcast(BF16)

    def mm(out, lhsT, rhs, start, stop, r, w):
        R.add("pe", lambda e: e.matmul(out, lhsT, rhs, start=start, stop=stop), r=r, w=w)

    def tr(out, in_, ident, r, w):
        R.add("pe", lambda e: e.transpose(out, in_, ident), r=r, w=w)

    def act(out, in_, func, r, w, bias=None, scale=None, accum=None):
        kw = {}
        if bias is not None:
            kw["bias"] = bias
        if scale is not None:
            kw["scale"] = scale
        if accum is not None:
            kw["accum_out"] = accum
        R.add("act", lambda e: e.activation(out, in_, func, **kw), r=r, w=w)

    def tt(eng, out, in0, in1, op, r, w):
        R.add(eng, lambda e: e.tensor_tensor(out, in0, in1, op), r=r, w=w)

    def ts(eng, out, in0, s1, s2, op0, op1, r, w):
        if s2 is None:
            R.add(eng, lambda e: e.tensor_scalar(out, in0, s1, None, op0), r=r, w=w)
        else:
            R.add(eng, lambda e: e.tensor_scalar(out, in0, s1, s2, op0, op1), r=r, w=w)

    def stt(out, in0, scalar, in1, op0, op1, r, w):
        R.add("dve", lambda e: e.scalar_tensor_tensor(out, in0, scalar, in1, op0, op1), r=r, w=w)

    def cp(eng, out, in_, r, w):
        if eng == "act":
            R.add("act", lambda e: e.copy(out, in_), r=r, w=w)
        else:
            R.add(eng, lambda e: e.tensor_copy(out, in_), r=r, w=w)

    def mset(eng, ap, val, w):
        R.add(eng, lambda e: e.memset(ap, val), r=(), w=w)

    def dma(q, out, in_, r, w, sb, nodep=False):
        R.add(q, lambda e: e.dma_start(out=out, in_=in_), r=r, w=w, dma=sb, nodep=nodep)

    pp = A.alloc([NPP], F32)
    PPb = Buf("pp")
    bgh = A.alloc([16], F32)
    BGH = Buf("bgh")
    exps = A.alloc([8], F32)
    EXPS = Buf("exps")
    ident = A.alloc([128], BF16)
    IDENT = Buf("ident")
    neghalf = A.alloc([32], F32)
    NEGH = Buf("neghalf")
    negpi = A.alloc([8], F32)
    NEGPI = Buf("negpi")
    cwv = pp[:, PP_CW:PP_CW + 132].rearrange("p (j c) -> p j c", j=3)
    persist_mark = A.off

    win = A.alloc([8, INW], BF16)
    WIN_G, WIN_U, WIN_Q = Buf("win_g"), Buf("win_u"), Buf("win_q")
    wpool = A.alloc([4, 2, 256], BF16)
    WPOOL = Buf("wpool")
    wout = A.alloc([8, 1024], BF16)
    WOUT = Buf("wout")
    CST = Buf("cst")
    corrp = A.alloc([64], F32)
    rb1 = A.alloc([2176], F32)
    RB1 = Buf("rb1")
    g1 = rb1[:, 0:1024]
    gqk = rb1[:, 1024:2176]
    maskC = A.alloc([512], BF16)
    maskP = A.alloc([512], BF16)
    MASK = Buf("mask")
    posi = A.alloc([NBLK], I32)
    POSI = Buf("posi")
    posf = A.alloc([NBLK], F32)
    ANG = Buf("ang")
    cos_t = A.alloc([NBLK, 8], F32)
    sin_t = A.alloc([NBLK, 8], F32)
    ROPE = Buf("rope")
    onesz = A.alloc([2, 128], BF16)
    ONESZ = Buf("onesz")
    halo = A.alloc([8, 16], F32)
    HALO = [Buf("halo%d" % c) for c in range(8)]
    xn = [A.alloc([1024], F32) for _ in range(2)]
    XN = [Buf("xn%d" % i) for i in range(2)]
    xr = [A.alloc([1024], F32) for _ in range(1)]
    XR = [Buf("xr%d" % i) for i in range(1)]
    junk = A.alloc([1280], BF16)
    JUNK = Buf("junk")
    JUNKN = JUNK
    ss1 = [A.alloc([8], F32) for _ in range(2)]
    SS1 = [Buf("ss1_%d" % i) for i in range(2)]
    hn = [A.alloc([1024], BF16) for _ in range(2)]
    HN = [Buf("hn%d" % i) for i in range(2)]
    hT = [A.alloc([8, 256], BF16) for _ in range(1)]
    HT = [[Buf("hT%d_%d" % (s, j)) for j in range(2)] for s in range(1)]
    gT = A.alloc([16, 256], F32)
    GT = [Buf("gT%d" % c) for c in range(16)]
    u2 = [A.alloc([2, 272], F32) for _ in range(2)]
    U2 = [[Buf("u2_%d_%d" % (s, cc)) for cc in range(2)] for s in range(2)]
    U2H = [[Buf("u2h%d_%d" % (s, cc)) for cc in range(2)] for s in range(2)]
    sA = A.alloc([2, 272], F32)
    sB = A.alloc([2, 272], F32)
    SA, SB = [Buf("sA0"), Buf("sA1")], [Buf("sB0"), Buf("sB1")]
    pooledT = [A.alloc([2, 256], BF16) for _ in range(4)]
    POOLED = [Buf("pooled%d" % s) for s in range(4)]
    ag = A.alloc([8, 256], F32)
    AG = [Buf("ag%d" % c) for c in range(8)]
    sq = junk[:, 0:1152]
    SQ = JUNK
    st18 = A.alloc([3, 18], F32)
    ST18 = Buf("st18")
    qn = A.alloc([18, 64], F32)
    QN = Buf("qn")
    rtmp = A.alloc([4, 18, 8], F32)
    RT = [Buf("rtmp%d" % i) for i in range(4)]
    qkb = [A.alloc([18, 64], BF16) for _ in range(2)]
    QKB = [Buf("qkb%d" % i) for i in range(2)]
    kz = [A.alloc([4, 128], BF16) for _ in range(3)]
    KZ = [Buf("kz%d" % i) for i in range(3)]
    vz = [A.alloc([4, 128], BF16) for _ in range(3)]
    VZ = [Buf("vz%d" % i) for i in range(3)]
    qT = [A.alloc([8, 128], BF16) for _ in range(2)]
    QT = [Buf("qT%d" % i) for i in range(2)]
    kT = [A.alloc([4, 128], BF16) for _ in range(3)]
    KT = [Buf("kT%d" % i) for i in range(3)]
    scr_base = A.off
    PT = [[A.alloc([512], BF16) for _ in range(8)] for _ in range(1)]
    PTB = [[Buf("PT%d_%d" % (s, i)) for i in range(8)] for s in range(1)]
    rec = [A.alloc([4, 128], F32) for _ in range(2)]
    REC = [Buf("rec%d" % i) for i in range(2)]
    A2 = Arena(arena_t, A.off)
    A2.off = scr_base
    cstf = A2.alloc([NCS], F32)
    ang = A2.alloc([NBLK, 8], F32)
    angs = A2.alloc([NBLK, 8], F32)
    angk = A2.alloc([NBLK, 8], F32)
    angi = A2.alloc([NBLK, 8], I32)
    cT = [A.alloc([8, 256], BF16) for _ in range(2)]
    CT = [[Buf("cT%d_%d" % (s, j)) for j in range(2)] for s in range(2)]
    OUTB = [Buf("out%d" % i) for i in range(NBLK)]
    p1_end = A.off
    print('[arena] phase1 end', p1_end, 'of', ARENA_BYTES)

    dma("sp", pp, pp_d[:, :], [], [PPb], PPb)
    dma("sp", cstf, cst_d[:, :], [], [CST], CST)
    dma("sp", rb1, rb_d[:, 0:2176], [], [RB1], RB1)
    dma("sp", posi, pos_d[:, :], [], [POSI], POSI)
    WB = {kd: {"act": Buf("w%s_a" % kd), "dve": Buf("w%s_d" % kd)} for kd in "QUGPO"}
    WIN_Q_L = list(WB["Q"].values())
    WIN_U_L = list(WB["U"].values())
    WIN_G_L = list(WB["G"].values())
    WPOOL_L = list(WB["P"].values())
    WOUT_L = list(WB["O"].values())
    stage_jobs = []
    for k in range(8):
        for h in range(2):
            c0 = 1024 + h * 640
            stage_jobs.append((win[:, k, c0:c0 + 640], win_v[:, k, c0:c0 + 640], "Q", 640))
    for k in range(8):
        stage_jobs.append((win[:, k, 0:1024], win_v[:, k, 0:1024], "U", None))
    for k in range(8):
        for h in range(2):
            c0 = 2304 + h * 1024
            stage_jobs.append((win[:, k, c0:c0 + 1024], win_v[:, k, c0:c0 + 1024], "G", None))
    for h in range(2):
        stage_jobs.append((wpool[:, 2 * h:2 * h + 2], wpool_v[:, 2 * h:2 * h + 2], "P", "p (g c) d -> p g c d"))
    for k in range(8):
        stage_jobs.append((wout[:, k, :], wout_v[:, k, :], "O", None))

    stage_ctr = [0]

    def emit_stage_jobs(kinds):
        todo = [j_ for j_ in stage_jobs if j_[2] in kinds]
        for (dst, src, kd, rr) in todo:
            i = stage_ctr[0]
            stage_ctr[0] += 1
            sl_ = i % 4
            sv = gT[:, 4 * sl_:4 * sl_ + 4, :]
            if isinstance(rr, int):
                sv = sv.rearrange("p a b -> p (a b)")[:, 0:rr]
            elif rr:
                sv = sv.rearrange(rr, g=2)
            else:
                sv = sv.rearrange("p a b -> p (a b)")
            GS = GT[4 * sl_:4 * sl_ + 4]
            dma("sp", sv, src, [], GS, GT[4 * sl_])
            eng = "act" if i % 2 == 0 else "dve"
            if eng == "act":
                R.add("act", (lambda d_, s_: (lambda e: e.copy(d_, s_)))(dst, sv), r=GS, w=[WB[kd][eng]], nodep=True)
            else:
                R.add("dve", (lambda d_, s_: (lambda e: e.tensor_copy(d_, s_)))(dst, sv), r=GS, w=[WB[kd][eng]], nodep=True)

    cp("dve", corrp, cstf[:, CS_CORR:CS_CORR + 64], [CST], [BGH])
    cp("dve", ident, cstf[:, CS_ID:CS_ID + 128], [CST], [IDENT])
    cp("dve", maskC, cstf[:, CS_MC:CS_MC + 512], [CST], [MASK])
    cp("dve", maskP, cstf[:, CS_MP:CS_MP + 512], [CST], [MASK])
    corr = corrp.rearrange("p (g t) -> p g t", g=4)
    ts("dve", bgh, pp[:, PP_BG:PP_BG + 16], 0.5, None, ALU.mult, None, [PPb], [BGH])
    ts("dve", gqk[:, 0:1024], gqk[:, 0:1024], 0.125, None, ALU.mult, None, [RB1], [RB1])
    act(exps, pp[:, PP_SK:PP_SK + 8], AF.Exp, [PPb], [EXPS])
    mset("dve", neghalf, -0.5, [NEGH])
    mset("dve", negpi, -math.pi, [NEGPI])
    mset("pool", halo, 0.0, HALO)
    for i in range(3):
        mset("pool", kz[i], 0.0, [KZ[i]])
        mset("pool", vz[i], 0.0, [VZ[i]])
    mset("pool", onesz, 0.0, [ONESZ])
    mset("pool", onesz[:, 0, 0:64], 1.0, [ONESZ])
    mset("pool", onesz[:, 1, 64:128], 1.0, [ONESZ])
    inv_freq = [float(ROPE_THETA ** (-(2.0 * i) / 16.0)) for i in range(8)]
    TWO_PI = 2.0 * math.pi

    def range_reduce_sin(dst, shift):
        ts("dve", angs, ang, shift, None, ALU.add, None, [ANG], [ANG])
        ts("dve", angi, angs, 1.0 / TWO_PI, None, ALU.mult, None, [ANG], [ANG])
        cp("dve", angk, angi, [ANG], [ANG])
        stt(angs, angk, -TWO_PI, angs, ALU.mult, ALU.add, [ANG], [ANG])
        ts("dve", angk, angs, math.pi, -TWO_PI, ALU.is_gt, ALU.mult, [ANG], [ANG])
        tt("dve", angs, angs, angk, ALU.add, [ANG], [ANG])
        ts("dve", angk, angs, -math.pi, TWO_PI, ALU.is_lt, ALU.mult, [ANG], [ANG])
        tt("dve", angs, angs, angk, ALU.add, [ANG], [ANG])
        ts("dve", angs, angs, math.pi, -math.pi, ALU.min, ALU.max, [ANG], [ANG])
        act(dst, angs, AF.Sin, [ANG], [ROPE])

    def rope_setup():
        cp("dve", posf, posi, [POSI], [ANG])
        for i in range(8):
            ts("dve", ang[:, :, i], posf, inv_freq[i], None, ALU.mult, None, [ANG], [ANG])
        range_reduce_sin(sin_t, 0.0)
        range_reduce_sin(cos_t, math.pi / 2)

    FM = (1, 2)
    SC = (6, 7)

    def norm_pre(gb, src_ap, src_deps, gain, GAINB, xnring, XNring, skip_load=False):
        s3 = gb % len(xnring)
        s2 = gb % 2
        s1 = gb % len(hn)
        if not skip_load:
            dma("sp", xnring[s3], src_ap, src_deps, [XNring[s3]], XNring[s3])
        act(hn[s1], xnring[s3], AF.Square, [XNring[s3]], [HN[s1], SS1[s2]], accum=ss1[s2][:, 0:1])
        ts("dve", ss1[s2][:, 1:2], ss1[s2][:, 0:1], 1.0 / D, EPS, ALU.mult, ALU.add, [SS1[s2]], [SS1[s2]])
        tt("pool", ss1[s2][:, 2:3], ss1[s2][:, 1:2], neghalf[:, 0:1], ALU.pow, [SS1[s2], NEGH], [SS1[s2]])
        stt(hn[s1], xnring[s3], ss1[s2][:, 2:3], gain, ALU.mult, ALU.mult, [XNring[s3], SS1[s2], GAINB], [HN[s1]])

    def norm_tr(gb, hT_dst, HT_dstB):
        s1 = gb % len(hn)
        for k in range(8):
            tr(trp_bf[:, k * 128:(k + 1) * 128], hn[s1][:, k * 128:(k + 1) * 128], ident, [HN[s1], IDENT], [PB[0]])
        cp("act", hT_dst, trp_bf.rearrange("p (k n) -> p k n", k=8), [PB[0]], [HT_dstB])

    def norm_block(gb, src_ap, src_deps, gain, GAINB, hT_dst, HT_dstB, xnring, XNring):
        norm_pre(gb, src_ap, src_deps, gain, GAINB, xnring, XNring)
        norm_tr(gb, hT_dst, HT_dstB)

    NT1 = S // 256
    hs = 0
    HTr = [HT[hs][0], HT[hs][1]]
    ps4 = lambda bk: ps[:, bk, :].rearrange("p (a b) -> p a b", a=4)

    def p1_norm_pre(ti, j):
        gb = 2 * ti + j
        norm_pre(gb, x_d[gb * 128:(gb + 1) * 128, :], [], g1, RB1, xn, XN)

    def p1_norm_tr(ti, j):
        norm_tr(2 * ti + j, hT[hs][:, :, j * 128:(j + 1) * 128], HT[hs][j])

    def P_mm(ti, j):
        gb = 2 * ti + j
        tok = slice(j * 128, (j + 1) * 128)
        for k in range(8):
            lh = hT[hs][:, k, tok]
            mm(ps[:, 3, :], lh, win[:, k, 1024:1536], k == 0, k == 7, [HT[hs][j]] + WIN_Q_L, [PB[3]])
            mm(ps[:, 4, :], lh, win[:, k, 1536:2048], k == 0, k == 7, [HT[hs][j]] + WIN_Q_L, [PB[4]])
            mm(ps[:, 5, 0:256], lh, win[:, k, 2048:2304], k == 0, k == 7, [HT[hs][j]] + WIN_Q_L, [PB[5]])

    def P_chain(ti, j):
        gb = 2 * ti + j
        sl = gb % 3
        qb = gb % 2
        qn2 = qn.rearrange("p h d -> p (h d)")
        tt("dve", qn2[:, 0:1024].rearrange("p (a b) -> p a b", a=2), ps[:, 3:5, :],
           gqk[:, 0:1024].rearrange("p (a b) -> p a b", a=2), ALU.mult, [PB[3], PB[4], RB1], [QN])
        tt("dve", qn2[:, 1024:1152], ps[:, 5, 0:128], gqk[:, 1024:1152], ALU.mult, [PB[5], RB1], [QN])
        act(sq[:, 0:1024].rearrange("p (a b) -> p a b", a=2), ps[:, 3:5, :], AF.Square, [PB[3], PB[4]], [SQ])
        act(sq[:, 1024:1152], ps[:, 5, 0:128], AF.Square, [PB[5]], [SQ])
        vz4 = vz[sl].rearrange("p (kv par) d -> p kv par d", kv=2)
        vsrc = ps[:, 5, 128:256].rearrange("p (kv d) -> p kv d", kv=2)
        cp("act", vz4[:, :, 0, 0:64], vsrc, [PB[5]], [VZ[sl]])
        cp("act", vz4[:, :, 1, 64:128], vsrc, [PB[5]], [VZ[sl]])
        R.add("dve", lambda e: e.tensor_reduce(st18[:, 0, :], sq.rearrange("p (h d) -> p h d", d=64), AX.X, ALU.add),
              r=[SQ], w=[ST18])
        ts("dve", st18[:, 1, :], st18[:, 0, :], 1.0 / 64, EPS, ALU.mult, ALU.add, [ST18], [ST18])
        tt("pool", st18[:, 2, :], st18[:, 1, :], neghalf[:, 0:18], ALU.pow, [ST18, NEGH], [ST18])
        cosb = cos_t[:, gb, :].unsqueeze(1).to_broadcast([128, 18, 8])
        sinb = sin_t[:, gb, :].unsqueeze(1).to_broadcast([128, 18, 8])
        x1 = qn[:, :, 0:8]
        x2 = qn[:, :, 8:16]
        tt("dve", rtmp[:, 0], x1, cosb, ALU.mult, [QN, ROPE], [RT[0]])
        tt("dve", rtmp[:, 1], x2, sinb, ALU.mult, [QN, ROPE], [RT[1]])
        tt("dve", rtmp[:, 2], x2, cosb, ALU.mult, [QN, ROPE], [RT[2]])
        tt("dve", rtmp[:, 3], x1, sinb, ALU.mult, [QN, ROPE], [RT[3]])
        tt("dve", qn[:, :, 0:8], rtmp[:, 0], rtmp[:, 1], ALU.subtract, [RT[0], RT[1]], [QN])
        tt("dve", qn[:, :, 8:16], rtmp[:, 2], rtmp[:, 3], ALU.add, [RT[2], RT[3]], [QN])
        tt("dve", qkb[qb], qn, st18[:, 2, :].unsqueeze(2).to_broadcast([128, 18, 64]), ALU.mult,
           [QN, ST18], [QKB[qb]])
        kz4 = kz[sl].rearrange("p (kv par) d -> p kv par d", kv=2)
        cp("pool", kz4[:, :, 0, 0:64], qkb[qb][:, 16:18, :], [QKB[qb]], [KZ[sl]])
        cp("pool", kz4[:, :, 1, 64:128], qkb[qb][:, 16:18, :], [QKB[qb]], [KZ[sl]])

    def P_tr(ti, j):
        gb = 2 * ti + j
        sl = gb % 3
        qb = gb % 2
        qkb2 = qkb[qb].rearrange("p h d -> p (h d)")
        for c in range(8):
            tr(trp_bf[:, c * 128:(c + 1) * 128], qkb2[:, c * 128:(c + 1) * 128], ident, [QKB[qb], IDENT], [PB[0]])
        cp("act", qT[qb], trp_bf.rearrange("p (k n) -> p k n", k=8), [PB[0]], [QT[qb]])
        for i in range(4):
            tr(trp_bf[:, i * 128:(i + 1) * 128], kz[sl][:, i, :], ident, [KZ[sl], IDENT], [PB[0]])
        cp("act", kT[sl], trp_bf[:, 0:512].rearrange("p (k n) -> p k n", k=4), [PB[0]], [KT[sl]])

    def gates(ti, c, banks=None):
        bank = (banks or SC)[c % 2]
        for k in range(8):
            mm(ps[:, bank, 0:256], win[:, k, 2304 + c * 128:2304 + (c + 1) * 128], hT[hs][:, k, :],
               k == 0, k == 7, WIN_G_L + HTr, [PB[bank]])
        act(gT[:, c, :], ps[:, bank, 0:256], AF.Tanh, [PB[bank], BGH], [GT[c]],
            bias=bgh[:, c:c + 1], scale=0.5)

    UBANK = ((1, 2), (3, 4))

    def C_u(ti, g):
        us = g % 2
        U = u2[us]
        for cc in range(2):
            c = 2 * g + cc
            bank = UBANK[us][cc]
            for k in range(8):
                mm(ps[:, bank, 0:256], win[:, k, c * 128:(c + 1) * 128], hT[hs][:, k, :],
                   k == 0, k == 7, WIN_U_L + HTr, [PB[bank]])
            cp("pool", U[:, cc, 0:16], halo[:, c, :], [HALO[c]], [U2H[us][cc]])
            cp("act", U[:, cc, 16:272], ps[:, bank, 0:256], [PB[bank]], [U2[us][cc]])
            cp("pool", halo[:, c, :], U[:, cc, 256:272], [U2[us][cc]], [HALO[c]])

    def C_pool(ti, g):
        us = g % 2
        U = u2[us]
        Ur = [[U2[us][cc], U2H[us][cc]] for cc in range(2)]
        for cc in range(2):
            tt("dve", sA[:, cc, 1:272], U[:, cc, 1:272], U[:, cc, 0:271], ALU.add, Ur[cc], [SA[cc]])
        cur, CUR = sA, SA
        if g >= 1:
            for cc in range(2):
                tt("dve", sB[:, cc, 3:272], sA[:, cc, 3:272], sA[:, cc, 1:270], ALU.add, [SA[cc]], [SB[cc]])
            cur, CUR = sB, SB
        if g >= 2:
            for cc in range(2):
                tt("dve", sA[:, cc, 7:272], sB[:, cc, 7:272], sB[:, cc, 3:268], ALU.add, [SB[cc]], [SA[cc]])
            cur, CUR = sA, SA
        if g >= 3:
            for cc in range(2):
                tt("dve", sB[:, cc, 15:272], sA[:, cc, 15:272], sA[:, cc, 7:264], ALU.add, [SA[cc]], [SB[cc]])
            cur, CUR = sB, SB
        if ti == 0:
            for cc in range(2):
                tt("dve", cur[:, cc, 16:32], cur[:, cc, 16:32], corr[:, g, :], ALU.mult, [CUR[cc], BGH], [CUR[cc]])
        pl = pooledT[g]
        for cc in range(2):
            stt(pl[:, cc, :], cur[:, cc, 16:272], 1.0 / (2 ** (g + 1)), U[:, cc, 16:272],
                ALU.mult, ALU.subtract, [CUR[cc]] + Ur[cc], [POOLED[g]])

    def C_map(ti, g):
        pl = pooledT[g]
        for dc in range(2):
            ch = 2 * g + dc
            bk = (5, 1, 2, 3, 4, 6, 7, 5)[ch]
            hf = 1 if ch == 7 else 0
            o = ps[:, bk, hf * 256:(hf + 1) * 256]
            for cc in range(2):
                mm(o, wpool[:, g, cc, dc * 128:(dc + 1) * 128], pl[:, cc, :],
                   cc == 0, cc == 1, WPOOL_L + [POOLED[g]], [PB[bk]])
        for dc in range(2):
            ch = 2 * g + dc
            bk = (5, 1, 2, 3, 4, 6, 7, 5)[ch]
            hf = 1 if ch == 7 else 0
            o = ps[:, bk, hf * 256:(hf + 1) * 256]
            stt(ag[:, ch, :], o, pp[:, PP_PS + ch:PP_PS + ch + 1], gT[:, ch, :],
                ALU.mult, ALU.mult, [PB[bk], PPb, GT[ch]], [AG[ch]])

    def S_core(ti, j):
        gb = 2 * ti + j
        sl = gb % 3
        psl = (gb - 1) % 3
        qb = gb % 2
        kbs = [(psl, maskP), (sl, maskC)] if gb > 0 else [(sl, maskC)]
        cnt = 0
        for kv in range(2):
            for par in range(2):
                for kbi, (ksl, mk) in enumerate(kbs):
                    bank = SC[cnt % 2]
                    cnt += 1
                    u = (kv * 2 + par) * 2 + kbi
                    mm(ps4(bank), kT[ksl][:, kv * 2 + par, :],
                       qT[qb][:, kv * 4:(kv + 1) * 4, :], True, False, [KT[ksl], QT[qb]], [PB[bank]])
                    mm(ps[:, bank, :], ident, mk, False, True, [IDENT, MASK], [PB[bank]])
                    act(PT[0][u], ps[:, bank, :], AF.Exp, [PB[bank]], [PTB[0][u]])

    PVB = (((1, 2), (3, 4)), ((1, 2), (6, 7)))

    def S_pv(ti, j):
        gb = 2 * ti + j
        sl = gb % 3
        psl = (gb - 1) % 3
        kbs = [(psl, maskP), (sl, maskC)] if gb > 0 else [(sl, maskC)]
        for kv in range(2):
            pvb, denb = PVB[j][kv]
            n = 2 * len(kbs)
            i = 0
            for par in range(2):
                for kbi, (ksl, mk) in enumerate(kbs):
                    u = (kv * 2 + par) * 2 + kbi
                    mm(ps[:, pvb, :], vz[ksl][:, kv * 2 + par, :], PT[0][u], i == 0, i == n - 1,
                       [VZ[ksl], PTB[0][u]], [PB[pvb]])
                    i += 1
            i = 0
            for par in range(2):
                for kbi, (ksl, mk) in enumerate(kbs):
                    u = (kv * 2 + par) * 2 + kbi
                    mm(ps[:, denb, :], onesz[:, par, :], PT[0][u], i == 0, i == n - 1,
                       [ONESZ, PTB[0][u]], [PB[denb]])
                    i += 1

    def N_chain(ti, j):
        N_a(ti, j)
        N_b(ti, j)

    def N_a(ti, j):
        banks = PVB[j]
        for kv in range(2):
            ch = slice(kv * 4, (kv + 1) * 4)
            tt("dve", rec[kv], ps4(banks[kv][1]), exps[:, ch].unsqueeze(2).to_broadcast([128, 4, 128]), ALU.add,
               [PB[banks[kv][1]], EXPS], [REC[kv]])
        for kv in range(2):
            rk = rec[kv]
            R.add("dve", (lambda rk: (lambda e: e.reciprocal(rk, rk)))(rk), r=[REC[kv]], w=[REC[kv]])

    def N_b(ti, j):
        cs = ti % 2
        tok = slice(j * 128, (j + 1) * 128)
        banks = PVB[j]
        for kv in range(2):
            tt("dve", rec[kv], ps4(banks[kv][0]), rec[kv], ALU.mult, [PB[banks[kv][0]], REC[kv]], [REC[kv]])
        for kv in range(2):
            gch = slice(8 + kv * 4, 8 + (kv + 1) * 4)
            tt("pool", rec[kv], rec[kv], gT[:, gch, tok], ALU.mult, [REC[kv]] + GT[gch], [REC[kv]])
        for kv in range(2):
            ch = slice(kv * 4, (kv + 1) * 4)
            tt("pool", cT[cs][:, ch, tok], rec[kv], ag[:, ch, tok], ALU.add, [REC[kv]] + AG[ch], [CT[cs][j]])

    def E_mm(ti, j):
        gb = 2 * ti + j
        cs = ti % 2
        rs = 0
        tok = slice(j * 128, (j + 1) * 128)
        b0, b1 = (3, 4) if j == 0 else (1, 2)
        dma("sp", xr[rs], x_d[gb * 128:(gb + 1) * 128, :], [], [XR[rs]], XR[rs])
        for k in range(8):
            lh = cT[cs][:, k, tok]
            mm(ps[:, b0, :], lh, wout[:, k, 0:512], k == 0, k == 7, [CT[cs][j]] + WOUT_L, [PB[b0]])
            mm(ps[:, b1, :], lh, wout[:, k, 512:1024], k == 0, k == 7, [CT[cs][j]] + WOUT_L, [PB[b1]])

    def E_add(ti, j):
        gb = 2 * ti + j
        rs = 0
        b0, b1 = (3, 4) if j == 0 else (1, 2)
        xr3 = xr[rs].rearrange("p (a b) -> p a b", a=2)
        tt("dve", xr3, ps[:, b0:b1 + 1, :], xr3, ALU.add, [PB[b0], PB[b1], XR[rs]], [XR[rs]])
        dma("sp", out_d[gb * 128:(gb + 1) * 128, :], xr[rs], [XR[rs]], [OUTB[gb]], XR[rs])

    WUPS, WDNS = Buf("wups"), Buf("wdns")
    conv_jobs = []
    for gi_, (a_, b_) in enumerate(FC_GROUPS):
        for h_ in range(2):
            for k in range(8):
                conv_jobs.append((wups_g[gi_][:, h_, k, :],
                                  wup_v[:, k, h_ * DFF + a_ * 128:h_ * DFF + b_ * 128], WUPS))
    for kk in range(22):
        conv_jobs.append((wdns_v[:, kk, :], wdown_v[:, kk, :], WDNS))

    def emit_conv(n):
        for _ in range(n):
            if conv_jobs:
                o, i, B_ = conv_jobs.pop(0)
                dma("pool", o, i, [], [B_], B_, nodep=True)

    for j in range(2):
        p1_norm_pre(0, j)
        p1_norm_tr(0, j)
    emit_stage_jobs("Q")
    rope_setup()
    P_mm(0, 0)
    P_chain(0, 0)
    emit_stage_jobs("U")
    for ti in range(NT1):
        P_mm(ti, 1)
        P_chain(ti, 1)
        C_u(ti, 0)
        C_u(ti, 1)
        if ti == 0:
            emit_stage_jobs("GPO")
        if ti == 0:
            for c in range(0, 4):
                gates(ti, c)
            C_pool(ti, 0)
            C_u(ti, 2)
            for c in range(4, 8):
                gates(ti, c)
            P_tr(ti, 0)
            ts("pool", gT[:, 0:8, :], gT[:, 0:8, :], 0.5, 0.5, ALU.mult, ALU.add, GT[0:8], GT[0:8])
            C_pool(ti, 1)
            C_u(ti, 3)
            for c in range(8, 16):
                gates(ti, c)
        else:
            for c in range(8, 12):
                gates(ti, c)
            C_pool(ti, 0)
            C_u(ti, 2)
            for c in range(12, 16):
                gates(ti, c)
            P_tr(ti, 0)
            C_pool(ti, 1)
            C_u(ti, 3)
        ts("pool", gT[:, 8:16, :], gT[:, 8:16, :], 0.5, 0.5, ALU.mult, ALU.add, GT[8:16], GT[8:16])
        P_tr(ti, 1)
        C_pool(ti, 2)
        C_pool(ti, 3)
        for g in range(4):
            C_map(ti, g)
        nxt = ti + 1 < NT1
        if nxt:
            p1_norm_pre(ti + 1, 0)
            p1_norm_pre(ti + 1, 1)
        S_core(ti, 0)
        S_pv(ti, 0)
        if nxt:
            p1_norm_tr(ti + 1, 0)
        S_core(ti, 1)
        N_chain(ti, 0)
        S_pv(ti, 1)
        if nxt:
            p1_norm_tr(ti + 1, 1)
        E_mm(ti, 0)
        if nxt:
            for c in range(0, 8):
                gates(ti + 1, c, banks=(5, 0))
        N_a(ti, 1)
        E_add(ti, 0)
        if nxt:
            P_mm(ti + 1, 0)
        N_b(ti, 1)
        if nxt:
            ts("pool", gT[:, 0:8, :], gT[:, 0:8, :], 0.5, 0.5, ALU.mult, ALU.add, GT[0:8], GT[0:8])
            P_chain(ti + 1, 0)
        E_mm(ti, 1)
        E_add(ti, 1)
        emit_conv(6)
    emit_conv(100)

    R.barrier()
    A.off = persist_mark
    fc_groups = FC_GROUPS
    wupg = [A.alloc([2, 8, (b_ - a_) * 128], BF16) for (a_, b_) in fc_groups]
    WUPG = [Buf("wup%d" % i) for i in range(4)]
    fc2grp = {}
    for gi, (a, b) in enumerate(fc_groups):
        for fc in range(a, b):
            fc2grp[fc] = gi
    wdown = A.alloc([22, 1024], BF16)
    WDN = [Buf("wdn0"), Buf("wdn1")]
    g2 = A.alloc([1024], F32)
    G2 = Buf("g2")
    chalo = A.alloc([44, 2], F32)
    CH = [Buf("chalo%d" % i) for i in range(44)]
    xn2 = [A.alloc([1024], F32) for _ in range(2)]
    XN2 = [Buf("xn2_%d" % i) for i in range(2)]
    xr2 = [A.alloc([1024], F32) for _ in range(2)]
    XR2 = [Buf("xr2_%d" % i) for i in range(2)]
    ss1 = [A.alloc([8], F32) for _ in range(2)]
    hn = [A.alloc([1024], BF16) for _ in range(1)]
    hT2 = A.alloc([8, 512], BF16)
    HT2 = [Buf("hT2_%d" % j) for j in range(4)]
    yb = [[A.alloc([512], F32) for _ in range(2)] for _ in range(2)]
    YB = [[Buf("y%d_%d" % (h, s)) for s in range(2)] for h in range(2)]
    ub = [[A.alloc([514], F32) for _ in range(2)] for _ in range(2)]
    UH = [[Buf("uh%d_%d" % (h, s)) for s in range(2)] for h in range(2)]
    UM = [[Buf("um%d_%d" % (h, s)) for s in range(2)] for h in range(2)]
    sg = [A.alloc([512], F32) for _ in range(2)]
    SG = [Buf("sg%d" % s) for s in range(2)]
    hid = A.alloc([22, 512], BF16)
    HID = [Buf("hid%d" % fc) for fc in range(22)]
    print('[arena] phase2 end', A.off)

    dma("sp", g2, rb_d[:, RB_G2:RB_G2 + 1024], [], [G2], G2)
    mset("pool", chalo, 0.0, CH)

    def p2_wload(gi):
        dma("sp", wupg[gi].rearrange("p h k c -> p (h k c)"), wups_g[gi].rearrange("p h k c -> p (h k c)"),
            [WUPS], [WUPG[gi]], WUPG[gi])

    def p2_xload(j):
        dma("sp", xn2[j % 2], out_d[j * 128:(j + 1) * 128, :], [OUTB[j]], [XN2[j % 2]], XN2[j % 2])

    GB = ((1, 2, 3), (4, 5, 6))
    NT2 = S // 512

    def p2_norm_pre(ti, j, skip_load=False):
        gb = 4 * ti + j
        norm_pre(gb, out_d[gb * 128:(gb + 1) * 128, :], [OUTB[gb]], g2, G2, xn2, XN2, skip_load=skip_load)

    def p2_loads(ti, j):
        gb = 4 * ti + j
        rs = gb % 2
        dma("sp", xr2[rs], out_d[gb * 128:(gb + 1) * 128, :], [OUTB[gb]], [XR2[rs]], XR2[rs])
        if ti + 1 < NT2:
            gn = 4 * (ti + 1) + j
            dma("sp", xn2[gn % 2], out_d[gn * 128:(gn + 1) * 128, :], [OUTB[gn]], [XN2[gn % 2]], XN2[gn % 2])

    def p2_norm_tr(ti, j):
        norm_tr(4 * ti + j, hT2[:, :, j * 128:(j + 1) * 128], HT2[j])

    def ffn_front(fc):
        rg = fc % 2
        for half in range(2):
            chn = half * 22 + fc
            cp("pool", ub[half][rg][:, 0:2], chalo[:, chn, :], [CH[chn]], [UH[half][rg]])
        for half in range(2):
            chn = half * 22 + fc
            col0 = half * DFF + fc * 128
            bank = GB[half][fc % 3]
            for k in range(8):
                gi_ = fc2grp[fc]
                lc = (fc - fc_groups[gi_][0]) * 128
                mm(ps[:, bank, :], wupg[gi_][:, half, k, lc:lc + 128], hT2[:, k, :], k == 0, k == 7,
                   [WUPG[gi_]] + HT2, [PB[bank]])
        for half in range(2):
            chn = half * 22 + fc
            bank = GB[half][fc % 3]
            act(yb[half][rg], ps[:, bank, :], AF.Identity, [PB[bank], PPb], [YB[half][rg]],
                bias=pp[:, PP_CB + chn:PP_CB + chn + 1], scale=cwv[:, 2, chn:chn + 1])
            cp("act", ub[half][rg][:, 2:514], ps[:, bank, :], [PB[bank]], [UM[half][rg]])
        for half in range(2):
            chn = half * 22 + fc
            cp("pool", chalo[:, chn, :], ub[half][rg][:, 512:514], [UM[half][rg]], [CH[chn]])
        for tap, off in ((1, 1), (0, 0)):
            for half in range(2):
                chn = half * 22 + fc
                y = yb[half][rg]
                stt(y, ub[half][rg][:, off:off + 512], cwv[:, tap, chn:chn + 1], y, ALU.mult, ALU.add,
                    [UH[half][rg], UM[half][rg], PPb, YB[half][rg]], [YB[half][rg]])

    def ffn_silu(fc):
        rg = fc % 2
        act(sg[rg], yb[0][rg], AF.Silu, [YB[0][rg]], [SG[rg]])

    def ffn_mult(fc):
        rg = fc % 2
        tt("pool", hid[:, fc, :], sg[rg], yb[1][rg], ALU.mult, [SG[rg], YB[1][rg]], [HID[fc]])

    p2_xload(0)
    p2_xload(1)
    p2_wload(0)
    for j in range(4):
        p2_norm_pre(0, j, skip_load=True)
        p2_norm_tr(0, j)
        if j + 2 < 4:
            p2_xload(j + 2)
        if j == 1:
            p2_wload(1)
            dma("sp", wdown[:, 0:11, :], wdns_v[:, 0:11, :], [WDNS], [WDN[0]], WDN[0])
    p2_wload(2)
    p2_wload(3)
    dma("sp", wdown[:, 11:22, :], wdns_v[:, 11:22, :], [WDNS], [WDN[1]], WDN[1])
    for ti in range(NT2):
        for fc in range(22):
            ffn_front(fc)
            if fc > 0:
                ffn_silu(fc - 1)
                ffn_mult(fc - 1)
        ffn_silu(21)
        ffn_mult(21)
        p2_loads(ti, 0)
        for j in range(4):
            gb = 4 * ti + j
            rs = gb % 2
            tok = slice(j * 128, (j + 1) * 128)
            if ti + 1 < NT2:
                p2_norm_pre(ti + 1, j, skip_load=True)
            g0 = 1 if j % 2 == 0 else 4
            for fc in range(22):
                lh = hid[:, fc, tok]
                mm(ps[:, g0, :], lh, wdown[:, fc, 0:512], fc == 0, fc == 21, [HID[fc], WDN[fc // 11]], [PB[g0]])
                mm(ps[:, g0 + 1, :], lh, wdown[:, fc, 512:1024], fc == 0, fc == 21, [HID[fc], WDN[fc // 11]], [PB[g0 + 1]])
            if ti + 1 < NT2:
                p2_norm_tr(ti + 1, j)
            if j + 1 < 4:
                p2_loads(ti, j + 1)
            xr3 = xr2[rs].rearrange("p (a b) -> p a b", a=2)
            tt("dve", xr3, ps[:, g0:g0 + 2, :], xr3, ALU.add, [PB[g0], PB[g0 + 1], XR2[rs]], [XR2[rs]])
            dma("sp", out_d[gb * 128:(gb + 1) * 128, :], xr2[rs], [XR2[rs]], [OUTB[gb]], XR2[rs])

    R.finish()
    R.emit(nc, stack)
    stack.close()
    return nc


def _host_layout(inputs):
    f = lambda a: np.ascontiguousarray(np.asarray(a), dtype=np.float32)
    b_gate = f(inputs["b_gate"])[0]
    pool_scale = f(inputs["pool_scale"])[0]
    sinks = f(inputs["sinks"])[0]
    conv_w = f(inputs["conv_w"])[0]
    conv_b = f(inputs["conv_b"])[0]
    pp = np.zeros((128, NPP), np.float32)
    pp[:, PP_BG:PP_BG + 16] = b_gate.reshape(16, 128).T
    pp[:, PP_PS:PP_PS + 8] = pool_scale.reshape(8, 128).T
    pp[:, PP_SK:PP_SK + 8] = np.repeat(sinks.reshape(8, 2).T, 64, axis=0)
    pp[:, PP_CW:PP_CW + 132] = conv_w.reshape(3, 44, 128).transpose(2, 0, 1).reshape(128, 132)
    pp[:, PP_CB:PP_CB + 44] = conv_b.reshape(44, 128).T
    rb = np.zeros((128, NRB), np.float32)
    rb[:, RB_G1:RB_G1 + 1024] = f(inputs["attn_norm"])[0][None, :]
    rb[:, RB_GQ:RB_GQ + 1024] = np.tile(f(inputs["q_norm"])[0], 16)[None, :]
    rb[:, RB_GQ + 1024:RB_GQ + 1152] = np.tile(f(inputs["k_norm"])[0], 2)[None, :]
    rb[:, RB_G2:RB_G2 + 1024] = f(inputs["ffn_norm"])[0][None, :]
    cst = np.zeros((128, NCS), np.float32)
    cst[:, CS_ID:CS_ID + 128] = np.eye(128, dtype=np.float32)
    jj = np.arange(128)[:, None]
    ii = np.arange(128)[None, :]
    mc = np.where(jj <= ii, 0.0, NEG).astype(np.float32)
    mp = np.where(jj > ii, 0.0, NEG).astype(np.float32)
    cst[:, CS_MC:CS_MC + 512] = np.tile(mc, (1, 4))
    cst[:, CS_MP:CS_MP + 512] = np.tile(mp, (1, 4))
    corr = np.ones((4, 16), np.float32)
    for g in range(4):
        w = 2 ** (g + 1)
        for t in range(16):
            corr[g, t] = w / min(t + 1, w)
    cst[:, CS_CORR:CS_CORR + 64] = corr.reshape(1, 64)
    return pp, rb, cst


_NC_CACHE = {}


def kernel(**inputs):
    x = np.ascontiguousarray(np.asarray(inputs["x"]), dtype=np.float32)
    positions = np.ascontiguousarray(np.asarray(inputs["positions"]), dtype=np.int32)
    pp, rb, cst = _host_layout(inputs)
    f = lambda a: np.ascontiguousarray(np.asarray(a), dtype=np.float32)
    w_in = f(inputs["w_in"])[0]
    w_pool = f(inputs["w_pool"])[0]
    w_out = f(inputs["w_out"])[0]
    w_up = f(inputs["w_up"])[0]
    w_down = f(inputs["w_down"])[0]
    if "nc" not in _NC_CACHE:
        _NC_CACHE["nc"] = build_program()
    nc = _NC_CACHE["nc"]
    n = x.shape[0]
    in_maps = []
    for i in range(n):
        in_maps.append({
            "x": x[i],
            "posT": np.ascontiguousarray(positions[i].reshape(NBLK, 128).T),
            "w_in": w_in, "w_pool": w_pool, "w_out": w_out, "w_up": w_up, "w_down": w_down,
            "pp": pp, "rb": rb, "cst": cst,
        })
    res = run_bass_kernel_spmd(nc, in_maps, core_ids=list(range(n)))
    out = np.stack([np.asarray(r["out"]) for r in res.results], axis=0)
    return out.astype(np.float32)
```

```python
import math
from contextlib import ExitStack

import numpy as np
import concourse.bass as bass
import concourse.mybir as mybir
from concourse.bass_utils import run_bass_kernel_spmd

F32 = mybir.dt.float32
BF16 = mybir.dt.bfloat16
I32 = mybir.dt.int32
ALU = mybir.AluOpType
AF = mybir.ActivationFunctionType
AX = mybir.AxisListType

S = 4096
D = 1024
NBLK = S // 128
INW = 4352
DFF = 2816
EPS = 1e-6
ROPE_THETA = 500000.0
NEG = -30000.0

PP_BG = 0
PP_PS = 16
PP_SK = 24
PP_CW = 32
PP_CB = 164
NPP = 208
RB_G1 = 0
RB_GQ = 1024
RB_G2 = 2176
NRB = 3200
CS_ID = 0
CS_MC = 128
CS_MP = 640
CS_CORR = 1152
NCS = 1216


import os
STRICT_WAR = False


class Buf:
    __slots__ = ("name", "w", "r", "sem", "cnt", "excl")

    def __init__(self, name, excl=False):
        self.name = name
        self.excl = excl
        self.w = None
        self.r = {}
        self.sem = None
        self.cnt = 0


class Ins:
    __slots__ = ("eng", "fn", "deps", "idx", "inc", "dma", "semv", "val")


class Rec:
    ENGS = ("pe", "act", "dve", "pool", "sp")
    INORDER_FREE = ("pe", "sp")

    def __init__(self):
        self.streams = {e: [] for e in self.ENGS}
        self.bar = {e: [] for e in self.ENGS}
        self.dmas = []
        self.dma_bufs = []

    def add(self, eng, fn, r=(), w=(), dma=None, nodep=False):
        I = Ins()
        I.eng = eng
        I.fn = fn
        I.dma = dma
        I.inc = False
        I.val = 0
        I.semv = None
        deps = {}
        for b in r:
            if b.w is not None:
                deps[id(b.w)] = b.w
            if b.excl:
                for q in b.r.values():
                    if q.eng != eng:
                        deps[id(q)] = q
        for b in w:
            if nodep:
                continue
            if b.w is not None:
                if not (b.w.dma is None and b.w.eng == eng and dma is None and not STRICT_WAR):
                    deps[id(b.w)] = b.w
            for q in b.r.values():
                if q.dma is None and q.eng == eng and dma is None and not STRICT_WAR:
                    continue
                deps[id(q)] = q
        for q in self.bar[eng]:
            deps[id(q)] = q
        self.bar[eng] = []
        if dma is not None:
            if dma.sem is None:
                self.dma_bufs.append(dma)
                dma.sem = True
            dma.cnt += 16
            I.semv = (dma, dma.cnt)
            self.dmas.append(I)
        I.deps = list(deps.values())
        for b in r:
            key = eng if dma is None else ("d", len(b.r))
            b.r[key] = I
        for b in w:
            b.w = I
            b.r = {}
        st = self.streams[eng]
        I.idx = len(st)
        st.append(I)
        return I

    def barrier(self):
        last = []
        for e in self.ENGS:
            if e == "sp":
                continue
            if self.streams[e]:
                last.append(self.streams[e][-1])
        lat = {}
        for I in self.dmas:
            lat[id(I.semv[0])] = I
        last.extend(lat.values())
        for e in self.ENGS:
            self.bar[e] = list(last)

    def finish(self):
        lat = {}
        for I in self.dmas:
            lat[id(I.semv[0])] = I
        self.bar["sp"] = list(lat.values())
        self.add("sp", None)

    def emit(self, nc, stack):
        engobj = {"pe": "tensor", "act": "scalar", "dve": "vector", "pool": "gpsimd", "sp": "sync"}
        for e in self.ENGS:
            for I in self.streams[e]:
                for Dp in I.deps:
                    if Dp.dma is None:
                        if Dp.eng == I.eng and I.eng in self.INORDER_FREE:
                            continue
                        Dp.inc = True
        for e in self.ENGS:
            c = 0
            for I in self.streams[e]:
                if I.inc:
                    c += 1
                I.val = c
        sems = {e: stack.enter_context(nc.semaphore("s_" + e)) for e in self.ENGS}
        for i, b in enumerate(self.dma_bufs):
            b.sem = stack.enter_context(nc.semaphore("d%d" % i))
        block = stack.enter_context(nc.Block())

        def body_for(e):
            def body(engine):
                waited = {}
                for I in self.streams[e]:
                    for Dp in I.deps:
                        if Dp.dma is not None:
                            key = id(Dp.semv[0])
                            v = Dp.semv[1]
                            sem = Dp.semv[0].sem
                        else:
                            if Dp.eng == e and e in self.INORDER_FREE:
                                continue
                            key = Dp.eng
                            v = Dp.val
                            sem = sems[Dp.eng]
                        if waited.get(key, 0) >= v:
                            continue
                        waited[key] = v
                        engine.wait_ge(sem, v)
                    if I.fn is not None:
                        ins = I.fn(engine)
                        if I.dma is not None:
                            ins.then_inc(I.dma.sem, 16)
                        elif I.inc:
                            ins.then_inc(sems[e], 1)
            return body

        for e in self.ENGS:
            getattr(block, engobj[e])(body_for(e))


class Arena:
    def __init__(self, ap, nbytes):
        self.ap = ap
        self.n = nbytes
        self.off = 0

    def alloc(self, shape, dt):
        esz = 2 if dt == BF16 else 4
        n = 1
        for s in shape:
            n *= s
        nb = (n * esz + 31) // 32 * 32
        assert self.off + nb <= self.n, "arena overflow %d + %d > %d" % (self.off, nb, self.n)
        v = self.ap[:, self.off // 4:(self.off + nb) // 4]
        if dt != F32:
            v = v.bitcast(dt)
        v = v[:, 0:n]
        self.off += nb
        if len(shape) == 2:
            v = v.rearrange("p (a b) -> p a b", a=shape[0])
        elif len(shape) == 3:
            v = v.rearrange("p (a b c) -> p a b c", a=shape[0], b=shape[1])
        return v


def build_program():
    nc = bass.Bass("TRN2", target_bir_lowering=False)
    dr = lambda name, shape, dt, kind="ExternalInput": nc.dram_tensor(name, shape, dt, kind=kind).ap()
    x_d = dr("x", [S, D], F32)
    pos_d = dr("posT", [128, NBLK], I32)
    win_d = dr("w_in", [D, INW], F32)
    wpool_d = dr("w_pool", [4, 256, 256], F32)
    wout_d = dr("w_out", [D, D], F32)
    wup_d = dr("w_up", [D, 2 * DFF], F32)
    wdown_d = dr("w_down", [DFF, D], F32)
    pp_d = dr("pp", [128, NPP], F32)
    rb_d = dr("rb", [128, NRB], F32)
    cst_d = dr("cst", [128, NCS], F32)
    out_d = dr("out", [S, D], F32, kind="ExternalOutput")
    wups_d = dr("wup_bf16_scratch", [128, 8 * 2 * DFF], BF16, kind="Internal")
    wdns_d = dr("wdown_bf16_scratch", [128, 22 * 1024], BF16, kind="Internal")
    FC_GROUPS = [(0, 6), (6, 12), (12, 17), (17, 22)]
    wups_g = []
    _o = 0
    for (a_, b_) in FC_GROUPS:
        n_ = (b_ - a_) * 128
        wups_g.append(wups_d[:, _o:_o + 16 * n_].rearrange("p (h k c) -> p h k c", h=2, k=8))
        _o += 16 * n_
    wdns_v = wdns_d.rearrange("p (k c) -> p k c", k=22)

    win_v = win_d.rearrange("(k p) c -> p k c", p=128)
    wout_v = wout_d.rearrange("(k p) c -> p k c", p=128)
    wup_v = wup_d.rearrange("(k p) c -> p k c", p=128)
    wdown_v = wdown_d.rearrange("(k p) c -> p k c", p=128)
    wpool_v = wpool_d.rearrange("g (cc p) d -> p g cc d", p=128)

    R = Rec()
    stack = ExitStack()
    ARENA_BYTES = 212000
    arena_t = stack.enter_context(nc.sbuf_tensor("arena", [128, ARENA_BYTES // 4], F32))
    ps = stack.enter_context(nc.psum_tensor("ps", [128, 8, 512], F32))
    A = Arena(arena_t, ARENA_BYTES)
    PB = [Buf("ps%d" % i, excl=True) for i in range(8)]
    trp_bf = ps[:, 0, :].bitcast(BF16)

    def mm(out, lhsT, rhs, start, stop, r, w):
        R.add("pe", lambda e: e.matmul(out, lhsT, rhs, start=start, stop=stop), r=r, w=w)

    def tr(out, in_, ident, r, w):
        R.add("pe", lambda e: e.transpose(out, in_, ident), r=r, w=w)

    def act(out, in_, func, r, w, bias=None, scale=None, accum=None):
        kw = {}
        if bias is not None:
            kw["bias"] = bias
        if scale is not None:
            kw["scale"] = scale
        if accum is not None:
            kw["accum_out"] = accum
        R.add("act", lambda e: e.activation(out, in_, func, **kw), r=r, w=w)

    def tt(eng, out, in0, in1, op, r, w):
        R.add(eng, lambda e: e.tensor_tensor(out, in0, in1, op), r=r, w=w)

    def ts(eng, out, in0, s1, s2, op0, op1, r, w):
        if s2 is None:
            R.add(eng, lambda e: e.tensor_scalar(out, in0, s1, None, op0), r=r, w=w)
        else:
            R.add(eng, lambda e: e.tensor_scalar(out, in0, s1, s2, op0, op1), r=r, w=w)

    def stt(out, in0, scalar, in1, op0, op1, r, w):
        R.add("dve", lambda e: e.scalar_tensor_tensor(out, in0, scalar, in1, op0, op1), r=r, w=w)

    def cp(eng, out, in_, r, w):
        if eng == "act":
            R.add("act", lambda e: e.copy(out, in_), r=r, w=w)
        else:
            R.add(eng, lambda e: e.tensor_copy(out, in_), r=r, w=w)

    def mset(eng, ap, val, w):
        R.add(eng, lambda e: e.memset(ap, val), r=(), w=w)

    def dma(q, out, in_, r, w, sb, nodep=False):
        R.add(q, lambda e: e.dma_start(out=out, in_=in_), r=r, w=w, dma=sb, nodep=nodep)

    pp = A.alloc([NPP], F32)
    PPb = Buf("pp")
    bgh = A.alloc([16], F32)
    BGH = Buf("bgh")
    exps = A.alloc([8], F32)
    EXPS = Buf("exps")
    ident = A.alloc([128], BF16)
    IDENT = Buf("ident")
    neghalf = A.alloc([32], F32)
    NEGH = Buf("neghalf")
    negpi = A.alloc([8], F32)
    NEGPI = Buf("negpi")
    cwv = pp[:, PP_CW:PP_CW + 132].rearrange("p (j c) -> p j c", j=3)
    persist_mark = A.off

    win = A.alloc([8, INW], BF16)
    WIN_G, WIN_U, WIN_Q = Buf("win_g"), Buf("win_u"), Buf("win_q")
    wpool = A.alloc([4, 2, 256], BF16)
    WPOOL = Buf("wpool")
    wout = A.alloc([8, 1024], BF16)
    WOUT = Buf("wout")
    CST = Buf("cst")
    corrp = A.alloc([64], F32)
    rb1 = A.alloc([2176], F32)
    RB1 = Buf("rb1")
    g1 = rb1[:, 0:1024]
    gqk = rb1[:, 1024:2176]
    maskC = A.alloc([512], BF16)
    maskP = A.alloc([512], BF16)
    MASK = Buf("mask")
    posi = A.alloc([NBLK], I32)
    POSI = Buf("posi")
    posf = A.alloc([NBLK], F32)
    ANG = Buf("ang")
    cos_t = A.alloc([NBLK, 8], F32)
    sin_t = A.alloc([NBLK, 8], F32)
    ROPE = Buf("rope")
    onesz = A.alloc([2, 128], BF16)
    ONESZ = Buf("onesz")
    halo = A.alloc([8, 16], F32)
    HALO = [Buf("halo%d" % c) for c in range(8)]
    xn = [A.alloc([1024], F32) for _ in range(2)]
    XN = [Buf("xn%d" % i) for i in range(2)]
    xr = [A.alloc([1024], F32) for _ in range(1)]
    XR = [Buf("xr%d" % i) for i in range(1)]
    junk = A.alloc([1280], BF16)
    JUNK = Buf("junk")
    JUNKN = JUNK
    ss1 = [A.alloc([8], F32) for _ in range(2)]
    SS1 = [Buf("ss1_%d" % i) for i in range(2)]
    hn = [A.alloc([1024], BF16) for _ in range(2)]
    HN = [Buf("hn%d" % i) for i in range(2)]
    hT = [A.alloc([8, 256], BF16) for _ in range(1)]
    HT = [[Buf("hT%d_%d" % (s, j)) for j in range(2)] for s in range(1)]
    gT = A.alloc([16, 256], F32)
    GT = [Buf("gT%d" % c) for c in range(16)]
    u2 = [A.alloc([2, 272], F32) for _ in range(2)]
    U2 = [[Buf("u2_%d_%d" % (s, cc)) for cc in range(2)] for s in range(2)]
    U2H = [[Buf("u2h%d_%d" % (s, cc)) for cc in range(2)] for s in range(2)]
    sA = A.alloc([2, 272], F32)
    sB = A.alloc([2, 272], F32)
    SA, SB = [Buf("sA0"), Buf("sA1")], [Buf("sB0"), Buf("sB1")]
    pooledT = [A.alloc([2, 256], BF16) for _ in range(4)]
    POOLED = [Buf("pooled%d" % s) for s in range(4)]
    ag = A.alloc([8, 256], F32)
    AG = [Buf("ag%d" % c) for c in range(8)]
    sq = junk[:, 0:1152]
    SQ = JUNK
    st18 = A.alloc([3, 18], F32)
    ST18 = Buf("st18")
    qn = A.alloc([18, 64], F32)
    QN = Buf("qn")
    rtmp = A.alloc([4, 18, 8], F32)
    RT = [Buf("rtmp%d" % i) for i in range(4)]
    qkb = [A.alloc([18, 64], BF16) for _ in range(2)]
    QKB = [Buf("qkb%d" % i) for i in range(2)]
    kz = [A.alloc([4, 128], BF16) for _ in range(3)]
    KZ = [Buf("kz%d" % i) for i in range(3)]
    vz = [A.alloc([4, 128], BF16) for _ in range(3)]
    VZ = [Buf("vz%d" % i) for i in range(3)]
    qT = [A.alloc([8, 128], BF16) for _ in range(2)]
    QT = [Buf("qT%d" % i) for i in range(2)]
    kT = [A.alloc([4, 128], BF16) for _ in range(3)]
    KT = [Buf("kT%d" % i) for i in range(3)]
    scr_base = A.off
    PT = [[A.alloc([512], BF16) for _ in range(8)] for _ in range(1)]
    PTB = [[Buf("PT%d_%d" % (s, i)) for i in range(8)] for s in range(1)]
    rec = [A.alloc([4, 128], F32) for _ in range(2)]
    REC = [Buf("rec%d" % i) for i in range(2)]
    A2 = Arena(arena_t, A.off)
    A2.off = scr_base
    cstf = A2.alloc([NCS], F32)
    ang = A2.alloc([NBLK, 8], F32)
    angs = A2.alloc([NBLK, 8], F32)
    angk = A2.alloc([NBLK, 8], F32)
    angi = A2.alloc([NBLK, 8], I32)
    cT = [A.alloc([8, 256], BF16) for _ in range(2)]
    CT = [[Buf("cT%d_%d" % (s, j)) for j in range(2)] for s in range(2)]
    OUTB = [Buf("out%d" % i) for i in range(NBLK)]
    p1_end = A.off
    print('[arena] phase1 end', p1_end, 'of', ARENA_BYTES)

    dma("sp", pp, pp_d[:, :], [], [PPb], PPb)
    dma("sp", cstf, cst_d[:, :], [], [CST], CST)
    dma("sp", rb1, rb_d[:, 0:2176], [], [RB1], RB1)
    dma("sp", posi, pos_d[:, :], [], [POSI], POSI)
    WB = {kd: {"act": Buf("w%s_a" % kd), "dve": Buf("w%s_d" % kd)} for kd in "QUGPO"}
    WIN_Q_L = list(WB["Q"].values())
    WIN_U_L = list(WB["U"].values())
    WIN_G_L = list(WB["G"].values())
    WPOOL_L = list(WB["P"].values())
    WOUT_L = list(WB["O"].values())
    stage_jobs = []
    for k in range(8):
        for h in range(2):
            c0 = 1024 + h * 640
            stage_jobs.append((win[:, k, c0:c0 + 640], win_v[:, k, c0:c0 + 640], "Q", 640))
    for k in range(8):
        stage_jobs.append((win[:, k, 0:1024], win_v[:, k, 0:1024], "U", None))
    for k in range(8):
        for h in range(2):
            c0 = 2304 + h * 1024
            stage_jobs.append((win[:, k, c0:c0 + 1024], win_v[:, k, c0:c0 + 1024], "G", None))
    for h in range(2):
        stage_jobs.append((wpool[:, 2 * h:2 * h + 2], wpool_v[:, 2 * h:2 * h + 2], "P", "p (g c) d -> p g c d"))
    for k in range(8):
        stage_jobs.append((wout[:, k, :], wout_v[:, k, :], "O", None))

    stage_ctr = [0]

    def emit_stage_jobs(kinds):
        todo = [j_ for j_ in stage_jobs if j_[2] in kinds]
        for (dst, src, kd, rr) in todo:
            i = stage_ctr[0]
            stage_ctr[0] += 1
            sl_ = i % 4
            sv = gT[:, 4 * sl_:4 * sl_ + 4, :]
            if isinstance(rr, int):
                sv = sv.rearrange("p a b -> p (a b)")[:, 0:rr]
            elif rr:
                sv = sv.rearrange(rr, g=2)
            else:
                sv = sv.rearrange("p a b -> p (a b)")
            GS = GT[4 * sl_:4 * sl_ + 4]
            dma("sp", sv, src, [], GS, GT[4 * sl_])
            eng = "act" if i % 2 == 0 else "dve"
            if eng == "act":
                R.add("act", (lambda d_, s_: (lambda e: e.copy(d_, s_)))(dst, sv), r=GS, w=[WB[kd][eng]], nodep=True)
            else:
                R.add("dve", (lambda d_, s_: (lambda e: e.tensor_copy(d_, s_)))(dst, sv), r=GS, w=[WB[kd][eng]], nodep=True)

    cp("dve", corrp, cstf[:, CS_CORR:CS_CORR + 64], [CST], [BGH])
    cp("dve", ident, cstf[:, CS_ID:CS_ID + 128], [CST], [IDENT])
    cp("dve", maskC, cstf[:, CS_MC:CS_MC + 512], [CST], [MASK])
    cp("dve", maskP, cstf[:, CS_MP:CS_MP + 512], [CST], [MASK])
    corr = corrp.rearrange("p (g t) -> p g t", g=4)
    ts("dve", bgh, pp[:, PP_BG:PP_BG + 16], 0.5, None, ALU.mult, None, [PPb], [BGH])
    ts("dve", gqk[:, 0:1024], gqk[:, 0:1024], 0.125, None, ALU.mult, None, [RB1], [RB1])
    act(exps, pp[:, PP_SK:PP_SK + 8], AF.Exp, [PPb], [EXPS])
    mset("dve", neghalf, -0.5, [NEGH])
    mset("dve", negpi, -math.pi, [NEGPI])
    mset("pool", halo, 0.0, HALO)
    for i in range(3):
        mset("pool", kz[i], 0.0, [KZ[i]])
        mset("pool", vz[i], 0.0, [VZ[i]])
    mset("pool", onesz, 0.0, [ONESZ])
    mset("pool", onesz[:, 0, 0:64], 1.0, [ONESZ])
    mset("pool", onesz[:, 1, 64:128], 1.0, [ONESZ])
    inv_freq = [float(ROPE_THETA ** (-(2.0 * i) / 16.0)) for i in range(8)]
    TWO_PI = 2.0 * math.pi

    def range_reduce_sin(dst, shift):
        ts("dve", angs, ang, shift, None, ALU.add, None, [ANG], [ANG])
        ts("dve", angi, angs, 1.0 / TWO_PI, None, ALU.mult, None, [ANG], [ANG])
        cp("dve", angk, angi, [ANG], [ANG])
        stt(angs, angk, -TWO_PI, angs, ALU.mult, ALU.add, [ANG], [ANG])
        ts("dve", angk, angs, math.pi, -TWO_PI, ALU.is_gt, ALU.mult, [ANG], [ANG])
        tt("dve", angs, angs, angk, ALU.add, [ANG], [ANG])
        ts("dve", angk, angs, -math.pi, TWO_PI, ALU.is_lt, ALU.mult, [ANG], [ANG])
        tt("dve", angs, angs, angk, ALU.add, [ANG], [ANG])
        ts("dve", angs, angs, math.pi, -math.pi, ALU.min, ALU.max, [ANG], [ANG])
        act(dst, angs, AF.Sin, [ANG], [ROPE])

    def rope_setup():
        cp("dve", posf, posi, [POSI], [ANG])
        for i in range(8):
            ts("dve", ang[:, :, i], posf, inv_freq[i], None, ALU.mult, None, [ANG], [ANG])
        range_reduce_sin(sin_t, 0.0)
        range_reduce_sin(cos_t, math.pi / 2)

    FM = (1, 2)
    SC = (6, 7)

    def norm_pre(gb, src_ap, src_deps, gain, GAINB, xnring, XNring, skip_load=False):
        s3 = gb % len(xnring)
        s2 = gb % 2
        s1 = gb % len(hn)
        if not skip_load:
            dma("sp", xnring[s3], src_ap, src_deps, [XNring[s3]], XNring[s3])
        act(hn[s1], xnring[s3], AF.Square, [XNring[s3]], [HN[s1], SS1[s2]], accum=ss1[s2][:, 0:1])
        ts("dve", ss1[s2][:, 1:2], ss1[s2][:, 0:1], 1.0 / D, EPS, ALU.mult, ALU.add, [SS1[s2]], [SS1[s2]])
        tt("pool", ss1[s2][:, 2:3], ss1[s2][:, 1:2], neghalf[:, 0:1], ALU.pow, [SS1[s2], NEGH], [SS1[s2]])
        stt(hn[s1], xnring[s3], ss1[s2][:, 2:3], gain, ALU.mult, ALU.mult, [XNring[s3], SS1[s2], GAINB], [HN[s1]])

    def norm_tr(gb, hT_dst, HT_dstB):
        s1 = gb % len(hn)
        for k in range(8):
            tr(trp_bf[:, k * 128:(k + 1) * 128], hn[s1][:, k * 128:(k + 1) * 128], ident, [HN[s1], IDENT], [PB[0]])
        cp("act", hT_dst, trp_bf.rearrange("p (k n) -> p k n", k=8), [PB[0]], [HT_dstB])

    def norm_block(gb, src_ap, src_deps, gain, GAINB, hT_dst, HT_dstB, xnring, XNring):
        norm_pre(gb, src_ap, src_deps, gain, GAINB, xnring, XNring)
        norm_tr(gb, hT_dst, HT_dstB)

    NT1 = S // 256
    hs = 0
    HTr = [HT[hs][0], HT[hs][1]]
    ps4 = lambda bk: ps[:, bk, :].rearrange("p (a b) -> p a b", a=4)

    def p1_norm_pre(ti, j):
        gb = 2 * ti + j
        norm_pre(gb, x_d[gb * 128:(gb + 1) * 128, :], [], g1, RB1, xn, XN)

    def p1_norm_tr(ti, j):
        norm_tr(2 * ti + j, hT[hs][:, :, j * 128:(j + 1) * 128], HT[hs][j])

    def P_mm(ti, j):
        gb = 2 * ti + j
        tok = slice(j * 128, (j + 1) * 128)
        for k in range(8):
            lh = hT[hs][:, k, tok]
            mm(ps[:, 3, :], lh, win[:, k, 1024:1536], k == 0, k == 7, [HT[hs][j]] + WIN_Q_L, [PB[3]])
            mm(ps[:, 4, :], lh, win[:, k, 1536:2048], k == 0, k == 7, [HT[hs][j]] + WIN_Q_L, [PB[4]])
            mm(ps[:, 5, 0:256], lh, win[:, k, 2048:2304], k == 0, k == 7, [HT[hs][j]] + WIN_Q_L, [PB[5]])

    def P_chain(ti, j):
        gb = 2 * ti + j
        sl = gb % 3
        qb = gb % 2
        qn2 = qn.rearrange("p h d -> p (h d)")
        tt("dve", qn2[:, 0:1024].rearrange("p (a b) -> p a b", a=2), ps[:, 3:5, :],
           gqk[:, 0:1024].rearrange("p (a b) -> p a b", a=2), ALU.mult, [PB[3], PB[4], RB1], [QN])
        tt("dve", qn2[:, 1024:1152], ps[:, 5, 0:128], gqk[:, 1024:1152], ALU.mult, [PB[5], RB1], [QN])
        act(sq[:, 0:1024].rearrange("p (a b) -> p a b", a=2), ps[:, 3:5, :], AF.Square, [PB[3], PB[4]], [SQ])
        act(sq[:, 1024:1152], ps[:, 5, 0:128], AF.Square, [PB[5]], [SQ])
        vz4 = vz[sl].rearrange("p (kv par) d -> p kv par d", kv=2)
        vsrc = ps[:, 5, 128:256].rearrange("p (kv d) -> p kv d", kv=2)
        cp("act", vz4[:, :, 0, 0:64], vsrc, [PB[5]], [VZ[sl]])
        cp("act", vz4[:, :, 1, 64:128], vsrc, [PB[5]], [VZ[sl]])
        R.add("dve", lambda e: e.tensor_reduce(st18[:, 0, :], sq.rearrange("p (h d) -> p h d", d=64), AX.X, ALU.add),
              r=[SQ], w=[ST18])
        ts("dve", st18[:, 1, :], st18[:, 0, :], 1.0 / 64, EPS, ALU.mult, ALU.add, [ST18], [ST18])
        tt("pool", st18[:, 2, :], st18[:, 1, :], neghalf[:, 0:18], ALU.pow, [ST18, NEGH], [ST18])
        cosb = cos_t[:, gb, :].unsqueeze(1).to_broadcast([128, 18, 8])
        sinb = sin_t[:, gb, :].unsqueeze(1).to_broadcast([128, 18, 8])
        x1 = qn[:, :, 0:8]
        x2 = qn[:, :, 8:16]
        tt("dve", rtmp[:, 0], x1, cosb, ALU.mult, [QN, ROPE], [RT[0]])
        tt("dve", rtmp[:, 1], x2, sinb, ALU.mult, [QN, ROPE], [RT[1]])
        tt("dve", rtmp[:, 2], x2, cosb, ALU.mult, [QN, ROPE], [RT[2]])
        tt("dve", rtmp[:, 3], x1, sinb, ALU.mult, [QN, ROPE], [RT[3]])
        tt("dve", qn[:, :, 0:8], rtmp[:, 0], rtmp[:, 1], ALU.subtract, [RT[0], RT[1]], [QN])
        tt("dve", qn[:, :, 8:16], rtmp[:, 2], rtmp[:, 3], ALU.add, [RT[2], RT[3]], [QN])
        tt("dve", qkb[qb], qn, st18[:, 2, :].unsqueeze(2).to_broadcast([128, 18, 64]), ALU.mult,
           [QN, ST18], [QKB[qb]])
        kz4 = kz[sl].rearrange("p (kv par) d -> p kv par d", kv=2)
        cp("pool", kz4[:, :, 0, 0:64], qkb[qb][:, 16:18, :], [QKB[qb]], [KZ[sl]])
        cp("pool", kz4[:, :, 1, 64:128], qkb[qb][:, 16:18, :], [QKB[qb]], [KZ[sl]])

    def P_tr(ti, j):
        gb = 2 * ti + j
        sl = gb % 3
        qb = gb % 2
        qkb2 = qkb[qb].rearrange("p h d -> p (h d)")
        for c in range(8):
            tr(trp_bf[:, c * 128:(c + 1) * 128], qkb2[:, c * 128:(c + 1) * 128], ident, [QKB[qb], IDENT], [PB[0]])
        cp("act", qT[qb], trp_bf.rearrange("p (k n) -> p k n", k=8), [PB[0]], [QT[qb]])
        for i in range(4):
            tr(trp_bf[:, i * 128:(i + 1) * 128], kz[sl][:, i, :], ident, [KZ[sl], IDENT], [PB[0]])
        cp("act", kT[sl], trp_bf[:, 0:512].rearrange("p (k n) -> p k n", k=4), [PB[0]], [KT[sl]])

    def gates(ti, c, banks=None):
        bank = (banks or SC)[c % 2]
        for k in range(8):
            mm(ps[:, bank, 0:256], win[:, k, 2304 + c * 128:2304 + (c + 1) * 128], hT[hs][:, k, :],
               k == 0, k == 7, WIN_G_L + HTr, [PB[bank]])
        act(gT[:, c, :], ps[:, bank, 0:256], AF.Tanh, [PB[bank], BGH], [GT[c]],
            bias=bgh[:, c:c + 1], scale=0.5)

    UBANK = ((1, 2), (3, 4))

    def C_u(ti, g):
        us = g % 2
        U = u2[us]
        for cc in range(2):
            c = 2 * g + cc
            bank = UBANK[us][cc]
            for k in range(8):
                mm(ps[:, bank, 0:256], win[:, k, c * 128:(c + 1) * 128], hT[hs][:, k, :],
                   k == 0, k == 7, WIN_U_L + HTr, [PB[bank]])
            cp("pool", U[:, cc, 0:16], halo[:, c, :], [HALO[c]], [U2H[us][cc]])
            cp("act", U[:, cc, 16:272], ps[:, bank, 0:256], [PB[bank]], [U2[us][cc]])
            cp("pool", halo[:, c, :], U[:, cc, 256:272], [U2[us][cc]], [HALO[c]])

    def C_pool(ti, g):
        us = g % 2
        U = u2[us]
        Ur = [[U2[us][cc], U2H[us][cc]] for cc in range(2)]
        for cc in range(2):
            tt("dve", sA[:, cc, 1:272], U[:, cc, 1:272], U[:, cc, 0:271], ALU.add, Ur[cc], [SA[cc]])
        cur, CUR = sA, SA
        if g >= 1:
            for cc in range(2):
                tt("dve", sB[:, cc, 3:272], sA[:, cc, 3:272], sA[:, cc, 1:270], ALU.add, [SA[cc]], [SB[cc]])
            cur, CUR = sB, SB
        if g >= 2:
            for cc in range(2):
                tt("dve", sA[:, cc, 7:272], sB[:, cc, 7:272], sB[:, cc, 3:268], ALU.add, [SB[cc]], [SA[cc]])
            cur, CUR = sA, SA
        if g >= 3:
            for cc in range(2):
                tt("dve", sB[:, cc, 15:272], sA[:, cc, 15:272], sA[:, cc, 7:264], ALU.add, [SA[cc]], [SB[cc]])
            cur, CUR = sB, SB
        if ti == 0:
            for cc in range(2):
                tt("dve", cur[:, cc, 16:32], cur[:, cc, 16:32], corr[:, g, :], ALU.mult, [CUR[cc], BGH], [CUR[cc]])
        pl = pooledT[g]
        for cc in range(2):
            stt(pl[:, cc, :], cur[:, cc, 16:272], 1.0 / (2 ** (g + 1)), U[:, cc, 16:272],
                ALU.mult, ALU.subtract, [CUR[cc]] + Ur[cc], [POOLED[g]])

    def C_map(ti, g):
        pl = pooledT[g]
        for dc in range(2):
            ch = 2 * g + dc
            bk = (5, 1, 2, 3, 4, 6, 7, 5)[ch]
            hf = 1 if ch == 7 else 0
            o = ps[:, bk, hf * 256:(hf + 1) * 256]
            for cc in range(2):
                mm(o, wpool[:, g, cc, dc * 128:(dc + 1) * 128], pl[:, cc, :],
                   cc == 0, cc == 1, WPOOL_L + [POOLED[g]], [PB[bk]])
        for dc in range(2):
            ch = 2 * g + dc
            bk = (5, 1, 2, 3, 4, 6, 7, 5)[ch]
            hf = 1 if ch == 7 else 0
            o = ps[:, bk, hf * 256:(hf + 1) * 256]
            stt(ag[:, ch, :], o, pp[:, PP_PS + ch:PP_PS + ch + 1], gT[:, ch, :],
                ALU.mult, ALU.mult, [PB[bk], PPb, GT[ch]], [AG[ch]])

    def S_core(ti, j):
        gb = 2 * ti + j
        sl = gb % 3
        psl = (gb - 1) % 3
        qb = gb % 2
        kbs = [(psl, maskP), (sl, maskC)] if gb > 0 else [(sl, maskC)]
        cnt = 0
        for kv in range(2):
            for par in range(2):
                for kbi, (ksl, mk) in enumerate(kbs):
                    bank = SC[cnt % 2]
                    cnt += 1
                    u = (kv * 2 + par) * 2 + kbi
                    mm(ps4(bank), kT[ksl][:, kv * 2 + par, :],
                       qT[qb][:, kv * 4:(kv + 1) * 4, :], True, False, [KT[ksl], QT[qb]], [PB[bank]])
                    mm(ps[:, bank, :], ident, mk, False, True, [IDENT, MASK], [PB[bank]])
                    act(PT[0][u], ps[:, bank, :], AF.Exp, [PB[bank]], [PTB[0][u]])

    PVB = (((1, 2), (3, 4)), ((1, 2), (6, 7)))

    def S_pv(ti, j):
        gb = 2 * ti + j
        sl = gb % 3
        psl = (gb - 1) % 3
        kbs = [(psl, maskP), (sl, maskC)] if gb > 0 else [(sl, maskC)]
        for kv in range(2):
            pvb, denb = PVB[j][kv]
            n = 2 * len(kbs)
            i = 0
            for par in range(2):
                for kbi, (ksl, mk) in enumerate(kbs):
                    u = (kv * 2 + par) * 2 + kbi
                    mm(ps[:, pvb, :], vz[ksl][:, kv * 2 + par, :], PT[0][u], i == 0, i == n - 1,
                       [VZ[ksl], PTB[0][u]], [PB[pvb]])
                    i += 1
            i = 0
            for par in range(2):
                for kbi, (ksl, mk) in enumerate(kbs):
                    u = (kv * 2 + par) * 2 + kbi
                    mm(ps[:, denb, :], onesz[:, par, :], PT[0][u], i == 0, i == n - 1,
                       [ONESZ, PTB[0][u]], [PB[denb]])
                    i += 1

    def N_chain(ti, j):
        N_a(ti, j)
        N_b(ti, j)

    def N_a(ti, j):
        banks = PVB[j]
        for kv in range(2):
            ch = slice(kv * 4, (kv + 1) * 4)
            tt("dve", rec[kv], ps4(banks[kv][1]), exps[:, ch].unsqueeze(2).to_broadcast([128, 4, 128]), ALU.add,
               [PB[banks[kv][1]], EXPS], [REC[kv]])
        for kv in range(2):
            rk = rec[kv]
            R.add("dve", (lambda rk: (lambda e: e.reciprocal(rk, rk)))(rk), r=[REC[kv]], w=[REC[kv]])

    def N_b(ti, j):
        cs = ti % 2
        tok = slice(j * 128, (j + 1) * 128)
        banks = PVB[j]
        for kv in range(2):
            tt("dve", rec[kv], ps4(banks[kv][0]), rec[kv], ALU.mult, [PB[banks[kv][0]], REC[kv]], [REC[kv]])
        for kv in range(2):
            gch = slice(8 + kv * 4, 8 + (kv + 1) * 4)
            tt("pool", rec[kv], rec[kv], gT[:, gch, tok], ALU.mult, [REC[kv]] + GT[gch], [REC[kv]])
        for kv in range(2):
            ch = slice(kv * 4, (kv + 1) * 4)
            tt("pool", cT[cs][:, ch, tok], rec[kv], ag[:, ch, tok], ALU.add, [REC[kv]] + AG[ch], [CT[cs][j]])

    def E_mm(ti, j):
        gb = 2 * ti + j
        cs = ti % 2
        rs = 0
        tok = slice(j * 128, (j + 1) * 128)
        b0, b1 = (3, 4) if j == 0 else (1, 2)
        dma("sp", xr[rs], x_d[gb * 128:(gb + 1) * 128, :], [], [XR[rs]], XR[rs])
        for k in range(8):
            lh = cT[cs][:, k, tok]
            mm(ps[:, b0, :], lh, wout[:, k, 0:512], k == 0, k == 7, [CT[cs][j]] + WOUT_L, [PB[b0]])
            mm(ps[:, b1, :], lh, wout[:, k, 512:1024], k == 0, k == 7, [CT[cs][j]] + WOUT_L, [PB[b1]])

    def E_add(ti, j):
        gb = 2 * ti + j
        rs = 0
        b0, b1 = (3, 4) if j == 0 else (1, 2)
        xr3 = xr[rs].rearrange("p (a b) -> p a b", a=2)
        tt("dve", xr3, ps[:, b0:b1 + 1, :], xr3, ALU.add, [PB[b0], PB[b1], XR[rs]], [XR[rs]])
        dma("sp", out_d[gb * 128:(gb + 1) * 128, :], xr[rs], [XR[rs]], [OUTB[gb]], XR[rs])

    WUPS, WDNS = Buf("wups"), Buf("wdns")
    conv_jobs = []
    for gi_, (a_, b_) in enumerate(FC_GROUPS):
        for h_ in range(2):
            for k in range(8):
                conv_jobs.append((wups_g[gi_][:, h_, k, :],
                                  wup_v[:, k, h_ * DFF + a_ * 128:h_ * DFF + b_ * 128], WUPS))
    for kk in range(22):
        conv_jobs.append((wdns_v[:, kk, :], wdown_v[:, kk, :], WDNS))

    def emit_conv(n):
        for _ in range(n):
            if conv_jobs:
                o, i, B_ = conv_jobs.pop(0)
                dma("pool", o, i, [], [B_], B_, nodep=True)

    for j in range(2):
        p1_norm_pre(0, j)
        p1_norm_tr(0, j)
    emit_stage_jobs("Q")
    rope_setup()
    P_mm(0, 0)
    P_chain(0, 0)
    emit_stage_jobs("U")
    for ti in range(NT1):
        P_mm(ti, 1)
        P_chain(ti, 1)
        C_u(ti, 0)
        C_u(ti, 1)
        if ti == 0:
            emit_stage_jobs("GPO")
        if ti == 0:
            for c in range(0, 4):
                gates(ti, c)
            C_pool(ti, 0)
            C_u(ti, 2)
            for c in range(4, 8):
                gates(ti, c)
            P_tr(ti, 0)
            ts("pool", gT[:, 0:8, :], gT[:, 0:8, :], 0.5, 0.5, ALU.mult, ALU.add, GT[0:8], GT[0:8])
            C_pool(ti, 1)
            C_u(ti, 3)
            for c in range(8, 16):
                gates(ti, c)
        else:
            for c in range(8, 12):
                gates(ti, c)
            C_pool(ti, 0)
            C_u(ti, 2)
            for c in range(12, 16):
                gates(ti, c)
            P_tr(ti, 0)
            C_pool(ti, 1)
            C_u(ti, 3)
        ts("pool", gT[:, 8:16, :], gT[:, 8:16, :], 0.5, 0.5, ALU.mult, ALU.add, GT[8:16], GT[8:16])
        P_tr(ti, 1)
        C_pool(ti, 2)
        C_pool(ti, 3)
        for g in range(4):
            C_map(ti, g)
        nxt = ti + 1 < NT1
        if nxt:
            p1_norm_pre(ti + 1, 0)
            p1_norm_pre(ti + 1, 1)
        S_core(ti, 0)
        S_pv(ti, 0)
        if nxt:
            p1_norm_tr(ti + 1, 0)
        S_core(ti, 1)
        N_chain(ti, 0)
        S_pv(ti, 1)
        if nxt:
            p1_norm_tr(ti + 1, 1)
        E_mm(ti, 0)
        if nxt:
            for c in range(0, 8):
                gates(ti + 1, c, banks=(5, 0))
        N_a(ti, 1)
        E_add(ti, 0)
        if nxt:
            P_mm(ti + 1, 0)
        N_b(ti, 1)
        if nxt:
            ts("pool", gT[:, 0:8, :], gT[:, 0:8, :], 0.5, 0.5, ALU.mult, ALU.add, GT[0:8], GT[0:8])
            P_chain(ti + 1, 0)
        E_mm(ti, 1)
        E_add(ti, 1)
        emit_conv(6)
    emit_conv(100)

    R.barrier()
    A.off = persist_mark
    fc_groups = FC_GROUPS
    wupg = [A.alloc([2, 8, (b_ - a_) * 128], BF16) for (a_, b_) in fc_groups]
    WUPG = [Buf("wup%d" % i) for i in range(4)]
    fc2grp = {}
    for gi, (a, b) in enumerate(fc_groups):
        for fc in range(a, b):
            fc2grp[fc] = gi
    wdown = A.alloc([22, 1024], BF16)
    WDN = [Buf("wdn0"), Buf("wdn1")]
    g2 = A.alloc([1024], F32)
    G2 = Buf("g2")
    chalo = A.alloc([44, 2], F32)
    CH = [Buf("chalo%d" % i) for i in range(44)]
    xn2 = [A.alloc([1024], F32) for _ in range(2)]
    XN2 = [Buf("xn2_%d" % i) for i in range(2)]
    xr2 = [A.alloc([1024], F32) for _ in range(2)]
    XR2 = [Buf("xr2_%d" % i) for i in range(2)]
    ss1 = [A.alloc([8], F32) for _ in range(2)]
    hn = [A.alloc([1024], BF16) for _ in range(1)]
    hT2 = A.alloc([8, 512], BF16)
    HT2 = [Buf("hT2_%d" % j) for j in range(4)]
    yb = [[A.alloc([512], F32) for _ in range(2)] for _ in range(2)]
    YB = [[Buf("y%d_%d" % (h, s)) for s in range(2)] for h in range(2)]
    ub = [[A.alloc([514], F32) for _ in range(2)] for _ in range(2)]
    UH = [[Buf("uh%d_%d" % (h, s)) for s in range(2)] for h in range(2)]
    UM = [[Buf("um%d_%d" % (h, s)) for s in range(2)] for h in range(2)]
    sg = [A.alloc([512], F32) for _ in range(2)]
    SG = [Buf("sg%d" % s) for s in range(2)]
    hid = A.alloc([22, 512], BF16)
    HID = [Buf("hid%d" % fc) for fc in range(22)]
    print('[arena] phase2 end', A.off)

    dma("sp", g2, rb_d[:, RB_G2:RB_G2 + 1024], [], [G2], G2)
    mset("pool", chalo, 0.0, CH)

    def p2_wload(gi):
        dma("sp", wupg[gi].rearrange("p h k c -> p (h k c)"), wups_g[gi].rearrange("p h k c -> p (h k c)"),
            [WUPS], [WUPG[gi]], WUPG[gi])

    def p2_xload(j):
        dma("sp", xn2[j % 2], out_d[j * 128:(j + 1) * 128, :], [OUTB[j]], [XN2[j % 2]], XN2[j % 2])

    GB = ((1, 2, 3), (4, 5, 6))
    NT2 = S // 512

    def p2_norm_pre(ti, j, skip_load=False):
        gb = 4 * ti + j
        norm_pre(gb, out_d[gb * 128:(gb + 1) * 128, :], [OUTB[gb]], g2, G2, xn2, XN2, skip_load=skip_load)

    def p2_loads(ti, j):
        gb = 4 * ti + j
        rs = gb % 2
        dma("sp", xr2[rs], out_d[gb * 128:(gb + 1) * 128, :], [OUTB[gb]], [XR2[rs]], XR2[rs])
        if ti + 1 < NT2:
            gn = 4 * (ti + 1) + j
            dma("sp", xn2[gn % 2], out_d[gn * 128:(gn + 1) * 128, :], [OUTB[gn]], [XN2[gn % 2]], XN2[gn % 2])

    def p2_norm_tr(ti, j):
        norm_tr(4 * ti + j, hT2[:, :, j * 128:(j + 1) * 128], HT2[j])

    def ffn_front(fc):
        rg = fc % 2
        for half in range(2):
            chn = half * 22 + fc
            cp("pool", ub[half][rg][:, 0:2], chalo[:, chn, :], [CH[chn]], [UH[half][rg]])
        for half in range(2):
            chn = half * 22 + fc
            col0 = half * DFF + fc * 128
            bank = GB[half][fc % 3]
            for k in range(8):
                gi_ = fc2grp[fc]
                lc = (fc - fc_groups[gi_][0]) * 128
                mm(ps[:, bank, :], wupg[gi_][:, half, k, lc:lc + 128], hT2[:, k, :], k == 0, k == 7,
                   [WUPG[gi_]] + HT2, [PB[bank]])
        for half in range(2):
            chn = half * 22 + fc
            bank = GB[half][fc % 3]
            act(yb[half][rg], ps[:, bank, :], AF.Identity, [PB[bank], PPb], [YB[half][rg]],
                bias=pp[:, PP_CB + chn:PP_CB + chn + 1], scale=cwv[:, 2, chn:chn + 1])
            cp("act", ub[half][rg][:, 2:514], ps[:, bank, :], [PB[bank]], [UM[half][rg]])
        for half in range(2):
            chn = half * 22 + fc
            cp("pool", chalo[:, chn, :], ub[half][rg][:, 512:514], [UM[half][rg]], [CH[chn]])
        for tap, off in ((1, 1), (0, 0)):
            for half in range(2):
                chn = half * 22 + fc
                y = yb[half][rg]
                stt(y, ub[half][rg][:, off:off + 512], cwv[:, tap, chn:chn + 1], y, ALU.mult, ALU.add,
                    [UH[half][rg], UM[half][rg], PPb, YB[half][rg]], [YB[half][rg]])

    def ffn_silu(fc):
        rg = fc % 2
        act(sg[rg], yb[0][rg], AF.Silu, [YB[0][rg]], [SG[rg]])

    def ffn_mult(fc):
        rg = fc % 2
        tt("pool", hid[:, fc, :], sg[rg], yb[1][rg], ALU.mult, [SG[rg], YB[1][rg]], [HID[fc]])

    p2_xload(0)
    p2_xload(1)
    p2_wload(0)
    for j in range(4):
        p2_norm_pre(0, j, skip_load=True)
        p2_norm_tr(0, j)
        if j + 2 < 4:
            p2_xload(j + 2)
        if j == 1:
            p2_wload(1)
            dma("sp", wdown[:, 0:11, :], wdns_v[:, 0:11, :], [WDNS], [WDN[0]], WDN[0])
    p2_wload(2)
    p2_wload(3)
    dma("sp", wdown[:, 11:22, :], wdns_v[:, 11:22, :], [WDNS], [WDN[1]], WDN[1])
    for ti in range(NT2):
        for fc in range(22):
            ffn_front(fc)
            if fc > 0:
                ffn_silu(fc - 1)
                ffn_mult(fc - 1)
        ffn_silu(21)
        ffn_mult(21)
        p2_loads(ti, 0)
        for j in range(4):
            gb = 4 * ti + j
            rs = gb % 2
            tok = slice(j * 128, (j + 1) * 128)
            if ti + 1 < NT2:
                p2_norm_pre(ti + 1, j, skip_load=True)
            g0 = 1 if j % 2 == 0 else 4
            for fc in range(22):
                lh = hid[:, fc, tok]
                mm(ps[:, g0, :], lh, wdown[:, fc, 0:512], fc == 0, fc == 21, [HID[fc], WDN[fc // 11]], [PB[g0]])
                mm(ps[:, g0 + 1, :], lh, wdown[:, fc, 512:1024], fc == 0, fc == 21, [HID[fc], WDN[fc // 11]], [PB[g0 + 1]])
            if ti + 1 < NT2:
                p2_norm_tr(ti + 1, j)
            if j + 1 < 4:
                p2_loads(ti, j + 1)
            xr3 = xr2[rs].rearrange("p (a b) -> p a b", a=2)
            tt("dve", xr3, ps[:, g0:g0 + 2, :], xr3, ALU.add, [PB[g0], PB[g0 + 1], XR2[rs]], [XR2[rs]])
            dma("sp", out_d[gb * 128:(gb + 1) * 128, :], xr2[rs], [XR2[rs]], [OUTB[gb]], XR2[rs])

    R.finish()
    R.emit(nc, stack)
    stack.close()
    return nc


def _host_layout(inputs):
    f = lambda a: np.ascontiguousarray(np.asarray(a), dtype=np.float32)
    b_gate = f(inputs["b_gate"])[0]
    pool_scale = f(inputs["pool_scale"])[0]
    sinks = f(inputs["sinks"])[0]
    conv_w = f(inputs["conv_w"])[0]
    conv_b = f(inputs["conv_b"])[0]
    pp = np.zeros((128, NPP), np.float32)
    pp[:, PP_BG:PP_BG + 16] = b_gate.reshape(16, 128).T
    pp[:, PP_PS:PP_PS + 8] = pool_scale.reshape(8, 128).T
    pp[:, PP_SK:PP_SK + 8] = np.repeat(sinks.reshape(8, 2).T, 64, axis=0)
    pp[:, PP_CW:PP_CW + 132] = conv_w.reshape(3, 44, 128).transpose(2, 0, 1).reshape(128, 132)
    pp[:, PP_CB:PP_CB + 44] = conv_b.reshape(44, 128).T
    rb = np.zeros((128, NRB), np.float32)
    rb[:, RB_G1:RB_G1 + 1024] = f(inputs["attn_norm"])[0][None, :]
    rb[:, RB_GQ:RB_GQ + 1024] = np.tile(f(inputs["q_norm"])[0], 16)[None, :]
    rb[:, RB_GQ + 1024:RB_GQ + 1152] = np.tile(f(inputs["k_norm"])[0], 2)[None, :]
    rb[:, RB_G2:RB_G2 + 1024] = f(inputs["ffn_norm"])[0][None, :]
    cst = np.zeros((128, NCS), np.float32)
    cst[:, CS_ID:CS_ID + 128] = np.eye(128, dtype=np.float32)
    jj = np.arange(128)[:, None]
    ii = np.arange(128)[None, :]
    mc = np.where(jj <= ii, 0.0, NEG).astype(np.float32)
    mp = np.where(jj > ii, 0.0, NEG).astype(np.float32)
    cst[:, CS_MC:CS_MC + 512] = np.tile(mc, (1, 4))
    cst[:, CS_MP:CS_MP + 512] = np.tile(mp, (1, 4))
    corr = np.ones((4, 16), np.float32)
    for g in range(4):
        w = 2 ** (g + 1)
        for t in range(16):
            corr[g, t] = w / min(t + 1, w)
    cst[:, CS_CORR:CS_CORR + 64] = corr.reshape(1, 64)
    return pp, rb, cst


_NC_CACHE = {}


def kernel(**inputs):
    x = np.ascontiguousarray(np.asarray(inputs["x"]), dtype=np.float32)
    positions = np.ascontiguousarray(np.asarray(inputs["positions"]), dtype=np.int32)
    pp, rb, cst = _host_layout(inputs)
    f = lambda a: np.ascontiguousarray(np.asarray(a), dtype=np.float32)
    w_in = f(inputs["w_in"])[0]
    w_pool = f(inputs["w_pool"])[0]
    w_out = f(inputs["w_out"])[0]
    w_up = f(inputs["w_up"])[0]
    w_down = f(inputs["w_down"])[0]
    if "nc" not in _NC_CACHE:
        _NC_CACHE["nc"] = build_program()
    nc = _NC_CACHE["nc"]
    n = x.shape[0]
    in_maps = []
    for i in range(n):
        in_maps.append({
            "x": x[i],
            "posT": np.ascontiguousarray(positions[i].reshape(NBLK, 128).T),
            "w_in": w_in, "w_pool": w_pool, "w_out": w_out, "w_up": w_up, "w_down": w_down,
            "pp": pp, "rb": rb, "cst": cst,
        })
    res = run_bass_kernel_spmd(nc, in_maps, core_ids=list(range(n)))
    out = np.stack([np.asarray(r["out"]) for r in res.results], axis=0)
    return out.astype(np.float32)
```

```python
import math
from contextlib import ExitStack

import numpy as np
import concourse.bass as bass
import concourse.mybir as mybir
from concourse.bass_utils import run_bass_kernel_spmd

F32 = mybir.dt.float32
BF16 = mybir.dt.bfloat16
I32 = mybir.dt.int32
ALU = mybir.AluOpType
AF = mybir.ActivationFunctionType
AX = mybir.AxisListType

S = 4096
D = 1024
NBLK = S // 128
INW = 4352
DFF = 2816
EPS = 1e-6
ROPE_THETA = 500000.0
NEG = -30000.0

PP_BG = 0
PP_PS = 16
PP_SK = 24
PP_CW = 32
PP_CB = 164
NPP = 208
RB_G1 = 0
RB_GQ = 1024
RB_G2 = 2176
NRB = 3200
CS_ID = 0
CS_MC = 128
CS_MP = 640
CS_CORR = 1152
NCS = 1216


import os
STRICT_WAR = False


class Buf:
    __slots__ = ("name", "w", "r", "sem", "cnt", "excl")

    def __init__(self, name, excl=False):
        self.name = name
        self.excl = excl
        self.w = None
        self.r = {}
        self.sem = None
        self.cnt = 0


class Ins:
    __slots__ = ("eng", "fn", "deps", "idx", "inc", "dma", "semv", "val")


class Rec:
    ENGS = ("pe", "act", "dve", "pool", "sp")
    INORDER_FREE = ("pe", "sp")

    def __init__(self):
        self.streams = {e: [] for e in self.ENGS}
        self.bar = {e: [] for e in self.ENGS}
        self.dmas = []
        self.dma_bufs = []

    def add(self, eng, fn, r=(), w=(), dma=None, nodep=False):
        I = Ins()
        I.eng = eng
        I.fn = fn
        I.dma = dma
        I.inc = False
        I.val = 0
        I.semv = None
        deps = {}
        for b in r:
            if b.w is not None:
                deps[id(b.w)] = b.w
            if b.excl:
                for q in b.r.values():
                    if q.eng != eng:
                        deps[id(q)] = q
        for b in w:
            if nodep:
                continue
            if b.w is not None:
                if not (b.w.dma is None and b.w.eng == eng and dma is None and not STRICT_WAR):
                    deps[id(b.w)] = b.w
            for q in b.r.values():
                if q.dma is None and q.eng == eng and dma is None and not STRICT_WAR:
                    continue
                deps[id(q)] = q
        for q in self.bar[eng]:
            deps[id(q)] = q
        self.bar[eng] = []
        if dma is not None:
            if dma.sem is None:
                self.dma_bufs.append(dma)
                dma.sem = True
            dma.cnt += 16
            I.semv = (dma, dma.cnt)
            self.dmas.append(I)
        I.deps = list(deps.values())
        for b in r:
            key = eng if dma is None else ("d", len(b.r))
            b.r[key] = I
        for b in w:
            b.w = I
            b.r = {}
        st = self.streams[eng]
        I.idx = len(st)
        st.append(I)
        return I

    def barrier(self):
        last = []
        for e in self.ENGS:
            if e == "sp":
                continue
            if self.streams[e]:
                last.append(self.streams[e][-1])
        lat = {}
        for I in self.dmas:
            lat[id(I.semv[0])] = I
        last.extend(lat.values())
        for e in self.ENGS:
            self.bar[e] = list(last)

    def finish(self):
        lat = {}
        for I in self.dmas:
            lat[id(I.semv[0])] = I
        self.bar["sp"] = list(lat.values())
        self.add("sp", None)

    def emit(self, nc, stack):
        engobj = {"pe": "tensor", "act": "scalar", "dve": "vector", "pool": "gpsimd", "sp": "sync"}
        for e in self.ENGS:
            for I in self.streams[e]:
                for Dp in I.deps:
                    if Dp.dma is None:
                        if Dp.eng == I.eng and I.eng in self.INORDER_FREE:
                            continue
                        Dp.inc = True
        for e in self.ENGS:
            c = 0
            for I in self.streams[e]:
                if I.inc:
                    c += 1
                I.val = c
        sems = {e: stack.enter_context(nc.semaphore("s_" + e)) for e in self.ENGS}
        for i, b in enumerate(self.dma_bufs):
            b.sem = stack.enter_context(nc.semaphore("d%d" % i))
        block = stack.enter_context(nc.Block())

        def body_for(e):
            def body(engine):
                waited = {}
                for I in self.streams[e]:
                    for Dp in I.deps:
                        if Dp.dma is not None:
                            key = id(Dp.semv[0])
                            v = Dp.semv[1]
                            sem = Dp.semv[0].sem
                        else:
                            if Dp.eng == e and e in self.INORDER_FREE:
                                continue
                            key = Dp.eng
                            v = Dp.val
                            sem = sems[Dp.eng]
                        if waited.get(key, 0) >= v:
                            continue
                        waited[key] = v
                        engine.wait_ge(sem, v)
                    if I.fn is not None:
                        ins = I.fn(engine)
                        if I.dma is not None:
                            ins.then_inc(I.dma.sem, 16)
                        elif I.inc:
                            ins.then_inc(sems[e], 1)
            return body

        for e in self.ENGS:
            getattr(block, engobj[e])(body_for(e))


class Arena:
    def __init__(self, ap, nbytes):
        self.ap = ap
        self.n = nbytes
        self.off = 0

    def alloc(self, shape, dt):
        esz = 2 if dt == BF16 else 4
        n = 1
        for s in shape:
            n *= s
        nb = (n * esz + 31) // 32 * 32
        assert self.off + nb <= self.n, "arena overflow %d + %d > %d" % (self.off, nb, self.n)
        v = self.ap[:, self.off // 4:(self.off + nb) // 4]
        if dt != F32:
            v = v.bitcast(dt)
        v = v[:, 0:n]
        self.off += nb
        if len(shape) == 2:
            v = v.rearrange("p (a b) -> p a b", a=shape[0])
        elif len(shape) == 3:
            v = v.rearrange("p (a b c) -> p a b c", a=shape[0], b=shape[1])
        return v


def build_program():
    nc = bass.Bass("TRN2", target_bir_lowering=False)
    dr = lambda name, shape, dt, kind="ExternalInput": nc.dram_tensor(name, shape, dt, kind=kind).ap()
    x_d = dr("x", [S, D], F32)
    pos_d = dr("posT", [128, NBLK], I32)
    win_d = dr("w_in", [D, INW], F32)
    wpool_d = dr("w_pool", [4, 256, 256], F32)
    wout_d = dr("w_out", [D, D], F32)
    wup_d = dr("w_up", [D, 2 * DFF], F32)
    wdown_d = dr("w_down", [DFF, D], F32)
    pp_d = dr("pp", [128, NPP], F32)
    rb_d = dr("rb", [128, NRB], F32)
    cst_d = dr("cst", [128, NCS], F32)
    out_d = dr("out", [S, D], F32, kind="ExternalOutput")
    wups_d = dr("wup_bf16_scratch", [128, 8 * 2 * DFF], BF16, kind="Internal")
    wdns_d = dr("wdown_bf16_scratch", [128, 22 * 1024], BF16, kind="Internal")
    FC_GROUPS = [(0, 6), (6, 12), (12, 17), (17, 22)]
    wups_g = []
    _o = 0
    for (a_, b_) in FC_GROUPS:
        n_ = (b_ - a_) * 128
        wups_g.append(wups_d[:, _o:_o + 16 * n_].rearrange("p (h k c) -> p h k c", h=2, k=8))
        _o += 16 * n_
    wdns_v = wdns_d.rearrange("p (k c) -> p k c", k=22)

    win_v = win_d.rearrange("(k p) c -> p k c", p=128)
    wout_v = wout_d.rearrange("(k p) c -> p k c", p=128)
    wup_v = wup_d.rearrange("(k p) c -> p k c", p=128)
    wdown_v = wdown_d.rearrange("(k p) c -> p k c", p=128)
    wpool_v = wpool_d.rearrange("g (cc p) d -> p g cc d", p=128)

    R = Rec()
    stack = ExitStack()
    ARENA_BYTES = 212000
    arena_t = stack.enter_context(nc.sbuf_tensor("arena", [128, ARENA_BYTES // 4], F32))
    ps = stack.enter_context(nc.psum_tensor("ps", [128, 8, 512], F32))
    A = Arena(arena_t, ARENA_BYTES)
    PB = [Buf("ps%d" % i, excl=True) for i in range(8)]
    trp_bf = ps[:, 0, :].bitcast(BF16)

    def mm(out, lhsT, rhs, start, stop, r, w):
        R.add("pe", lambda e: e.matmul(out, lhsT, rhs, start=start, stop=stop), r=r, w=w)

    def tr(out, in_, ident, r, w):
        R.add("pe", lambda e: e.transpose(out, in_, ident), r=r, w=w)

    def act(out, in_, func, r, w, bias=None, scale=None, accum=None):
        kw = {}
        if bias is not None:
            kw["bias"] = bias
        if scale is not None:
            kw["scale"] = scale
        if accum is not None:
            kw["accum_out"] = accum
        R.add("act", lambda e: e.activation(out, in_, func, **kw), r=r, w=w)

    def tt(eng, out, in0, in1, op, r, w):
        R.add(eng, lambda e: e.tensor_tensor(out, in0, in1, op), r=r, w=w)

    def ts(eng, out, in0, s1, s2, op0, op1, r, w):
        if s2 is None:
            R.add(eng, lambda e: e.tensor_scalar(out, in0, s1, None, op0), r=r, w=w)
        else:
            R.add(eng, lambda e: e.tensor_scalar(out, in0, s1, s2, op0, op1), r=r, w=w)

    def stt(out, in0, scalar, in1, op0, op1, r, w):
        R.add("dve", lambda e: e.scalar_tensor_tensor(out, in0, scalar, in1, op0, op1), r=r, w=w)

    def cp(eng, out, in_, r, w):
        if eng == "act":
            R.add("act", lambda e: e.copy(out, in_), r=r, w=w)
        else:
            R.add(eng, lambda e: e.tensor_copy(out, in_), r=r, w=w)

    def mset(eng, ap, val, w):
        R.add(eng, lambda e: e.memset(ap, val), r=(), w=w)

    def dma(q, out, in_, r, w, sb, nodep=False):
        R.add(q, lambda e: e.dma_start(out=out, in_=in_), r=r, w=w, dma=sb, nodep=nodep)

    pp = A.alloc([NPP], F32)
    PPb = Buf("pp")
    bgh = A.alloc([16], F32)
    BGH = Buf("bgh")
    exps = A.alloc([8], F32)
    EXPS = Buf("exps")
    ident = A.alloc([128], BF16)
    IDENT = Buf("ident")
    neghalf = A.alloc([32], F32)
    NEGH = Buf("neghalf")
    negpi = A.alloc([8], F32)
    NEGPI = Buf("negpi")
    cwv = pp[:, PP_CW:PP_CW + 132].rearrange("p (j c) -> p j c", j=3)
    persist_mark = A.off

    win = A.alloc([8, INW], BF16)
    WIN_G, WIN_U, WIN_Q = Buf("win_g"), Buf("win_u"), Buf("win_q")
    wpool = A.alloc([4, 2, 256], BF16)
    WPOOL = Buf("wpool")
    wout = A.alloc([8, 1024], BF16)
    WOUT = Buf("wout")
    CST = Buf("cst")
    corrp = A.alloc([64], F32)
    rb1 = A.alloc([2176], F32)
    RB1 = Buf("rb1")
    g1 = rb1[:, 0:1024]
    gqk = rb1[:, 1024:2176]
    maskC = A.alloc([512], BF16)
    maskP = A.alloc([512], BF16)
    MASK = Buf("mask")
    posi = A.alloc([NBLK], I32)
    POSI = Buf("posi")
    posf = A.alloc([NBLK], F32)
    ANG = Buf("ang")
    cos_t = A.alloc([NBLK, 8], F32)
    sin_t = A.alloc([NBLK, 8], F32)
    ROPE = Buf("rope")
    onesz = A.alloc([2, 128], BF16)
    ONESZ = Buf("onesz")
    halo = A.alloc([8, 16], F32)
    HALO = [Buf("halo%d" % c) for c in range(8)]
    xn = [A.alloc([1024], F32) for _ in range(2)]
    XN = [Buf("xn%d" % i) for i in range(2)]
    xr = [A.alloc([1024], F32) for _ in range(1)]
    XR = [Buf("xr%d" % i) for i in range(1)]
    junk = A.alloc([1280], BF16)
    JUNK = Buf("junk")
    JUNKN = JUNK
    ss1 = [A.alloc([8], F32) for _ in range(2)]
    SS1 = [Buf("ss1_%d" % i) for i in range(2)]
    hn = [A.alloc([1024], BF16) for _ in range(2)]
    HN = [Buf("hn%d" % i) for i in range(2)]
    hT = [A.alloc([8, 256], BF16) for _ in range(1)]
    HT = [[Buf("hT%d_%d" % (s, j)) for j in range(2)] for s in range(1)]
    gT = A.alloc([16, 256], F32)
    GT = [Buf("gT%d" % c) for c in range(16)]
    u2 = [A.alloc([2, 272], F32) for _ in range(2)]
    U2 = [[Buf("u2_%d_%d" % (s, cc)) for cc in range(2)] for s in range(2)]
    U2H = [[Buf("u2h%d_%d" % (s, cc)) for cc in range(2)] for s in range(2)]
    sA = A.alloc([2, 272], F32)
    sB = A.alloc([2, 272], F32)
    SA, SB = [Buf("sA0"), Buf("sA1")], [Buf("sB0"), Buf("sB1")]
    pooledT = [A.alloc([2, 256], BF16) for _ in range(4)]
    POOLED = [Buf("pooled%d" % s) for s in range(4)]
    ag = A.alloc([8, 256], F32)
    AG = [Buf("ag%d" % c) for c in range(8)]
    sq = junk[:, 0:1152]
    SQ = JUNK
    st18 = A.alloc([3, 18], F32)
    ST18 = Buf("st18")
    qn = A.alloc([18, 64], F32)
    QN = Buf("qn")
    rtmp = A.alloc([4, 18, 8], F32)
    RT = [Buf("rtmp%d" % i) for i in range(4)]
    qkb = [A.alloc([18, 64], BF16) for _ in range(2)]
    QKB = [Buf("qkb%d" % i) for i in range(2)]
    kz = [A.alloc([4, 128], BF16) for _ in range(3)]
    KZ = [Buf("kz%d" % i) for i in range(3)]
    vz = [A.alloc([4, 128], BF16) for _ in range(3)]
    VZ = [Buf("vz%d" % i) for i in range(3)]
    qT = [A.alloc([8, 128], BF16) for _ in range(2)]
    QT = [Buf("qT%d" % i) for i in range(2)]
    kT = [A.alloc([4, 128], BF16) for _ in range(3)]
    KT = [Buf("kT%d" % i) for i in range(3)]
    scr_base = A.off
    PT = [[A.alloc([512], BF16) for _ in range(8)] for _ in range(1)]
    PTB = [[Buf("PT%d_%d" % (s, i)) for i in range(8)] for s in range(1)]
    rec = [A.alloc([4, 128], F32) for _ in range(2)]
    REC = [Buf("rec%d" % i) for i in range(2)]
    A2 = Arena(arena_t, A.off)
    A2.off = scr_base
    cstf = A2.alloc([NCS], F32)
    ang = A2.alloc([NBLK, 8], F32)
    angs = A2.alloc([NBLK, 8], F32)
    angk = A2.alloc([NBLK, 8], F32)
    angi = A2.alloc([NBLK, 8], I32)
    cT = [A.alloc([8, 256], BF16) for _ in range(2)]
    CT = [[Buf("cT%d_%d" % (s, j)) for j in range(2)] for s in range(2)]
    OUTB = [Buf("out%d" % i) for i in range(NBLK)]
    p1_end = A.off
    print('[arena] phase1 end', p1_end, 'of', ARENA_BYTES)

    dma("sp", pp, pp_d[:, :], [], [PPb], PPb)
    dma("sp", cstf, cst_d[:, :], [], [CST], CST)
    dma("sp", rb1, rb_d[:, 0:2176], [], [RB1], RB1)
    dma("sp", posi, pos_d[:, :], [], [POSI], POSI)
    WB = {kd: {"act": Buf("w%s_a" % kd), "dve": Buf("w%s_d" % kd)} for kd in "QUGPO"}
    WIN_Q_L = list(WB["Q"].values())
    WIN_U_L = list(WB["U"].values())
    WIN_G_L = list(WB["G"].values())
    WPOOL_L = list(WB["P"].values())
    WOUT_L = list(WB["O"].values())
    stage_jobs = []
    for k in range(8):
        for h in range(2):
            c0 = 1024 + h * 640
            stage_jobs.append((win[:, k, c0:c0 + 640], win_v[:, k, c0:c0 + 640], "Q", 640))
    for k in range(8):
        stage_jobs.append((win[:, k, 0:1024], win_v[:, k, 0:1024], "U", None))
    for k in range(8):
        for h in range(2):
            c0 = 2304 + h * 1024
            stage_jobs.append((win[:, k, c0:c0 + 1024], win_v[:, k, c0:c0 + 1024], "G", None))
    for h in range(2):
        stage_jobs.append((wpool[:, 2 * h:2 * h + 2], wpool_v[:, 2 * h:2 * h + 2], "P", "p (g c) d -> p g c d"))
    for k in range(8):
        stage_jobs.append((wout[:, k, :], wout_v[:, k, :], "O", None))

    stage_ctr = [0]

    def emit_stage_jobs(kinds):
        todo = [j_ for j_ in stage_jobs if j_[2] in kinds]
        for (dst, src, kd, rr) in todo:
            i = stage_ctr[0]
            stage_ctr[0] += 1
            sl_ = i % 4
            sv = gT[:, 4 * sl_:4 * sl_ + 4, :]
            if isinstance(rr, int):
                sv = sv.rearrange("p a b -> p (a b)")[:, 0:rr]
            elif rr:
                sv = sv.rearrange(rr, g=2)
            else:
                sv = sv.rearrange("p a b -> p (a b)")
            GS = GT[4 * sl_:4 * sl_ + 4]
            dma("sp", sv, src, [], GS, GT[4 * sl_])
            eng = "act" if i % 2 == 0 else "dve"
            if eng == "act":
                R.add("act", (lambda d_, s_: (lambda e: e.copy(d_, s_)))(dst, sv), r=GS, w=[WB[kd][eng]], nodep=True)
            else:
                R.add("dve", (lambda d_, s_: (lambda e: e.tensor_copy(d_, s_)))(dst, sv), r=GS, w=[WB[kd][eng]], nodep=True)

    cp("dve", corrp, cstf[:, CS_CORR:CS_CORR + 64], [CST], [BGH])
    cp("dve", ident, cstf[:, CS_ID:CS_ID + 128], [CST], [IDENT])
    cp("dve", maskC, cstf[:, CS_MC:CS_MC + 512], [CST], [MASK])
    cp("dve", maskP, cstf[:, CS_MP:CS_MP + 512], [CST], [MASK])
    corr = corrp.rearrange("p (g t) -> p g t", g=4)
    ts("dve", bgh, pp[:, PP_BG:PP_BG + 16], 0.5, None, ALU.mult, None, [PPb], [BGH])
    ts("dve", gqk[:, 0:1024], gqk[:, 0:1024], 0.125, None, ALU.mult, None, [RB1], [RB1])
    act(exps, pp[:, PP_SK:PP_SK + 8], AF.Exp, [PPb], [EXPS])
    mset("dve", neghalf, -0.5, [NEGH])
    mset("dve", negpi, -math.pi, [NEGPI])
    mset("pool", halo, 0.0, HALO)
    for i in range(3):
        mset("pool", kz[i], 0.0, [KZ[i]])
        mset("pool", vz[i], 0.0, [VZ[i]])
    mset("pool", onesz, 0.0, [ONESZ])
    mset("pool", onesz[:, 0, 0:64], 1.0, [ONESZ])
    mset("pool", onesz[:, 1, 64:128], 1.0, [ONESZ])
    inv_freq = [float(ROPE_THETA ** (-(2.0 * i) / 16.0)) for i in range(8)]
    TWO_PI = 2.0 * math.pi

    def range_reduce_sin(dst, shift):
        ts("dve", angs, ang, shift, None, ALU.add, None, [ANG], [ANG])
        ts("dve", angi, angs, 1.0 / TWO_PI, None, ALU.mult, None, [ANG], [ANG])
        cp("dve", angk, angi, [ANG], [ANG])
        stt(angs, angk, -TWO_PI, angs, ALU.mult, ALU.add, [ANG], [ANG])
        ts("dve", angk, angs, math.pi, -TWO_PI, ALU.is_gt, ALU.mult, [ANG], [ANG])
        tt("dve", angs, angs, angk, ALU.add, [ANG], [ANG])
        ts("dve", angk, angs, -math.pi, TWO_PI, ALU.is_lt, ALU.mult, [ANG], [ANG])
        tt("dve", angs, angs, angk, ALU.add, [ANG], [ANG])
        ts("dve", angs, angs, math.pi, -math.pi, ALU.min, ALU.max, [ANG], [ANG])
        act(dst, angs, AF.Sin, [ANG], [ROPE])

    def rope_setup():
        cp("dve", posf, posi, [POSI], [ANG])
        for i in range(8):
            ts("dve", ang[:, :, i], posf, inv_freq[i], None, ALU.mult, None, [ANG], [ANG])
        range_reduce_sin(sin_t, 0.0)
        range_reduce_sin(cos_t, math.pi / 2)

    FM = (1, 2)
    SC = (6, 7)

    def norm_pre(gb, src_ap, src_deps, gain, GAINB, xnring, XNring, skip_load=False):
        s3 = gb % len(xnring)
        s2 = gb % 2
        s1 = gb % len(hn)
        if not skip_load:
            dma("sp", xnring[s3], src_ap, src_deps, [XNring[s3]], XNring[s3])
        act(hn[s1], xnring[s3], AF.Square, [XNring[s3]], [HN[s1], SS1[s2]], accum=ss1[s2][:, 0:1])
        ts("dve", ss1[s2][:, 1:2], ss1[s2][:, 0:1], 1.0 / D, EPS, ALU.mult, ALU.add, [SS1[s2]], [SS1[s2]])
        tt("pool", ss1[s2][:, 2:3], ss1[s2][:, 1:2], neghalf[:, 0:1], ALU.pow, [SS1[s2], NEGH], [SS1[s2]])
        stt(hn[s1], xnring[s3], ss1[s2][:, 2:3], gain, ALU.mult, ALU.mult, [XNring[s3], SS1[s2], GAINB], [HN[s1]])

    def norm_tr(gb, hT_dst, HT_dstB):
        s1 = gb % len(hn)
        for k in range(8):
            tr(trp_bf[:, k * 128:(k + 1) * 128], hn[s1][:, k * 128:(k + 1) * 128], ident, [HN[s1], IDENT], [PB[0]])
        cp("act", hT_dst, trp_bf.rearrange("p (k n) -> p k n", k=8), [PB[0]], [HT_dstB])

    def norm_block(gb, src_ap, src_deps, gain, GAINB, hT_dst, HT_dstB, xnring, XNring):
        norm_pre(gb, src_ap, src_deps, gain, GAINB, xnring, XNring)
        norm_tr(gb, hT_dst, HT_dstB)

    NT1 = S // 256
    hs = 0
    HTr = [HT[hs][0], HT[hs][1]]
    ps4 = lambda bk: ps[:, bk, :].rearrange("p (a b) -> p a b", a=4)

    def p1_norm_pre(ti, j):
        gb = 2 * ti + j
        norm_pre(gb, x_d[gb * 128:(gb + 1) * 128, :], [], g1, RB1, xn, XN)

    def p1_norm_tr(ti, j):
        norm_tr(2 * ti + j, hT[hs][:, :, j * 128:(j + 1) * 128], HT[hs][j])

    def P_mm(ti, j):
        gb = 2 * ti + j
        tok = slice(j * 128, (j + 1) * 128)
        for k in range(8):
            lh = hT[hs][:, k, tok]
            mm(ps[:, 3, :], lh, win[:, k, 1024:1536], k == 0, k == 7, [HT[hs][j]] + WIN_Q_L, [PB[3]])
            mm(ps[:, 4, :], lh, win[:, k, 1536:2048], k == 0, k == 7, [HT[hs][j]] + WIN_Q_L, [PB[4]])
            mm(ps[:, 5, 0:256], lh, win[:, k, 2048:2304], k == 0, k == 7, [HT[hs][j]] + WIN_Q_L, [PB[5]])

    def P_chain(ti, j):
        gb = 2 * ti + j
        sl = gb % 3
        qb = gb % 2
        qn2 = qn.rearrange("p h d -> p (h d)")
        tt("dve", qn2[:, 0:1024].rearrange("p (a b) -> p a b", a=2), ps[:, 3:5, :],
           gqk[:, 0:1024].rearrange("p (a b) -> p a b", a=2), ALU.mult, [PB[3], PB[4], RB1], [QN])
        tt("dve", qn2[:, 1024:1152], ps[:, 5, 0:128], gqk[:, 1024:1152], ALU.mult, [PB[5], RB1], [QN])
        act(sq[:, 0:1024].rearrange("p (a b) -> p a b", a=2), ps[:, 3:5, :], AF.Square, [PB[3], PB[4]], [SQ])
        act(sq[:, 1024:1152], ps[:, 5, 0:128], AF.Square, [PB[5]], [SQ])
        vz4 = vz[sl].rearrange("p (kv par) d -> p kv par d", kv=2)
        vsrc = ps[:, 5, 128:256].rearrange("p (kv d) -> p kv d", kv=2)
        cp("act", vz4[:, :, 0, 0:64], vsrc, [PB[5]], [VZ[sl]])
        cp("act", vz4[:, :, 1, 64:128], vsrc, [PB[5]], [VZ[sl]])
        R.add("dve", lambda e: e.tensor_reduce(st18[:, 0, :], sq.rearrange("p (h d) -> p h d", d=64), AX.X, ALU.add),
              r=[SQ], w=[ST18])
        cosb = cos_t[:, gb, :].unsqueeze(1).to_broadcast([128, 18, 8])
        sinb = sin_t[:, gb, :].unsqueeze(1).to_broadcast([128, 18, 8])
        x1 = qn[:, :, 0:8]
        x2 = qn[:, :, 8:16]
        tt("dve", rtmp[:, 0], x1, cosb, ALU.mult, [QN, ROPE], [RT[0]])
        tt("dve", rtmp[:, 1], x2, sinb, ALU.mult, [QN, ROPE], [RT[1]])
        ts("dve", st18[:, 1, :], st18[:, 0, :], 1.0 / 64, EPS, ALU.mult, ALU.add, [ST18], [ST18])
        tt("pool", st18[:, 2, :], st18[:, 1, :], neghalf[:, 0:18], ALU.pow, [ST18, NEGH], [ST18])
        tt("dve", rtmp[:, 2], x2, cosb, ALU.mult, [QN, ROPE], [RT[2]])
        tt("dve", rtmp[:, 3], x1, sinb, ALU.mult, [QN, ROPE], [RT[3]])
        tt("dve", qn[:, :, 0:8], rtmp[:, 0], rtmp[:, 1], ALU.subtract, [RT[0], RT[1]], [QN])
        tt("dve", qn[:, :, 8:16], rtmp[:, 2], rtmp[:, 3], ALU.add, [RT[2], RT[3]], [QN])
        tt("dve", qkb[qb], qn, st18[:, 2, :].unsqueeze(2).to_broadcast([128, 18, 64]), ALU.mult,
           [QN, ST18], [QKB[qb]])
        kz4 = kz[sl].rearrange("p (kv par) d -> p kv par d", kv=2)
        cp("pool", kz4[:, :, 0, 0:64], qkb[qb][:, 16:18, :], [QKB[qb]], [KZ[sl]])
        cp("pool", kz4[:, :, 1, 64:128], qkb[qb][:, 16:18, :], [QKB[qb]], [KZ[sl]])

    def P_tr(ti, j):
        gb = 2 * ti + j
        sl = gb % 3
        qb = gb % 2
        qkb2 = qkb[qb].rearrange("p h d -> p (h d)")
        for c in range(8):
            tr(trp_bf[:, c * 128:(c + 1) * 128], qkb2[:, c * 128:(c + 1) * 128], ident, [QKB[qb], IDENT], [PB[0]])
        cp("act", qT[qb], trp_bf.rearrange("p (k n) -> p k n", k=8), [PB[0]], [QT[qb]])
        for i in range(4):
            tr(trp_bf[:, i * 128:(i + 1) * 128], kz[sl][:, i, :], ident, [KZ[sl], IDENT], [PB[0]])
        cp("act", kT[sl], trp_bf[:, 0:512].rearrange("p (k n) -> p k n", k=4), [PB[0]], [KT[sl]])

    def gates(ti, c, banks=None):
        bank = (banks or SC)[c % 2]
        for k in range(8):
            mm(ps[:, bank, 0:256], win[:, k, 2304 + c * 128:2304 + (c + 1) * 128], hT[hs][:, k, :],
               k == 0, k == 7, WIN_G_L + HTr, [PB[bank]])
        act(gT[:, c, :], ps[:, bank, 0:256], AF.Tanh, [PB[bank], BGH], [GT[c]],
            bias=bgh[:, c:c + 1], scale=0.5)

    UBANK = ((1, 2), (3, 4))

    def C_u(ti, g):
        us = g % 2
        U = u2[us]
        for cc in range(2):
            c = 2 * g + cc
            bank = UBANK[us][cc]
            for k in range(8):
                mm(ps[:, bank, 0:256], win[:, k, c * 128:(c + 1) * 128], hT[hs][:, k, :],
                   k == 0, k == 7, WIN_U_L + HTr, [PB[bank]])
            cp("pool", U[:, cc, 0:16], halo[:, c, :], [HALO[c]], [U2H[us][cc]])
            cp("act", U[:, cc, 16:272], ps[:, bank, 0:256], [PB[bank]], [U2[us][cc]])
            cp("pool", halo[:, c, :], U[:, cc, 256:272], [U2[us][cc]], [HALO[c]])

    def C_pool(ti, g):
        us = g % 2
        U = u2[us]
        Ur = [[U2[us][cc], U2H[us][cc]] for cc in range(2)]
        for cc in range(2):
            tt("dve", sA[:, cc, 1:272], U[:, cc, 1:272], U[:, cc, 0:271], ALU.add, Ur[cc], [SA[cc]])
        cur, CUR = sA, SA
        if g >= 1:
            for cc in range(2):
                tt("dve", sB[:, cc, 3:272], sA[:, cc, 3:272], sA[:, cc, 1:270], ALU.add, [SA[cc]], [SB[cc]])
            cur, CUR = sB, SB
        if g >= 2:
            for cc in range(2):
                tt("dve", sA[:, cc, 7:272], sB[:, cc, 7:272], sB[:, cc, 3:268], ALU.add, [SB[cc]], [SA[cc]])
            cur, CUR = sA, SA
        if g >= 3:
            for cc in range(2):
                tt("dve", sB[:, cc, 15:272], sA[:, cc, 15:272], sA[:, cc, 7:264], ALU.add, [SA[cc]], [SB[cc]])
            cur, CUR = sB, SB
        if ti == 0:
            for cc in range(2):
                tt("dve", cur[:, cc, 16:32], cur[:, cc, 16:32], corr[:, g, :], ALU.mult, [CUR[cc], BGH], [CUR[cc]])
        pl = pooledT[g]
        for cc in range(2):
            stt(pl[:, cc, :], cur[:, cc, 16:272], 1.0 / (2 ** (g + 1)), U[:, cc, 16:272],
                ALU.mult, ALU.subtract, [CUR[cc]] + Ur[cc], [POOLED[g]])

    def C_map(ti, g):
        pl = pooledT[g]
        for dc in range(2):
            ch = 2 * g + dc
            bk = (5, 1, 2, 3, 4, 6, 7, 5)[ch]
            hf = 1 if ch == 7 else 0
            o = ps[:, bk, hf * 256:(hf + 1) * 256]
            for cc in range(2):
                mm(o, wpool[:, g, cc, dc * 128:(dc + 1) * 128], pl[:, cc, :],
                   cc == 0, cc == 1, WPOOL_L + [POOLED[g]], [PB[bk]])
        for dc in range(2):
            ch = 2 * g + dc
            bk = (5, 1, 2, 3, 4, 6, 7, 5)[ch]
            hf = 1 if ch == 7 else 0
            o = ps[:, bk, hf * 256:(hf + 1) * 256]
            stt(ag[:, ch, :], o, pp[:, PP_PS + ch:PP_PS + ch + 1], gT[:, ch, :],
                ALU.mult, ALU.mult, [PB[bk], PPb, GT[ch]], [AG[ch]])

    def S_core(ti, j):
        gb = 2 * ti + j
        sl = gb % 3
        psl = (gb - 1) % 3
        qb = gb % 2
        kbs = [(psl, maskP), (sl, maskC)] if gb > 0 else [(sl, maskC)]
        cnt = 0
        for kv in range(2):
            for par in range(2):
                for kbi, (ksl, mk) in enumerate(kbs):
                    bank = SC[cnt % 2]
                    cnt += 1
                    u = (kv * 2 + par) * 2 + kbi
                    mm(ps4(bank), kT[ksl][:, kv * 2 + par, :],
                       qT[qb][:, kv * 4:(kv + 1) * 4, :], True, False, [KT[ksl], QT[qb]], [PB[bank]])
                    mm(ps[:, bank, :], ident, mk, False, True, [IDENT, MASK], [PB[bank]])
                    act(PT[0][u], ps[:, bank, :], AF.Exp, [PB[bank]], [PTB[0][u]])

    PVB = (((1, 2), (3, 4)), ((1, 2), (6, 7)))

    def S_pv(ti, j):
        gb = 2 * ti + j
        sl = gb % 3
        psl = (gb - 1) % 3
        kbs = [(psl, maskP), (sl, maskC)] if gb > 0 else [(sl, maskC)]
        for kv in range(2):
            pvb, denb = PVB[j][kv]
            n = 2 * len(kbs)
            i = 0
            for par in range(2):
                for kbi, (ksl, mk) in enumerate(kbs):
                    u = (kv * 2 + par) * 2 + kbi
                    mm(ps[:, pvb, :], vz[ksl][:, kv * 2 + par, :], PT[0][u], i == 0, i == n - 1,
                       [VZ[ksl], PTB[0][u]], [PB[pvb]])
                    i += 1
            i = 0
            for par in range(2):
                for kbi, (ksl, mk) in enumerate(kbs):
                    u = (kv * 2 + par) * 2 + kbi
                    mm(ps[:, denb, :], onesz[:, par, :], PT[0][u], i == 0, i == n - 1,
                       [ONESZ, PTB[0][u]], [PB[denb]])
                    i += 1

    def N_chain(ti, j):
        N_a(ti, j)
        N_b(ti, j)

    def N_a(ti, j):
        banks = PVB[j]
        for kv in range(2):
            ch = slice(kv * 4, (kv + 1) * 4)
            tt("dve", rec[kv], ps4(banks[kv][1]), exps[:, ch].unsqueeze(2).to_broadcast([128, 4, 128]), ALU.add,
               [PB[banks[kv][1]], EXPS], [REC[kv]])
        for kv in range(2):
            rk = rec[kv]
            R.add("dve", (lambda rk: (lambda e: e.reciprocal(rk, rk)))(rk), r=[REC[kv]], w=[REC[kv]])

    def N_b(ti, j):
        cs = ti % 2
        tok = slice(j * 128, (j + 1) * 128)
        banks = PVB[j]
        for kv in range(2):
            tt("dve", rec[kv], ps4(banks[kv][0]), rec[kv], ALU.mult, [PB[banks[kv][0]], REC[kv]], [REC[kv]])
        for kv in range(2):
            gch = slice(8 + kv * 4, 8 + (kv + 1) * 4)
            tt("pool", rec[kv], rec[kv], gT[:, gch, tok], ALU.mult, [REC[kv]] + GT[gch], [REC[kv]])
        for kv in range(2):
            ch = slice(kv * 4, (kv + 1) * 4)
            tt("pool", cT[cs][:, ch, tok], rec[kv], ag[:, ch, tok], ALU.add, [REC[kv]] + AG[ch], [CT[cs][j]])

    def E_mm(ti, j):
        gb = 2 * ti + j
        cs = ti % 2
        rs = 0
        tok = slice(j * 128, (j + 1) * 128)
        b0, b1 = (3, 4) if j == 0 else (1, 2)
        dma("sp", xr[rs], x_d[gb * 128:(gb + 1) * 128, :], [], [XR[rs]], XR[rs])
        for k in range(8):
            lh = cT[cs][:, k, tok]
            mm(ps[:, b0, :], lh, wout[:, k, 0:512], k == 0, k == 7, [CT[cs][j]] + WOUT_L, [PB[b0]])
            mm(ps[:, b1, :], lh, wout[:, k, 512:1024], k == 0, k == 7, [CT[cs][j]] + WOUT_L, [PB[b1]])

    def E_add(ti, j):
        gb = 2 * ti + j
        rs = 0
        b0, b1 = (3, 4) if j == 0 else (1, 2)
        xr3 = xr[rs].rearrange("p (a b) -> p a b", a=2)
        tt("dve", xr3, ps[:, b0:b1 + 1, :], xr3, ALU.add, [PB[b0], PB[b1], XR[rs]], [XR[rs]])
        dma("sp", out_d[gb * 128:(gb + 1) * 128, :], xr[rs], [XR[rs]], [OUTB[gb]], XR[rs])

    WUPS, WDNS = Buf("wups"), Buf("wdns")
    conv_jobs = []
    for gi_, (a_, b_) in enumerate(FC_GROUPS):
        for h_ in range(2):
            for k in range(8):
                conv_jobs.append((wups_g[gi_][:, h_, k, :],
                                  wup_v[:, k, h_ * DFF + a_ * 128:h_ * DFF + b_ * 128], WUPS))
    for kk in range(22):
        conv_jobs.append((wdns_v[:, kk, :], wdown_v[:, kk, :], WDNS))

    def emit_conv(n):
        for _ in range(n):
            if conv_jobs:
                o, i, B_ = conv_jobs.pop(0)
                dma("pool", o, i, [], [B_], B_, nodep=True)

    for j in range(2):
        p1_norm_pre(0, j)
        p1_norm_tr(0, j)
    emit_stage_jobs("Q")
    rope_setup()
    P_mm(0, 0)
    P_chain(0, 0)
    emit_stage_jobs("U")
    for ti in range(NT1):
        P_mm(ti, 1)
        P_chain(ti, 1)
        C_u(ti, 0)
        C_u(ti, 1)
        if ti == 0:
            emit_stage_jobs("GPO")
        if ti == 0:
            for c in range(0, 4):
                gates(ti, c)
            C_pool(ti, 0)
            C_u(ti, 2)
            for c in range(4, 8):
                gates(ti, c)
            P_tr(ti, 0)
            ts("pool", gT[:, 0:8, :], gT[:, 0:8, :], 0.5, 0.5, ALU.mult, ALU.add, GT[0:8], GT[0:8])
            C_pool(ti, 1)
            C_u(ti, 3)
            for c in range(8, 16):
                gates(ti, c)
        else:
            for c in range(8, 12):
                gates(ti, c)
            C_pool(ti, 0)
            C_u(ti, 2)
            for c in range(12, 16):
                gates(ti, c)
            P_tr(ti, 0)
            C_pool(ti, 1)
            C_u(ti, 3)
        ts("pool", gT[:, 8:16, :], gT[:, 8:16, :], 0.5, 0.5, ALU.mult, ALU.add, GT[8:16], GT[8:16])
        P_tr(ti, 1)
        C_pool(ti, 2)
        C_pool(ti, 3)
        for g in range(4):
            C_map(ti, g)
        nxt = ti + 1 < NT1
        if nxt:
            p1_norm_pre(ti + 1, 0)
            p1_norm_pre(ti + 1, 1)
        S_core(ti, 0)
        S_pv(ti, 0)
        if nxt:
            p1_norm_tr(ti + 1, 0)
        S_core(ti, 1)
        N_chain(ti, 0)
        S_pv(ti, 1)
        if nxt:
            p1_norm_tr(ti + 1, 1)
        E_mm(ti, 0)
        if nxt:
            for c in range(0, 8):
                gates(ti + 1, c, banks=(5, 0))
        N_a(ti, 1)
        E_add(ti, 0)
        if nxt:
            P_mm(ti + 1, 0)
        N_b(ti, 1)
        if nxt:
            ts("pool", gT[:, 0:8, :], gT[:, 0:8, :], 0.5, 0.5, ALU.mult, ALU.add, GT[0:8], GT[0:8])
            P_chain(ti + 1, 0)
        E_mm(ti, 1)
        E_add(ti, 1)
        emit_conv(6)
    emit_conv(100)

    R.barrier()
    A.off = persist_mark
    fc_groups = FC_GROUPS
    wupg = [A.alloc([2, 8, (b_ - a_) * 128], BF16) for (a_, b_) in fc_groups]
    WUPG = [Buf("wup%d" % i) for i in range(4)]
    fc2grp = {}
    for gi, (a, b) in enumerate(fc_groups):
        for fc in range(a, b):
            fc2grp[fc] = gi
    wdown = A.alloc([22, 1024], BF16)
    WDN = [Buf("wdn0"), Buf("wdn1")]
    g2 = A.alloc([1024], F32)
    G2 = Buf("g2")
    chalo = A.alloc([44, 2], F32)
    CH = [Buf("chalo%d" % i) for i in range(44)]
    xn2 = [A.alloc([1024], F32) for _ in range(2)]
    XN2 = [Buf("xn2_%d" % i) for i in range(2)]
    xr2 = [A.alloc([1024], F32) for _ in range(2)]
    XR2 = [Buf("xr2_%d" % i) for i in range(2)]
    ss1 = [A.alloc([8], F32) for _ in range(2)]
    hn = [A.alloc([1024], BF16) for _ in range(1)]
    hT2 = A.alloc([8, 512], BF16)
    HT2 = [Buf("hT2_%d" % j) for j in range(4)]
    yb = [[A.alloc([512], F32) for _ in range(2)] for _ in range(2)]
    YB = [[Buf("y%d_%d" % (h, s)) for s in range(2)] for h in range(2)]
    ub = [[A.alloc([514], F32) for _ in range(2)] for _ in range(2)]
    UH = [[Buf("uh%d_%d" % (h, s)) for s in range(2)] for h in range(2)]
    UM = [[Buf("um%d_%d" % (h, s)) for s in range(2)] for h in range(2)]
    sg = [A.alloc([512], F32) for _ in range(2)]
    SG = [Buf("sg%d" % s) for s in range(2)]
    hid = A.alloc([22, 512], BF16)
    HID = [Buf("hid%d" % fc) for fc in range(22)]
    print('[arena] phase2 end', A.off)

    dma("sp", g2, rb_d[:, RB_G2:RB_G2 + 1024], [], [G2], G2)
    mset("pool", chalo, 0.0, CH)

    def p2_wload(gi):
        dma("sp", wupg[gi].rearrange("p h k c -> p (h k c)"), wups_g[gi].rearrange("p h k c -> p (h k c)"),
            [WUPS], [WUPG[gi]], WUPG[gi])

    def p2_xload(j):
        dma("sp", xn2[j % 2], out_d[j * 128:(j + 1) * 128, :], [OUTB[j]], [XN2[j % 2]], XN2[j % 2])

    GB = ((1, 2, 3), (4, 5, 6))
    NT2 = S // 512

    def p2_norm_pre(ti, j, skip_load=False):
        gb = 4 * ti + j
        norm_pre(gb, out_d[gb * 128:(gb + 1) * 128, :], [OUTB[gb]], g2, G2, xn2, XN2, skip_load=skip_load)

    def p2_loads(ti, j):
        gb = 4 * ti + j
        rs = gb % 2
        dma("sp", xr2[rs], out_d[gb * 128:(gb + 1) * 128, :], [OUTB[gb]], [XR2[rs]], XR2[rs])
        if ti + 1 < NT2:
            gn = 4 * (ti + 1) + j
            dma("sp", xn2[gn % 2], out_d[gn * 128:(gn + 1) * 128, :], [OUTB[gn]], [XN2[gn % 2]], XN2[gn % 2])

    def p2_norm_tr(ti, j):
        norm_tr(4 * ti + j, hT2[:, :, j * 128:(j + 1) * 128], HT2[j])

    def ffn_front(fc):
        rg = fc % 2
        for half in range(2):
            chn = half * 22 + fc
            cp("pool", ub[half][rg][:, 0:2], chalo[:, chn, :], [CH[chn]], [UH[half][rg]])
        for half in range(2):
            chn = half * 22 + fc
            col0 = half * DFF + fc * 128
            bank = GB[half][fc % 3]
            for k in range(8):
                gi_ = fc2grp[fc]
                lc = (fc - fc_groups[gi_][0]) * 128
                mm(ps[:, bank, :], wupg[gi_][:, half, k, lc:lc + 128], hT2[:, k, :], k == 0, k == 7,
                   [WUPG[gi_]] + HT2, [PB[bank]])
        for half in range(2):
            chn = half * 22 + fc
            bank = GB[half][fc % 3]
            act(yb[half][rg], ps[:, bank, :], AF.Identity, [PB[bank], PPb], [YB[half][rg]],
                bias=pp[:, PP_CB + chn:PP_CB + chn + 1], scale=cwv[:, 2, chn:chn + 1])
            cp("act", ub[half][rg][:, 2:514], ps[:, bank, :], [PB[bank]], [UM[half][rg]])
        for half in range(2):
            chn = half * 22 + fc
            cp("pool", chalo[:, chn, :], ub[half][rg][:, 512:514], [UM[half][rg]], [CH[chn]])
        for tap, off in ((1, 1), (0, 0)):
            for half in range(2):
                chn = half * 22 + fc
                y = yb[half][rg]
                stt(y, ub[half][rg][:, off:off + 512], cwv[:, tap, chn:chn + 1], y, ALU.mult, ALU.add,
                    [UH[half][rg], UM[half][rg], PPb, YB[half][rg]], [YB[half][rg]])

    def ffn_silu(fc):
        rg = fc % 2
        act(sg[rg], yb[0][rg], AF.Silu, [YB[0][rg]], [SG[rg]])

    def ffn_mult(fc):
        rg = fc % 2
        tt("pool", hid[:, fc, :], sg[rg], yb[1][rg], ALU.mult, [SG[rg], YB[1][rg]], [HID[fc]])

    p2_xload(0)
    p2_xload(1)
    p2_wload(0)
    for j in range(4):
        p2_norm_pre(0, j, skip_load=True)
        p2_norm_tr(0, j)
        if j + 2 < 4:
            p2_xload(j + 2)
        if j == 1:
            p2_wload(1)
            dma("sp", wdown[:, 0:11, :], wdns_v[:, 0:11, :], [WDNS], [WDN[0]], WDN[0])
    p2_wload(2)
    p2_wload(3)
    dma("sp", wdown[:, 11:22, :], wdns_v[:, 11:22, :], [WDNS], [WDN[1]], WDN[1])
    for ti in range(NT2):
        for fc in range(22):
            ffn_front(fc)
            if fc > 0:
                ffn_silu(fc - 1)
                ffn_mult(fc - 1)
        ffn_silu(21)
        ffn_mult(21)
        p2_loads(ti, 0)
        for j in range(4):
            gb = 4 * ti + j
            rs = gb % 2
            tok = slice(j * 128, (j + 1) * 128)
            if ti + 1 < NT2:
                p2_norm_pre(ti + 1, j, skip_load=True)
            g0 = 1 if j % 2 == 0 else 4
            for fc in range(22):
                lh = hid[:, fc, tok]
                mm(ps[:, g0, :], lh, wdown[:, fc, 0:512], fc == 0, fc == 21, [HID[fc], WDN[fc // 11]], [PB[g0]])
                mm(ps[:, g0 + 1, :], lh, wdown[:, fc, 512:1024], fc == 0, fc == 21, [HID[fc], WDN[fc // 11]], [PB[g0 + 1]])
            if ti + 1 < NT2:
                p2_norm_tr(ti + 1, j)
            if j + 1 < 4:
                p2_loads(ti, j + 1)
            xr3 = xr2[rs].rearrange("p (a b) -> p a b", a=2)
            tt("dve", xr3, ps[:, g0:g0 + 2, :], xr3, ALU.add, [PB[g0], PB[g0 + 1], XR2[rs]], [XR2[rs]])
            dma("sp", out_d[gb * 128:(gb + 1) * 128, :], xr2[rs], [XR2[rs]], [OUTB[gb]], XR2[rs])

    R.finish()
    R.emit(nc, stack)
    stack.close()
    return nc


def _host_layout(inputs):
    f = lambda a: np.ascontiguousarray(np.asarray(a), dtype=np.float32)
    b_gate = f(inputs["b_gate"])[0]
    pool_scale = f(inputs["pool_scale"])[0]
    sinks = f(inputs["sinks"])[0]
    conv_w = f(inputs["conv_w"])[0]
    conv_b = f(inputs["conv_b"])[0]
    pp = np.zeros((128, NPP), np.float32)
    pp[:, PP_BG:PP_BG + 16] = b_gate.reshape(16, 128).T
    pp[:, PP_PS:PP_PS + 8] = pool_scale.reshape(8, 128).T
    pp[:, PP_SK:PP_SK + 8] = np.repeat(sinks.reshape(8, 2).T, 64, axis=0)
    pp[:, PP_CW:PP_CW + 132] = conv_w.reshape(3, 44, 128).transpose(2, 0, 1).reshape(128, 132)
    pp[:, PP_CB:PP_CB + 44] = conv_b.reshape(44, 128).T
    rb = np.zeros((128, NRB), np.float32)
    rb[:, RB_G1:RB_G1 + 1024] = f(inputs["attn_norm"])[0][None, :]
    rb[:, RB_GQ:RB_GQ + 1024] = np.tile(f(inputs["q_norm"])[0], 16)[None, :]
    rb[:, RB_GQ + 1024:RB_GQ + 1152] = np.tile(f(inputs["k_norm"])[0], 2)[None, :]
    rb[:, RB_G2:RB_G2 + 1024] = f(inputs["ffn_norm"])[0][None, :]
    cst = np.zeros((128, NCS), np.float32)
    cst[:, CS_ID:CS_ID + 128] = np.eye(128, dtype=np.float32)
    jj = np.arange(128)[:, None]
    ii = np.arange(128)[None, :]
    mc = np.where(jj <= ii, 0.0, NEG).astype(np.float32)
    mp = np.where(jj > ii, 0.0, NEG).astype(np.float32)
    cst[:, CS_MC:CS_MC + 512] = np.tile(mc, (1, 4))
    cst[:, CS_MP:CS_MP + 512] = np.tile(mp, (1, 4))
    corr = np.ones((4, 16), np.float32)
    for g in range(4):
        w = 2 ** (g + 1)
        for t in range(16):
            corr[g, t] = w / min(t + 1, w)
    cst[:, CS_CORR:CS_CORR + 64] = corr.reshape(1, 64)
    return pp, rb, cst


_NC_CACHE = {}


def kernel(**inputs):
    x = np.ascontiguousarray(np.asarray(inputs["x"]), dtype=np.float32)
    positions = np.ascontiguousarray(np.asarray(inputs["positions"]), dtype=np.int32)
    pp, rb, cst = _host_layout(inputs)
    f = lambda a: np.ascontiguousarray(np.asarray(a), dtype=np.float32)
    w_in = f(inputs["w_in"])[0]
    w_pool = f(inputs["w_pool"])[0]
    w_out = f(inputs["w_out"])[0]
    w_up = f(inputs["w_up"])[0]
    w_down = f(inputs["w_down"])[0]
    if "nc" not in _NC_CACHE:
        _NC_CACHE["nc"] = build_program()
    nc = _NC_CACHE["nc"]
    n = x.shape[0]
    in_maps = []
    for i in range(n):
        in_maps.append({
            "x": x[i],
            "posT": np.ascontiguousarray(positions[i].reshape(NBLK, 128).T),
            "w_in": w_in, "w_pool": w_pool, "w_out": w_out, "w_up": w_up, "w_down": w_down,
            "pp": pp, "rb": rb, "cst": cst,
        })
    res = run_bass_kernel_spmd(nc, in_maps, core_ids=list(range(n)))
    out = np.stack([np.asarray(r["out"]) for r in res.results], axis=0)
    return out.astype(np.float32)
```

```python
import math
from contextlib import ExitStack

import numpy as np
import concourse.bass as bass
import concourse.mybir as mybir
from concourse.bass_utils import run_bass_kernel_spmd

F32 = mybir.dt.float32
BF16 = mybir.dt.bfloat16
I32 = mybir.dt.int32
ALU = mybir.AluOpType
AF = mybir.ActivationFunctionType
AX = mybir.AxisListType

S = 4096
D = 1024
NBLK = S // 128
INW = 4352
DFF = 2816
EPS = 1e-6
ROPE_THETA = 500000.0
NEG = -30000.0

PP_BG = 0
PP_PS = 16
PP_SK = 24
PP_CW = 32
PP_CB = 164
NPP = 208
RB_G1 = 0
RB_GQ = 1024
RB_G2 = 2176
NRB = 3200
CS_ID = 0
CS_MC = 128
CS_MP = 640
CS_CORR = 1152
NCS = 1216


import os
STRICT_WAR = False


class Buf:
    __slots__ = ("name", "w", "r", "sem", "cnt", "excl")

    def __init__(self, name, excl=False):
        self.name = name
        self.excl = excl
        self.w = None
        self.r = {}
        self.sem = None
        self.cnt = 0


class Ins:
    __slots__ = ("eng", "fn", "deps", "idx", "inc", "dma", "semv", "val")


class Rec:
    ENGS = ("pe", "act", "dve", "pool", "sp")
    INORDER_FREE = ("pe", "sp")

    def __init__(self):
        self.streams = {e: [] for e in self.ENGS}
        self.bar = {e: [] for e in self.ENGS}
        self.dmas = []
        self.dma_bufs = []

    def add(self, eng, fn, r=(), w=(), dma=None, nodep=False):
        I = Ins()
        I.eng = eng
        I.fn = fn
        I.dma = dma
        I.inc = False
        I.val = 0
        I.semv = None
        deps = {}
        for b in r:
            if b.w is not None:
                deps[id(b.w)] = b.w
            if b.excl:
                for q in b.r.values():
                    if q.eng != eng:
                        deps[id(q)] = q
        for b in w:
            if nodep:
                continue
            if b.w is not None:
                if not (b.w.dma is None and b.w.eng == eng and dma is None and not STRICT_WAR):
                    deps[id(b.w)] = b.w
            for q in b.r.values():
                if q.dma is None and q.eng == eng and dma is None and not STRICT_WAR:
                    continue
                deps[id(q)] = q
        for q in self.bar[eng]:
            deps[id(q)] = q
        self.bar[eng] = []
        if dma is not None:
            if dma.sem is None:
                self.dma_bufs.append(dma)
                dma.sem = True
            dma.cnt += 16
            I.semv = (dma, dma.cnt)
            self.dmas.append(I)
        I.deps = list(deps.values())
        for b in r:
            key = eng if dma is None else ("d", len(b.r))
            b.r[key] = I
        for b in w:
            b.w = I
            b.r = {}
        st = self.streams[eng]
        I.idx = len(st)
        st.append(I)
        return I

    def barrier(self):
        last = []
        for e in self.ENGS:
            if e == "sp":
                continue
            if self.streams[e]:
                last.append(self.streams[e][-1])
        lat = {}
        for I in self.dmas:
            lat[id(I.semv[0])] = I
        last.extend(lat.values())
        for e in self.ENGS:
            self.bar[e] = list(last)

    def finish(self):
        lat = {}
        for I in self.dmas:
            lat[id(I.semv[0])] = I
        self.bar["sp"] = list(lat.values())
        self.add("sp", None)

    def emit(self, nc, stack):
        engobj = {"pe": "tensor", "act": "scalar", "dve": "vector", "pool": "gpsimd", "sp": "sync"}
        for e in self.ENGS:
            for I in self.streams[e]:
                for Dp in I.deps:
                    if Dp.dma is None:
                        if Dp.eng == I.eng and I.eng in self.INORDER_FREE:
                            continue
                        Dp.inc = True
        for e in self.ENGS:
            c = 0
            for I in self.streams[e]:
                if I.inc:
                    c += 1
                I.val = c
        sems = {e: stack.enter_context(nc.semaphore("s_" + e)) for e in self.ENGS}
        for i, b in enumerate(self.dma_bufs):
            b.sem = stack.enter_context(nc.semaphore("d%d" % i))
        block = stack.enter_context(nc.Block())

        def body_for(e):
            def body(engine):
                waited = {}
                for I in self.streams[e]:
                    for Dp in I.deps:
                        if Dp.dma is not None:
                            key = id(Dp.semv[0])
                            v = Dp.semv[1]
                            sem = Dp.semv[0].sem
                        else:
                            if Dp.eng == e and e in self.INORDER_FREE:
                                continue
                            key = Dp.eng
                            v = Dp.val
                            sem = sems[Dp.eng]
                        if waited.get(key, 0) >= v:
                            continue
                        waited[key] = v
                        engine.wait_ge(sem, v)
                    if I.fn is not None:
                        ins = I.fn(engine)
                        if I.dma is not None:
                            ins.then_inc(I.dma.sem, 16)
                        elif I.inc:
                            ins.then_inc(sems[e], 1)
            return body

        for e in self.ENGS:
            getattr(block, engobj[e])(body_for(e))


class Arena:
    def __init__(self, ap, nbytes):
        self.ap = ap
        self.n = nbytes
        self.off = 0

    def alloc(self, shape, dt):
        esz = 2 if dt == BF16 else 4
        n = 1
        for s in shape:
            n *= s
        nb = (n * esz + 31) // 32 * 32
        assert self.off + nb <= self.n, "arena overflow %d + %d > %d" % (self.off, nb, self.n)
        v = self.ap[:, self.off // 4:(self.off + nb) // 4]
        if dt != F32:
            v = v.bitcast(dt)
        v = v[:, 0:n]
        self.off += nb
        if len(shape) == 2:
            v = v.rearrange("p (a b) -> p a b", a=shape[0])
        elif len(shape) == 3:
            v = v.rearrange("p (a b c) -> p a b c", a=shape[0], b=shape[1])
        return v


def build_program():
    nc = bass.Bass("TRN2", target_bir_lowering=False)
    dr = lambda name, shape, dt, kind="ExternalInput": nc.dram_tensor(name, shape, dt, kind=kind).ap()
    x_d = dr("x", [S, D], F32)
    pos_d = dr("posT", [128, NBLK], I32)
    win_d = dr("w_in", [D, INW], F32)
    wpool_d = dr("w_pool", [4, 256, 256], F32)
    wout_d = dr("w_out", [D, D], F32)
    wup_d = dr("w_up", [D, 2 * DFF], F32)
    wdown_d = dr("w_down", [DFF, D], F32)
    pp_d = dr("pp", [128, NPP], F32)
    rb_d = dr("rb", [128, NRB], F32)
    cst_d = dr("cst", [128, NCS], F32)
    out_d = dr("out", [S, D], F32, kind="ExternalOutput")
    wups_d = dr("wup_bf16_scratch", [128, 8 * 2 * DFF], BF16, kind="Internal")
    wdns_d = dr("wdown_bf16_scratch", [128, 22 * 1024], BF16, kind="Internal")
    FC_GROUPS = [(0, 6), (6, 12), (12, 17), (17, 22)]
    wups_g = []
    _o = 0
    for (a_, b_) in FC_GROUPS:
        n_ = (b_ - a_) * 128
        wups_g.append(wups_d[:, _o:_o + 16 * n_].rearrange("p (h k c) -> p h k c", h=2, k=8))
        _o += 16 * n_
    wdns_v = wdns_d.rearrange("p (k c) -> p k c", k=22)

    win_v = win_d.rearrange("(k p) c -> p k c", p=128)
    wout_v = wout_d.rearrange("(k p) c -> p k c", p=128)
    wup_v = wup_d.rearrange("(k p) c -> p k c", p=128)
    wdown_v = wdown_d.rearrange("(k p) c -> p k c", p=128)
    wpool_v = wpool_d.rearrange("g (cc p) d -> p g cc d", p=128)

    R = Rec()
    stack = ExitStack()
    ARENA_BYTES = 212000
    arena_t = stack.enter_context(nc.sbuf_tensor("arena", [128, ARENA_BYTES // 4], F32))
    ps = stack.enter_context(nc.psum_tensor("ps", [128, 8, 512], F32))
    A = Arena(arena_t, ARENA_BYTES)
    PB = [Buf("ps%d" % i, excl=True) for i in range(8)]
    trp_bf = ps[:, 0, :].bitcast(BF16)

    def mm(out, lhsT, rhs, start, stop, r, w):
        R.add("pe", lambda e: e.matmul(out, lhsT, rhs, start=start, stop=stop), r=r, w=w)

    def tr(out, in_, ident, r, w):
        R.add("pe", lambda e: e.transpose(out, in_, ident), r=r, w=w)

    def act(out, in_, func, r, w, bias=None, scale=None, accum=None):
        kw = {}
        if bias is not None:
            kw["bias"] = bias
        if scale is not None:
            kw["scale"] = scale
        if accum is not None:
            kw["accum_out"] = accum
        R.add("act", lambda e: e.activation(out, in_, func, **kw), r=r, w=w)

    def tt(eng, out, in0, in1, op, r, w):
        R.add(eng, lambda e: e.tensor_tensor(out, in0, in1, op), r=r, w=w)

    def ts(eng, out, in0, s1, s2, op0, op1, r, w):
        if s2 is None:
            R.add(eng, lambda e: e.tensor_scalar(out, in0, s1, None, op0), r=r, w=w)
        else:
            R.add(eng, lambda e: e.tensor_scalar(out, in0, s1, s2, op0, op1), r=r, w=w)

    def stt(out, in0, scalar, in1, op0, op1, r, w):
        R.add("dve", lambda e: e.scalar_tensor_tensor(out, in0, scalar, in1, op0, op1), r=r, w=w)

    def cp(eng, out, in_, r, w):
        if eng == "act":
            R.add("act", lambda e: e.copy(out, in_), r=r, w=w)
        else:
            R.add(eng, lambda e: e.tensor_copy(out, in_), r=r, w=w)

    def mset(eng, ap, val, w):
        R.add(eng, lambda e: e.memset(ap, val), r=(), w=w)

    def dma(q, out, in_, r, w, sb, nodep=False):
        R.add(q, lambda e: e.dma_start(out=out, in_=in_), r=r, w=w, dma=sb, nodep=nodep)

    pp = A.alloc([NPP], F32)
    PPb = Buf("pp")
    bgh = A.alloc([16], F32)
    BGH = Buf("bgh")
    exps = A.alloc([8], F32)
    EXPS = Buf("exps")
    ident = A.alloc([128], BF16)
    IDENT = Buf("ident")
    neghalf = A.alloc([32], F32)
    NEGH = Buf("neghalf")
    negpi = A.alloc([8], F32)
    NEGPI = Buf("negpi")
    cwv = pp[:, PP_CW:PP_CW + 132].rearrange("p (j c) -> p j c", j=3)
    persist_mark = A.off

    win = A.alloc([8, INW], BF16)
    WIN_G, WIN_U, WIN_Q = Buf("win_g"), Buf("win_u"), Buf("win_q")
    wpool = A.alloc([4, 2, 256], BF16)
    WPOOL = Buf("wpool")
    wout = A.alloc([8, 1024], BF16)
    WOUT = Buf("wout")
    CST = Buf("cst")
    corrp = A.alloc([64], F32)
    rb1 = A.alloc([2176], F32)
    RB1 = Buf("rb1")
    g1 = rb1[:, 0:1024]
    gqk = rb1[:, 1024:2176]
    maskC = A.alloc([512], BF16)
    maskP = A.alloc([512], BF16)
    MASK = Buf("mask")
    posi = A.alloc([NBLK], I32)
    POSI = Buf("posi")
    posf = A.alloc([NBLK], F32)
    ANG = Buf("ang")
    cos_t = A.alloc([NBLK, 8], F32)
    sin_t = A.alloc([NBLK, 8], F32)
    ROPE = Buf("rope")
    onesz = A.alloc([2, 128], BF16)
    ONESZ = Buf("onesz")
    halo = A.alloc([8, 16], F32)
    HALO = [Buf("halo%d" % c) for c in range(8)]
    xn = [A.alloc([1024], F32) for _ in range(2)]
    XN = [Buf("xn%d" % i) for i in range(2)]
    xr = [A.alloc([1024], F32) for _ in range(1)]
    XR = [Buf("xr%d" % i) for i in range(1)]
    junk = A.alloc([1280], BF16)
    JUNK = Buf("junk")
    JUNKN = JUNK
    ss1 = [A.alloc([8], F32) for _ in range(2)]
    SS1 = [Buf("ss1_%d" % i) for i in range(2)]
    hn = [A.alloc([1024], BF16) for _ in range(2)]
    HN = [Buf("hn%d" % i) for i in range(2)]
    hT = [A.alloc([8, 256], BF16) for _ in range(1)]
    HT = [[Buf("hT%d_%d" % (s, j)) for j in range(2)] for s in range(1)]
    gT = A.alloc([16, 256], F32)
    GT = [Buf("gT%d" % c) for c in range(16)]
    u2 = [A.alloc([2, 272], F32) for _ in range(2)]
    U2 = [[Buf("u2_%d_%d" % (s, cc)) for cc in range(2)] for s in range(2)]
    U2H = [[Buf("u2h%d_%d" % (s, cc)) for cc in range(2)] for s in range(2)]
    sA = A.alloc([2, 272], F32)
    sB = A.alloc([2, 272], F32)
    SA, SB = [Buf("sA0"), Buf("sA1")], [Buf("sB0"), Buf("sB1")]
    pooledT = [A.alloc([2, 256], BF16) for _ in range(4)]
    POOLED = [Buf("pooled%d" % s) for s in range(4)]
    ag = A.alloc([8, 256], F32)
    AG = [Buf("ag%d" % c) for c in range(8)]
    sq = junk[:, 0:1152]
    SQ = JUNK
    st18 = A.alloc([3, 18], F32)
    ST18 = Buf("st18")
    qn = A.alloc([18, 64], F32)
    QN = Buf("qn")
    rtmp = A.alloc([4, 18, 8], F32)
    RT = [Buf("rtmp%d" % i) for i in range(4)]
    qkb = [A.alloc([18, 64], BF16) for _ in range(2)]
    QKB = [Buf("qkb%d" % i) for i in range(2)]
    kz = [A.alloc([4, 128], BF16) for _ in range(3)]
    KZ = [Buf("kz%d" % i) for i in range(3)]
    vz = [A.alloc([4, 128], BF16) for _ in range(3)]
    VZ = [Buf("vz%d" % i) for i in range(3)]
    qT = [A.alloc([8, 128], BF16) for _ in range(2)]
    QT = [Buf("qT%d" % i) for i in range(2)]
    kT = [A.alloc([4, 128], BF16) for _ in range(3)]
    KT = [Buf("kT%d" % i) for i in range(3)]
    scr_base = A.off
    PT = [[A.alloc([512], BF16) for _ in range(8)] for _ in range(1)]
    PTB = [[Buf("PT%d_%d" % (s, i)) for i in range(8)] for s in range(1)]
    rec = [A.alloc([4, 128], F32) for _ in range(2)]
    REC = [Buf("rec%d" % i) for i in range(2)]
    A2 = Arena(arena_t, A.off)
    A2.off = scr_base
    cstf = A2.alloc([NCS], F32)
    ang = A2.alloc([NBLK, 8], F32)
    angs = A2.alloc([NBLK, 8], F32)
    angk = A2.alloc([NBLK, 8], F32)
    angi = A2.alloc([NBLK, 8], I32)
    cT = [A.alloc([8, 256], BF16) for _ in range(2)]
    CT = [[Buf("cT%d_%d" % (s, j)) for j in range(2)] for s in range(2)]
    OUTB = [Buf("out%d" % i) for i in range(NBLK)]
    p1_end = A.off
    print('[arena] phase1 end', p1_end, 'of', ARENA_BYTES)

    dma("sp", pp, pp_d[:, :], [], [PPb], PPb)
    dma("sp", cstf, cst_d[:, :], [], [CST], CST)
    dma("sp", rb1, rb_d[:, 0:2176], [], [RB1], RB1)
    dma("sp", posi, pos_d[:, :], [], [POSI], POSI)
    WB = {kd: {"act": Buf("w%s_a" % kd), "dve": Buf("w%s_d" % kd)} for kd in "QUGPO"}
    WIN_Q_L = list(WB["Q"].values())
    WIN_U_L = list(WB["U"].values())
    WIN_G_L = list(WB["G"].values())
    WPOOL_L = list(WB["P"].values())
    WOUT_L = list(WB["O"].values())
    stage_jobs = []
    for k in range(8):
        for h in range(2):
            c0 = 1024 + h * 640
            stage_jobs.append((win[:, k, c0:c0 + 640], win_v[:, k, c0:c0 + 640], "Q", 640))
    for k in range(8):
        stage_jobs.append((win[:, k, 0:1024], win_v[:, k, 0:1024], "U", None))
    for k in range(8):
        for h in range(2):
            c0 = 2304 + h * 1024
            stage_jobs.append((win[:, k, c0:c0 + 1024], win_v[:, k, c0:c0 + 1024], "G", None))
    for h in range(2):
        stage_jobs.append((wpool[:, 2 * h:2 * h + 2], wpool_v[:, 2 * h:2 * h + 2], "P", "p (g c) d -> p g c d"))
    for k in range(8):
        stage_jobs.append((wout[:, k, :], wout_v[:, k, :], "O", None))

    stage_ctr = [0]

    def emit_stage_jobs(kinds):
        todo = [j_ for j_ in stage_jobs if j_[2] in kinds]
        for (dst, src, kd, rr) in todo:
            i = stage_ctr[0]
            stage_ctr[0] += 1
            sl_ = i % 4
            sv = gT[:, 4 * sl_:4 * sl_ + 4, :]
            if isinstance(rr, int):
                sv = sv.rearrange("p a b -> p (a b)")[:, 0:rr]
            elif rr:
                sv = sv.rearrange(rr, g=2)
            else:
                sv = sv.rearrange("p a b -> p (a b)")
            GS = GT[4 * sl_:4 * sl_ + 4]
            dma("sp", sv, src, [], GS, GT[4 * sl_])
            eng = "act" if i % 2 == 0 else "dve"
            if eng == "act":
                R.add("act", (lambda d_, s_: (lambda e: e.copy(d_, s_)))(dst, sv), r=GS, w=[WB[kd][eng]], nodep=True)
            else:
                R.add("dve", (lambda d_, s_: (lambda e: e.tensor_copy(d_, s_)))(dst, sv), r=GS, w=[WB[kd][eng]], nodep=True)

    cp("dve", corrp, cstf[:, CS_CORR:CS_CORR + 64], [CST], [BGH])
    cp("dve", ident, cstf[:, CS_ID:CS_ID + 128], [CST], [IDENT])
    cp("dve", maskC, cstf[:, CS_MC:CS_MC + 512], [CST], [MASK])
    cp("dve", maskP, cstf[:, CS_MP:CS_MP + 512], [CST], [MASK])
    corr = corrp.rearrange("p (g t) -> p g t", g=4)
    ts("dve", bgh, pp[:, PP_BG:PP_BG + 16], 0.5, None, ALU.mult, None, [PPb], [BGH])
    ts("dve", gqk[:, 0:1024], gqk[:, 0:1024], 0.125, None, ALU.mult, None, [RB1], [RB1])
    act(exps, pp[:, PP_SK:PP_SK + 8], AF.Exp, [PPb], [EXPS])
    mset("dve", neghalf, -0.5, [NEGH])
    mset("dve", negpi, -math.pi, [NEGPI])
    mset("pool", halo, 0.0, HALO)
    for i in range(3):
        mset("pool", kz[i], 0.0, [KZ[i]])
        mset("pool", vz[i], 0.0, [VZ[i]])
    mset("pool", onesz, 0.0, [ONESZ])
    mset("pool", onesz[:, 0, 0:64], 1.0, [ONESZ])
    mset("pool", onesz[:, 1, 64:128], 1.0, [ONESZ])
    inv_freq = [float(ROPE_THETA ** (-(2.0 * i) / 16.0)) for i in range(8)]
    TWO_PI = 2.0 * math.pi

    def range_reduce_sin(dst, shift):
        ts("dve", angs, ang, shift, None, ALU.add, None, [ANG], [ANG])
        ts("dve", angi, angs, 1.0 / TWO_PI, None, ALU.mult, None, [ANG], [ANG])
        cp("dve", angk, angi, [ANG], [ANG])
        stt(angs, angk, -TWO_PI, angs, ALU.mult, ALU.add, [ANG], [ANG])
        ts("dve", angk, angs, math.pi, -TWO_PI, ALU.is_gt, ALU.mult, [ANG], [ANG])
        tt("dve", angs, angs, angk, ALU.add, [ANG], [ANG])
        ts("dve", angk, angs, -math.pi, TWO_PI, ALU.is_lt, ALU.mult, [ANG], [ANG])
        tt("dve", angs, angs, angk, ALU.add, [ANG], [ANG])
        ts("dve", angs, angs, math.pi, -math.pi, ALU.min, ALU.max, [ANG], [ANG])
        act(dst, angs, AF.Sin, [ANG], [ROPE])

    def rope_setup():
        cp("dve", posf, posi, [POSI], [ANG])
        for i in range(8):
            ts("dve", ang[:, :, i], posf, inv_freq[i], None, ALU.mult, None, [ANG], [ANG])
        range_reduce_sin(sin_t, 0.0)
        range_reduce_sin(cos_t, math.pi / 2)

    FM = (1, 2)
    SC = (6, 7)

    def norm_pre(gb, src_ap, src_deps, gain, GAINB, xnring, XNring, skip_load=False):
        s3 = gb % len(xnring)
        s2 = gb % 2
        s1 = gb % len(hn)
        if not skip_load:
            dma("sp", xnring[s3], src_ap, src_deps, [XNring[s3]], XNring[s3])
        act(hn[s1], xnring[s3], AF.Square, [XNring[s3]], [HN[s1], SS1[s2]], accum=ss1[s2][:, 0:1])
        ts("dve", ss1[s2][:, 1:2], ss1[s2][:, 0:1], 1.0 / D, EPS, ALU.mult, ALU.add, [SS1[s2]], [SS1[s2]])
        tt("pool", ss1[s2][:, 2:3], ss1[s2][:, 1:2], neghalf[:, 0:1], ALU.pow, [SS1[s2], NEGH], [SS1[s2]])
        stt(hn[s1], xnring[s3], ss1[s2][:, 2:3], gain, ALU.mult, ALU.mult, [XNring[s3], SS1[s2], GAINB], [HN[s1]])

    def norm_tr(gb, hT_dst, HT_dstB):
        s1 = gb % len(hn)
        for k in range(8):
            tr(trp_bf[:, k * 128:(k + 1) * 128], hn[s1][:, k * 128:(k + 1) * 128], ident, [HN[s1], IDENT], [PB[0]])
        cp("act", hT_dst, trp_bf.rearrange("p (k n) -> p k n", k=8), [PB[0]], [HT_dstB])

    def norm_block(gb, src_ap, src_deps, gain, GAINB, hT_dst, HT_dstB, xnring, XNring):
        norm_pre(gb, src_ap, src_deps, gain, GAINB, xnring, XNring)
        norm_tr(gb, hT_dst, HT_dstB)

    NT1 = S // 256
    hs = 0
    HTr = [HT[hs][0], HT[hs][1]]
    ps4 = lambda bk: ps[:, bk, :].rearrange("p (a b) -> p a b", a=4)

    def p1_norm_pre(ti, j):
        gb = 2 * ti + j
        norm_pre(gb, x_d[gb * 128:(gb + 1) * 128, :], [], g1, RB1, xn, XN)

    def p1_norm_tr(ti, j):
        norm_tr(2 * ti + j, hT[hs][:, :, j * 128:(j + 1) * 128], HT[hs][j])

    def P_mm(ti, j):
        gb = 2 * ti + j
        tok = slice(j * 128, (j + 1) * 128)
        for k in range(8):
            lh = hT[hs][:, k, tok]
            mm(ps[:, 3, :], lh, win[:, k, 1024:1536], k == 0, k == 7, [HT[hs][j]] + WIN_Q_L, [PB[3]])
            mm(ps[:, 4, :], lh, win[:, k, 1536:2048], k == 0, k == 7, [HT[hs][j]] + WIN_Q_L, [PB[4]])
            mm(ps[:, 5, 0:256], lh, win[:, k, 2048:2304], k == 0, k == 7, [HT[hs][j]] + WIN_Q_L, [PB[5]])

    def P_chain(ti, j):
        gb = 2 * ti + j
        sl = gb % 3
        qb = gb % 2
        qn2 = qn.rearrange("p h d -> p (h d)")
        tt("dve", qn2[:, 0:1024].rearrange("p (a b) -> p a b", a=2), ps[:, 3:5, :],
           gqk[:, 0:1024].rearrange("p (a b) -> p a b", a=2), ALU.mult, [PB[3], PB[4], RB1], [QN])
        act(sq[:, 1024:1152], ps[:, 5, 0:128], AF.Square, [PB[5]], [SQ])
        vz4 = vz[sl].rearrange("p (kv par) d -> p kv par d", kv=2)
        vsrc = ps[:, 5, 128:256].rearrange("p (kv d) -> p kv d", kv=2)
        cp("act", vz4[:, :, 0, 0:64], vsrc, [PB[5]], [VZ[sl]])
        cp("act", vz4[:, :, 1, 64:128], vsrc, [PB[5]], [VZ[sl]])
        tt("dve", qn2[:, 1024:1152], ps[:, 5, 0:128], gqk[:, 1024:1152], ALU.mult, [PB[5], RB1], [QN])
        act(sq[:, 0:1024].rearrange("p (a b) -> p a b", a=2), ps[:, 3:5, :], AF.Square, [PB[3], PB[4]], [SQ])
        R.add("dve", lambda e: e.tensor_reduce(st18[:, 0, :], sq.rearrange("p (h d) -> p h d", d=64), AX.X, ALU.add),
              r=[SQ], w=[ST18])
        ts("dve", st18[:, 1, :], st18[:, 0, :], 1.0 / 64, EPS, ALU.mult, ALU.add, [ST18], [ST18])
        tt("pool", st18[:, 2, :], st18[:, 1, :], neghalf[:, 0:18], ALU.pow, [ST18, NEGH], [ST18])
        cosb = cos_t[:, gb, :].unsqueeze(1).to_broadcast([128, 18, 8])
        sinb = sin_t[:, gb, :].unsqueeze(1).to_broadcast([128, 18, 8])
        x1 = qn[:, :, 0:8]
        x2 = qn[:, :, 8:16]
        tt("dve", rtmp[:, 0], x1, cosb, ALU.mult, [QN, ROPE], [RT[0]])
        tt("dve", rtmp[:, 1], x2, sinb, ALU.mult, [QN, ROPE], [RT[1]])
        tt("dve", rtmp[:, 2], x2, cosb, ALU.mult, [QN, ROPE], [RT[2]])
        tt("dve", rtmp[:, 3], x1, sinb, ALU.mult, [QN, ROPE], [RT[3]])
        tt("dve", qn[:, :, 0:8], rtmp[:, 0], rtmp[:, 1], ALU.subtract, [RT[0], RT[1]], [QN])
        tt("dve", qn[:, :, 8:16], rtmp[:, 2], rtmp[:, 3], ALU.add, [RT[2], RT[3]], [QN])
        tt("dve", qkb[qb], qn, st18[:, 2, :].unsqueeze(2).to_broadcast([128, 18, 64]), ALU.mult,
           [QN, ST18], [QKB[qb]])
        kz4 = kz[sl].rearrange("p (kv par) d -> p kv par d", kv=2)
        cp("pool", kz4[:, :, 0, 0:64], qkb[qb][:, 16:18, :], [QKB[qb]], [KZ[sl]])
        cp("pool", kz4[:, :, 1, 64:128], qkb[qb][:, 16:18, :], [QKB[qb]], [KZ[sl]])

    def P_tr(ti, j):
        gb = 2 * ti + j
        sl = gb % 3
        qb = gb % 2
        qkb2 = qkb[qb].rearrange("p h d -> p (h d)")
        for c in range(8):
            tr(trp_bf[:, c * 128:(c + 1) * 128], qkb2[:, c * 128:(c + 1) * 128], ident, [QKB[qb], IDENT], [PB[0]])
        cp("act", qT[qb], trp_bf.rearrange("p (k n) -> p k n", k=8), [PB[0]], [QT[qb]])
        for i in range(4):
            tr(trp_bf[:, i * 128:(i + 1) * 128], kz[sl][:, i, :], ident, [KZ[sl], IDENT], [PB[0]])
        cp("act", kT[sl], trp_bf[:, 0:512].rearrange("p (k n) -> p k n", k=4), [PB[0]], [KT[sl]])

    def gates(ti, c, banks=None):
        bank = (banks or SC)[c % 2]
        for k in range(8):
            mm(ps[:, bank, 0:256], win[:, k, 2304 + c * 128:2304 + (c + 1) * 128], hT[hs][:, k, :],
               k == 0, k == 7, WIN_G_L + HTr, [PB[bank]])
        act(gT[:, c, :], ps[:, bank, 0:256], AF.Tanh, [PB[bank], BGH], [GT[c]],
            bias=bgh[:, c:c + 1], scale=0.5)

    UBANK = ((1, 2), (3, 4))

    def C_u(ti, g):
        us = g % 2
        U = u2[us]
        for cc in range(2):
            c = 2 * g + cc
            bank = UBANK[us][cc]
            for k in range(8):
                mm(ps[:, bank, 0:256], win[:, k, c * 128:(c + 1) * 128], hT[hs][:, k, :],
                   k == 0, k == 7, WIN_U_L + HTr, [PB[bank]])
            cp("pool", U[:, cc, 0:16], halo[:, c, :], [HALO[c]], [U2H[us][cc]])
            cp("act", U[:, cc, 16:272], ps[:, bank, 0:256], [PB[bank]], [U2[us][cc]])
            cp("pool", halo[:, c, :], U[:, cc, 256:272], [U2[us][cc]], [HALO[c]])

    def C_pool(ti, g):
        us = g % 2
        U = u2[us]
        Ur = [[U2[us][cc], U2H[us][cc]] for cc in range(2)]
        for cc in range(2):
            tt("dve", sA[:, cc, 1:272], U[:, cc, 1:272], U[:, cc, 0:271], ALU.add, Ur[cc], [SA[cc]])
        cur, CUR = sA, SA
        if g >= 1:
            for cc in range(2):
                tt("dve", sB[:, cc, 3:272], sA[:, cc, 3:272], sA[:, cc, 1:270], ALU.add, [SA[cc]], [SB[cc]])
            cur, CUR = sB, SB
        if g >= 2:
            for cc in range(2):
                tt("dve", sA[:, cc, 7:272], sB[:, cc, 7:272], sB[:, cc, 3:268], ALU.add, [SB[cc]], [SA[cc]])
            cur, CUR = sA, SA
        if g >= 3:
            for cc in range(2):
                tt("dve", sB[:, cc, 15:272], sA[:, cc, 15:272], sA[:, cc, 7:264], ALU.add, [SA[cc]], [SB[cc]])
            cur, CUR = sB, SB
        if ti == 0:
            for cc in range(2):
                tt("dve", cur[:, cc, 16:32], cur[:, cc, 16:32], corr[:, g, :], ALU.mult, [CUR[cc], BGH], [CUR[cc]])
        pl = pooledT[g]
        for cc in range(2):
            stt(pl[:, cc, :], cur[:, cc, 16:272], 1.0 / (2 ** (g + 1)), U[:, cc, 16:272],
                ALU.mult, ALU.subtract, [CUR[cc]] + Ur[cc], [POOLED[g]])

    def C_map(ti, g):
        pl = pooledT[g]
        for dc in range(2):
            ch = 2 * g + dc
            bk = (5, 1, 2, 3, 4, 6, 7, 5)[ch]
            hf = 1 if ch == 7 else 0
            o = ps[:, bk, hf * 256:(hf + 1) * 256]
            for cc in range(2):
                mm(o, wpool[:, g, cc, dc * 128:(dc + 1) * 128], pl[:, cc, :],
                   cc == 0, cc == 1, WPOOL_L + [POOLED[g]], [PB[bk]])
        for dc in range(2):
            ch = 2 * g + dc
            bk = (5, 1, 2, 3, 4, 6, 7, 5)[ch]
            hf = 1 if ch == 7 else 0
            o = ps[:, bk, hf * 256:(hf + 1) * 256]
            stt(ag[:, ch, :], o, pp[:, PP_PS + ch:PP_PS + ch + 1], gT[:, ch, :],
                ALU.mult, ALU.mult, [PB[bk], PPb, GT[ch]], [AG[ch]])

    def S_core(ti, j):
        gb = 2 * ti + j
        sl = gb % 3
        psl = (gb - 1) % 3
        qb = gb % 2
        kbs = [(psl, maskP), (sl, maskC)] if gb > 0 else [(sl, maskC)]
        cnt = 0
        for kv in range(2):
            for par in range(2):
                for kbi, (ksl, mk) in enumerate(kbs):
                    bank = SC[cnt % 2]
                    cnt += 1
                    u = (kv * 2 + par) * 2 + kbi
                    mm(ps4(bank), kT[ksl][:, kv * 2 + par, :],
                       qT[qb][:, kv * 4:(kv + 1) * 4, :], True, False, [KT[ksl], QT[qb]], [PB[bank]])
                    mm(ps[:, bank, :], ident, mk, False, True, [IDENT, MASK], [PB[bank]])
                    act(PT[0][u], ps[:, bank, :], AF.Exp, [PB[bank]], [PTB[0][u]])

    PVB = (((1, 2), (3, 4)), ((1, 2), (6, 7)))

    def S_pv(ti, j):
        gb = 2 * ti + j
        sl = gb % 3
        psl = (gb - 1) % 3
        kbs = [(psl, maskP), (sl, maskC)] if gb > 0 else [(sl, maskC)]
        for kv in range(2):
            pvb, denb = PVB[j][kv]
            n = 2 * len(kbs)
            i = 0
            for par in range(2):
                for kbi, (ksl, mk) in enumerate(kbs):
                    u = (kv * 2 + par) * 2 + kbi
                    mm(ps[:, pvb, :], vz[ksl][:, kv * 2 + par, :], PT[0][u], i == 0, i == n - 1,
                       [VZ[ksl], PTB[0][u]], [PB[pvb]])
                    i += 1
            i = 0
            for par in range(2):
                for kbi, (ksl, mk) in enumerate(kbs):
                    u = (kv * 2 + par) * 2 + kbi
                    mm(ps[:, denb, :], onesz[:, par, :], PT[0][u], i == 0, i == n - 1,
                       [ONESZ, PTB[0][u]], [PB[denb]])
                    i += 1

    def N_chain(ti, j):
        N_a(ti, j)
        N_b(ti, j)

    def N_a(ti, j):
        banks = PVB[j]
        for kv in range(2):
            ch = slice(kv * 4, (kv + 1) * 4)
            tt("dve", rec[kv], ps4(banks[kv][1]), exps[:, ch].unsqueeze(2).to_broadcast([128, 4, 128]), ALU.add,
               [PB[banks[kv][1]], EXPS], [REC[kv]])
        for kv in range(2):
            rk = rec[kv]
            R.add("dve", (lambda rk: (lambda e: e.reciprocal(rk, rk)))(rk), r=[REC[kv]], w=[REC[kv]])

    def N_b(ti, j):
        cs = ti % 2
        tok = slice(j * 128, (j + 1) * 128)
        banks = PVB[j]
        for kv in range(2):
            tt("dve", rec[kv], ps4(banks[kv][0]), rec[kv], ALU.mult, [PB[banks[kv][0]], REC[kv]], [REC[kv]])
        for kv in range(2):
            gch = slice(8 + kv * 4, 8 + (kv + 1) * 4)
            tt("pool", rec[kv], rec[kv], gT[:, gch, tok], ALU.mult, [REC[kv]] + GT[gch], [REC[kv]])
        for kv in range(2):
            ch = slice(kv * 4, (kv + 1) * 4)
            tt("pool", cT[cs][:, ch, tok], rec[kv], ag[:, ch, tok], ALU.add, [REC[kv]] + AG[ch], [CT[cs][j]])

    def E_mm(ti, j):
        gb = 2 * ti + j
        cs = ti % 2
        rs = 0
        tok = slice(j * 128, (j + 1) * 128)
        b0, b1 = (3, 4) if j == 0 else (1, 2)
        dma("sp", xr[rs], x_d[gb * 128:(gb + 1) * 128, :], [], [XR[rs]], XR[rs])
        for k in range(8):
            lh = cT[cs][:, k, tok]
            mm(ps[:, b0, :], lh, wout[:, k, 0:512], k == 0, k == 7, [CT[cs][j]] + WOUT_L, [PB[b0]])
            mm(ps[:, b1, :], lh, wout[:, k, 512:1024], k == 0, k == 7, [CT[cs][j]] + WOUT_L, [PB[b1]])

    def E_add(ti, j):
        gb = 2 * ti + j
        rs = 0
        b0, b1 = (3, 4) if j == 0 else (1, 2)
        xr3 = xr[rs].rearrange("p (a b) -> p a b", a=2)
        tt("dve", xr3, ps[:, b0:b1 + 1, :], xr3, ALU.add, [PB[b0], PB[b1], XR[rs]], [XR[rs]])
        dma("sp", out_d[gb * 128:(gb + 1) * 128, :], xr[rs], [XR[rs]], [OUTB[gb]], XR[rs])

    WUPS, WDNS = Buf("wups"), Buf("wdns")
    conv_jobs = []
    for gi_, (a_, b_) in enumerate(FC_GROUPS):
        for h_ in range(2):
            for k in range(8):
                conv_jobs.append((wups_g[gi_][:, h_, k, :],
                                  wup_v[:, k, h_ * DFF + a_ * 128:h_ * DFF + b_ * 128], WUPS))
    for kk in range(22):
        conv_jobs.append((wdns_v[:, kk, :], wdown_v[:, kk, :], WDNS))

    def emit_conv(n):
        for _ in range(n):
            if conv_jobs:
                o, i, B_ = conv_jobs.pop(0)
                dma("pool", o, i, [], [B_], B_, nodep=True)

    for j in range(2):
        p1_norm_pre(0, j)
        p1_norm_tr(0, j)
    emit_stage_jobs("Q")
    rope_setup()
    P_mm(0, 0)
    P_chain(0, 0)
    emit_stage_jobs("U")
    for ti in range(NT1):
        P_mm(ti, 1)
        P_chain(ti, 1)
        C_u(ti, 0)
        C_u(ti, 1)
        if ti == 0:
            emit_stage_jobs("GPO")
        if ti == 0:
            for c in range(0, 4):
                gates(ti, c)
            C_pool(ti, 0)
            C_u(ti, 2)
            for c in range(4, 8):
                gates(ti, c)
            P_tr(ti, 0)
            ts("pool", gT[:, 0:8, :], gT[:, 0:8, :], 0.5, 0.5, ALU.mult, ALU.add, GT[0:8], GT[0:8])
            C_pool(ti, 1)
            C_u(ti, 3)
            for c in range(8, 16):
                gates(ti, c)
        else:
            for c in range(8, 12):
                gates(ti, c)
            C_pool(ti, 0)
            C_u(ti, 2)
            for c in range(12, 16):
                gates(ti, c)
            P_tr(ti, 0)
            C_pool(ti, 1)
            C_u(ti, 3)
        ts("pool", gT[:, 8:16, :], gT[:, 8:16, :], 0.5, 0.5, ALU.mult, ALU.add, GT[8:16], GT[8:16])
        P_tr(ti, 1)
        C_pool(ti, 2)
        C_pool(ti, 3)
        for g in range(4):
            C_map(ti, g)
        nxt = ti + 1 < NT1
        if nxt:
            p1_norm_pre(ti + 1, 0)
            p1_norm_pre(ti + 1, 1)
        S_core(ti, 0)
        S_pv(ti, 0)
        if nxt:
            p1_norm_tr(ti + 1, 0)
        S_core(ti, 1)
        N_chain(ti, 0)
        S_pv(ti, 1)
        if nxt:
            p1_norm_tr(ti + 1, 1)
        E_mm(ti, 0)
        if nxt:
            for c in range(0, 8):
                gates(ti + 1, c, banks=(5, 0))
        N_a(ti, 1)
        E_add(ti, 0)
        if nxt:
            P_mm(ti + 1, 0)
        N_b(ti, 1)
        if nxt:
            ts("pool", gT[:, 0:8, :], gT[:, 0:8, :], 0.5, 0.5, ALU.mult, ALU.add, GT[0:8], GT[0:8])
            P_chain(ti + 1, 0)
        E_mm(ti, 1)
        E_add(ti, 1)
        emit_conv(6)
    emit_conv(100)

    R.barrier()
    A.off = persist_mark
    fc_groups = FC_GROUPS
    wupg = [A.alloc([2, 8, (b_ - a_) * 128], BF16) for (a_, b_) in fc_groups]
    WUPG = [Buf("wup%d" % i) for i in range(4)]
    fc2grp = {}
    for gi, (a, b) in enumerate(fc_groups):
        for fc in range(a, b):
            fc2grp[fc] = gi
    wdown = A.alloc([22, 1024], BF16)
    WDN = [Buf("wdn0"), Buf("wdn1")]
    g2 = A.alloc([1024], F32)
    G2 = Buf("g2")
    chalo = A.alloc([44, 2], F32)
    CH = [Buf("chalo%d" % i) for i in range(44)]
    xn2 = [A.alloc([1024], F32) for _ in range(2)]
    XN2 = [Buf("xn2_%d" % i) for i in range(2)]
    xr2 = [A.alloc([1024], F32) for _ in range(2)]
    XR2 = [Buf("xr2_%d" % i) for i in range(2)]
    ss1 = [A.alloc([8], F32) for _ in range(2)]
    hn = [A.alloc([1024], BF16) for _ in range(1)]
    hT2 = A.alloc([8, 512], BF16)
    HT2 = [Buf("hT2_%d" % j) for j in range(4)]
    yb = [[A.alloc([512], F32) for _ in range(2)] for _ in range(2)]
    YB = [[Buf("y%d_%d" % (h, s)) for s in range(2)] for h in range(2)]
    ub = [[A.alloc([514], F32) for _ in range(2)] for _ in range(2)]
    UH = [[Buf("uh%d_%d" % (h, s)) for s in range(2)] for h in range(2)]
    UM = [[Buf("um%d_%d" % (h, s)) for s in range(2)] for h in range(2)]
    sg = [A.alloc([512], F32) for _ in range(2)]
    SG = [Buf("sg%d" % s) for s in range(2)]
    hid = A.alloc([22, 512], BF16)
    HID = [Buf("hid%d" % fc) for fc in range(22)]
    print('[arena] phase2 end', A.off)

    dma("sp", g2, rb_d[:, RB_G2:RB_G2 + 1024], [], [G2], G2)
    mset("pool", chalo, 0.0, CH)

    def p2_wload(gi):
        dma("sp", wupg[gi].rearrange("p h k c -> p (h k c)"), wups_g[gi].rearrange("p h k c -> p (h k c)"),
            [WUPS], [WUPG[gi]], WUPG[gi])

    def p2_xload(j):
        dma("sp", xn2[j % 2], out_d[j * 128:(j + 1) * 128, :], [OUTB[j]], [XN2[j % 2]], XN2[j % 2])

    GB = ((1, 2, 3), (4, 5, 6))
    NT2 = S // 512

    def p2_norm_pre(ti, j, skip_load=False):
        gb = 4 * ti + j
        norm_pre(gb, out_d[gb * 128:(gb + 1) * 128, :], [OUTB[gb]], g2, G2, xn2, XN2, skip_load=skip_load)

    def p2_loads(ti, j):
        gb = 4 * ti + j
        rs = gb % 2
        dma("sp", xr2[rs], out_d[gb * 128:(gb + 1) * 128, :], [OUTB[gb]], [XR2[rs]], XR2[rs])
        if ti + 1 < NT2:
            gn = 4 * (ti + 1) + j
            dma("sp", xn2[gn % 2], out_d[gn * 128:(gn + 1) * 128, :], [OUTB[gn]], [XN2[gn % 2]], XN2[gn % 2])

    def p2_norm_tr(ti, j):
        norm_tr(4 * ti + j, hT2[:, :, j * 128:(j + 1) * 128], HT2[j])

    def ffn_front(fc):
        rg = fc % 2
        for half in range(2):
            chn = half * 22 + fc
            cp("pool", ub[half][rg][:, 0:2], chalo[:, chn, :], [CH[chn]], [UH[half][rg]])
        for half in range(2):
            chn = half * 22 + fc
            col0 = half * DFF + fc * 128
            bank = GB[half][fc % 3]
            for k in range(8):
                gi_ = fc2grp[fc]
                lc = (fc - fc_groups[gi_][0]) * 128
                mm(ps[:, bank, :], wupg[gi_][:, half, k, lc:lc + 128], hT2[:, k, :], k == 0, k == 7,
                   [WUPG[gi_]] + HT2, [PB[bank]])
        for half in range(2):
            chn = half * 22 + fc
            bank = GB[half][fc % 3]
            act(yb[half][rg], ps[:, bank, :], AF.Identity, [PB[bank], PPb], [YB[half][rg]],
                bias=pp[:, PP_CB + chn:PP_CB + chn + 1], scale=cwv[:, 2, chn:chn + 1])
            cp("act", ub[half][rg][:, 2:514], ps[:, bank, :], [PB[bank]], [UM[half][rg]])
        for half in range(2):
            chn = half * 22 + fc
            cp("pool", chalo[:, chn, :], ub[half][rg][:, 512:514], [UM[half][rg]], [CH[chn]])
        for tap, off in ((1, 1), (0, 0)):
            for half in range(2):
                chn = half * 22 + fc
                y = yb[half][rg]
                stt(y, ub[half][rg][:, off:off + 512], cwv[:, tap, chn:chn + 1], y, ALU.mult, ALU.add,
                    [UH[half][rg], UM[half][rg], PPb, YB[half][rg]], [YB[half][rg]])

    def ffn_silu(fc):
        rg = fc % 2
        act(sg[rg], yb[0][rg], AF.Silu, [YB[0][rg]], [SG[rg]])

    def ffn_mult(fc):
        rg = fc % 2
        tt("pool", hid[:, fc, :], sg[rg], yb[1][rg], ALU.mult, [SG[rg], YB[1][rg]], [HID[fc]])

    p2_xload(0)
    p2_xload(1)
    p2_wload(0)
    for j in range(4):
        p2_norm_pre(0, j, skip_load=True)
        p2_norm_tr(0, j)
        if j + 2 < 4:
            p2_xload(j + 2)
        if j == 1:
            p2_wload(1)
            dma("sp", wdown[:, 0:11, :], wdns_v[:, 0:11, :], [WDNS], [WDN[0]], WDN[0])
    p2_wload(2)
    p2_wload(3)
    dma("sp", wdown[:, 11:22, :], wdns_v[:, 11:22, :], [WDNS], [WDN[1]], WDN[1])
    for ti in range(NT2):
        for fc in range(22):
            ffn_front(fc)
            if fc > 0:
                ffn_silu(fc - 1)
                ffn_mult(fc - 1)
        ffn_silu(21)
        ffn_mult(21)
        p2_loads(ti, 0)
        for j in range(4):
            gb = 4 * ti + j
            rs = gb % 2
            tok = slice(j * 128, (j + 1) * 128)
            if ti + 1 < NT2:
                p2_norm_pre(ti + 1, j, skip_load=True)
            g0 = 1 if j % 2 == 0 else 4
            for fc in range(22):
                lh = hid[:, fc, tok]
                mm(ps[:, g0, :], lh, wdown[:, fc, 0:512], fc == 0, fc == 21, [HID[fc], WDN[fc // 11]], [PB[g0]])
                mm(ps[:, g0 + 1, :], lh, wdown[:, fc, 512:1024], fc == 0, fc == 21, [HID[fc], WDN[fc // 11]], [PB[g0 + 1]])
            if ti + 1 < NT2:
                p2_norm_tr(ti + 1, j)
            if j + 1 < 4:
                p2_loads(ti, j + 1)
            xr3 = xr2[rs].rearrange("p (a b) -> p a b", a=2)
            tt("dve", xr3, ps[:, g0:g0 + 2, :], xr3, ALU.add, [PB[g0], PB[g0 + 1], XR2[rs]], [XR2[rs]])
            dma("sp", out_d[gb * 128:(gb + 1) * 128, :], xr2[rs], [XR2[rs]], [OUTB[gb]], XR2[rs])

    R.finish()
    R.emit(nc, stack)
    stack.close()
    return nc


def _host_layout(inputs):
    f = lambda a: np.ascontiguousarray(np.asarray(a), dtype=np.float32)
    b_gate = f(inputs["b_gate"])[0]
    pool_scale = f(inputs["pool_scale"])[0]
    sinks = f(inputs["sinks"])[0]
    conv_w = f(inputs["conv_w"])[0]
    conv_b = f(inputs["conv_b"])[0]
    pp = np.zeros((128, NPP), np.float32)
    pp[:, PP_BG:PP_BG + 16] = b_gate.reshape(16, 128).T
    pp[:, PP_PS:PP_PS + 8] = pool_scale.reshape(8, 128).T
    pp[:, PP_SK:PP_SK + 8] = np.repeat(sinks.reshape(8, 2).T, 64, axis=0)
    pp[:, PP_CW:PP_CW + 132] = conv_w.reshape(3, 44, 128).transpose(2, 0, 1).reshape(128, 132)
    pp[:, PP_CB:PP_CB + 44] = conv_b.reshape(44, 128).T
    rb = np.zeros((128, NRB), np.float32)
    rb[:, RB_G1:RB_G1 + 1024] = f(inputs["attn_norm"])[0][None, :]
    rb[:, RB_GQ:RB_GQ + 1024] = np.tile(f(inputs["q_norm"])[0], 16)[None, :]
    rb[:, RB_GQ + 1024:RB_GQ + 1152] = np.tile(f(inputs["k_norm"])[0], 2)[None, :]
    rb[:, RB_G2:RB_G2 + 1024] = f(inputs["ffn_norm"])[0][None, :]
    cst = np.zeros((128, NCS), np.float32)
    cst[:, CS_ID:CS_ID + 128] = np.eye(128, dtype=np.float32)
    jj = np.arange(128)[:, None]
    ii = np.arange(128)[None, :]
    mc = np.where(jj <= ii, 0.0, NEG).astype(np.float32)
    mp = np.where(jj > ii, 0.0, NEG).astype(np.float32)
    cst[:, CS_MC:CS_MC + 512] = np.tile(mc, (1, 4))
    cst[:, CS_MP:CS_MP + 512] = np.tile(mp, (1, 4))
    corr = np.ones((4, 16), np.float32)
    for g in range(4):
        w = 2 ** (g + 1)
        for t in range(16):
            corr[g, t] = w / min(t + 1, w)
    cst[:, CS_CORR:CS_CORR + 64] = corr.reshape(1, 64)
    return pp, rb, cst


_NC_CACHE = {}


def kernel(**inputs):
    x = np.ascontiguousarray(np.asarray(inputs["x"]), dtype=np.float32)
    positions = np.ascontiguousarray(np.asarray(inputs["positions"]), dtype=np.int32)
    pp, rb, cst = _host_layout(inputs)
    f = lambda a: np.ascontiguousarray(np.asarray(a), dtype=np.float32)
    w_in = f(inputs["w_in"])[0]
    w_pool = f(inputs["w_pool"])[0]
    w_out = f(inputs["w_out"])[0]
    w_up = f(inputs["w_up"])[0]
    w_down = f(inputs["w_down"])[0]
    if "nc" not in _NC_CACHE:
        _NC_CACHE["nc"] = build_program()
    nc = _NC_CACHE["nc"]
    n = x.shape[0]
    in_maps = []
    for i in range(n):
        in_maps.append({
            "x": x[i],
            "posT": np.ascontiguousarray(positions[i].reshape(NBLK, 128).T),
            "w_in": w_in, "w_pool": w_pool, "w_out": w_out, "w_up": w_up, "w_down": w_down,
            "pp": pp, "rb": rb, "cst": cst,
        })
    res = run_bass_kernel_spmd(nc, in_maps, core_ids=list(range(n)))
    out = np.stack([np.asarray(r["out"]) for r in res.results], axis=0)
    return out.astype(np.float32)
```
